# Optimizing a Trainium2 kernel written in Bass

```python
import math
import jax
import jax.numpy as jnp
from jax import lax
import numpy as np

D_MODEL = 1024
BATCH = 4
SEQ = 8192
DEPTH = 2

N_EVEN = (DEPTH + 1) // 2
N_ODD = DEPTH // 2
HEAD_DIM = 64
NORM_EPS = 1e-6

LRU_WIDTH = D_MODEL // 2
LRU_BLOCKS = LRU_WIDTH // HEAD_DIM
LRU_BLOCK = LRU_WIDTH // LRU_BLOCKS
CONV_WIDTH = 4
LRU_C = 8.0
MOBA_HEADS = (D_MODEL // 2) // HEAD_DIM
MOBA_WIDTH = MOBA_HEADS * HEAD_DIM
MOBA_BLOCK = 256
MOBA_TOPK = 3
MOBA_QCHUNK = 32
EVEN_SPLITS = (LRU_WIDTH, LRU_WIDTH, MOBA_WIDTH, MOBA_WIDTH, MOBA_WIDTH, MOBA_WIDTH)
EVEN_IN = sum(EVEN_SPLITS)
EVEN_MIX = LRU_WIDTH + MOBA_WIDTH

SB_HEADS = (D_MODEL // 2) // HEAD_DIM
SB_WIDTH = SB_HEADS * HEAD_DIM
SB_QBLOCK = 128
S5_WIDTH = D_MODEL // 2
S5_GROUP = 16
S5_GROUPS = S5_WIDTH // S5_GROUP
S5_STATE = 64
ODD_SPLITS = (SB_WIDTH, SB_WIDTH, SB_WIDTH, SB_WIDTH, S5_WIDTH, S5_WIDTH)
ODD_IN = sum(ODD_SPLITS)
ODD_MIX = SB_WIDTH + S5_WIDTH

kernel_name = 'hybrid_rglru_moba_stickbreak_s5'


def split_points(sizes):
    return [int(s) for s in np.cumsum(sizes)[:-1]]


def rms_norm(x, gain):
    xf = x.astype(jnp.float32)
    y = xf * lax.rsqrt(jnp.mean(xf * xf, axis=-1, keepdims=True) + NORM_EPS)
    return (y * gain.astype(jnp.float32)).astype(x.dtype)


def ada_modulation(c, w, b):
    mod = jnp.einsum('bd,de->be', jax.nn.silu(c), w) + b
    shift, scale, gate = jnp.split(mod[:, None, :], 3, axis=-1)
    return shift, scale, gate


def to_heads(t, n_heads):
    b_, t_, _ = t.shape
    return t.reshape(b_, t_, n_heads, HEAD_DIM)


def from_heads(t):
    b_, h_, t_, d_ = t.shape
    return t.transpose(0, 2, 1, 3).reshape(b_, t_, h_ * d_)


def causal_depthwise_conv(x, w, b):
    ch = x.shape[-1]
    y = lax.conv_general_dilated(x, w[:, None, :].astype(x.dtype), window_strides=(1,),
                                 padding=[(CONV_WIDTH - 1, 0)],
                                 dimension_numbers=('NWC', 'WIO', 'NWC'),
                                 feature_group_count=ch)
    return y + b


def rg_lru(x, rg_w, rg_b, ig_w, ig_b, lam):
    b_, t_, w_ = x.shape
    xf = x.astype(jnp.float32)
    xg = xf.reshape(b_, t_, LRU_BLOCKS, LRU_BLOCK)
    r = jax.nn.sigmoid(jnp.einsum('btgi,gij->btgj', xg, rg_w.astype(jnp.float32)) + rg_b).reshape(b_, t_, w_)
    i = jax.nn.sigmoid(jnp.einsum('btgi,gij->btgj', xg, ig_w.astype(jnp.float32)) + ig_b).reshape(b_, t_, w_)
    log_a = -LRU_C * r * jax.nn.softplus(-lam.astype(jnp.float32))
    a = jnp.exp(log_a)
    inp = jnp.sqrt(-jnp.expm1(2.0 * log_a)) * (i * xf)

    def combine(left, right):
        a1, b1 = left
        a2, b2 = right
        return a1 * a2, a2 * b1 + b2

    _, h = lax.associative_scan(combine, (a, inp), axis=1)
    return h


def moba_attention(q, k, v):
    b_, h_, t_, d_ = q.shape
    nb = -(-t_ // MOBA_BLOCK)
    pad = nb * MOBA_BLOCK - t_
    kp = jnp.pad(k, ((0, 0), (0, 0), (0, pad), (0, 0))).reshape(b_, h_, nb, MOBA_BLOCK, d_)
    vp = jnp.pad(v, ((0, 0), (0, 0), (0, pad), (0, 0))).reshape(b_, h_, nb, MOBA_BLOCK, d_)
    k_mean = jnp.mean(kp, axis=3)
    q_blk = jnp.arange(t_) // MOBA_BLOCK
    gate = jnp.einsum('bhtd,bhnd->bhtn', q, k_mean)
    past = jnp.arange(nb)[None, :] < q_blk[:, None]
    gate = jnp.where(past, gate, -jnp.inf)
    topk = min(MOBA_TOPK, nb)
    _, sel = lax.top_k(gate, topk)
    sel_valid = sel < q_blk[:, None]
    nc = t_ // MOBA_QCHUNK
    scale = d_ ** -0.5
    gather = jax.vmap(jax.vmap(lambda blocks, idx: blocks[idx]))

    def chunk(a):
        return jnp.moveaxis(a.reshape((b_, h_, nc, MOBA_QCHUNK) + a.shape[3:]), 2, 0)

    def attend(args):
        ci, qc, selc, validc = args
        q0 = ci * MOBA_QCHUNK
        qpos = q0 + jnp.arange(MOBA_QCHUNK)
        own = q0 // MOBA_BLOCK
        ks = gather(kp, selc)
        vs = gather(vp, selc)
        s_sel = jnp.einsum('bhqd,bhqnkd->bhqnk', qc, ks) * scale
        s_sel = jnp.where(validc[..., None], s_sel, -jnp.inf).reshape(b_, h_, MOBA_QCHUNK, topk * MOBA_BLOCK)
        k_own = lax.dynamic_index_in_dim(kp, own, axis=2, keepdims=False)
        v_own = lax.dynamic_index_in_dim(vp, own, axis=2, keepdims=False)
        s_own = jnp.einsum('bhqd,bhkd->bhqk', qc, k_own) * scale
        kpos = own * MOBA_BLOCK + jnp.arange(MOBA_BLOCK)
        s_own = jnp.where(kpos[None, :] <= qpos[:, None], s_own, -jnp.inf)
        p = jax.nn.softmax(jnp.concatenate([s_sel, s_own], axis=-1), axis=-1)
        p_sel = p[..., :topk * MOBA_BLOCK].reshape(b_, h_, MOBA_QCHUNK, topk, MOBA_BLOCK)
        p_own = p[..., topk * MOBA_BLOCK:]
        return (jnp.einsum('bhqnk,bhqnkd->bhqd', p_sel, vs)
                + jnp.einsum('bhqk,bhkd->bhqd', p_own, v_own))

    out = lax.map(attend, (jnp.arange(nc), chunk(q), chunk(sel), chunk(sel_valid)))
    return jnp.moveaxis(out, 0, 2).reshape(b_, h_, t_, d_)


def stick_breaking_attention(q, k, v):
    t_ = q.shape[2]
    scale = q.shape[-1] ** -0.5
    outs = []
    for blk in range(t_ // SB_QBLOCK):
        q0 = blk * SB_QBLOCK
        end = q0 + SB_QBLOCK
        z = jnp.einsum('bhqd,bhkd->bhqk', q[:, :, q0:end], k[:, :, :end]) * scale
        qpos = q0 + jnp.arange(SB_QBLOCK)
        strict = jnp.arange(end)[None, :] < qpos[:, None]
        log_beta = jax.nn.log_sigmoid(z)
        log_keep = jnp.where(strict, jax.nn.log_sigmoid(-z), 0.0)
        later = lax.cumsum(log_keep, axis=3, reverse=True) - log_keep
        w = jnp.where(strict, jnp.exp(log_beta + later), 0.0)
        outs.append(jnp.einsum('bhqk,bhkd->bhqd', w, v[:, :, :end]))
    return jnp.concatenate(outs, axis=2)


def s5_ssm(u, lam_re, lam_im, log_step, b_re, b_im, c_re, c_im, d):
    f32 = jnp.float32
    lam_re = lam_re.astype(f32)
    lam_im = lam_im.astype(f32)
    step = jnp.exp(log_step.astype(f32))[:, None]
    decay = jnp.exp(lam_re * step)
    ab_re = decay * jnp.cos(lam_im * step)
    ab_im = decay * jnp.sin(lam_im * step)
    den = lam_re * lam_re + lam_im * lam_im
    f_re = ((ab_re - 1.0) * lam_re + ab_im * lam_im) / den
    f_im = (ab_im * lam_re - (ab_re - 1.0) * lam_im) / den
    b_re = b_re.astype(f32)
    b_im = b_im.astype(f32)
    bb_re = f_re[..., None] * b_re - f_im[..., None] * b_im
    bb_im = f_re[..., None] * b_im + f_im[..., None] * b_re
    bu_re = jnp.einsum('btgh,gph->btgp', u, bb_re)
    bu_im = jnp.einsum('btgh,gph->btgp', u, bb_im)
    t_ = u.shape[1]
    a_re = jnp.broadcast_to(ab_re[None, None], (1, t_) + ab_re.shape)
    a_im = jnp.broadcast_to(ab_im[None, None], (1, t_) + ab_im.shape)

    def combine(left, right):
        ar1, ai1, br1, bi1 = left
        ar2, ai2, br2, bi2 = right
        return (ar2 * ar1 - ai2 * ai1, ar2 * ai1 + ai2 * ar1,
                ar2 * br1 - ai2 * bi1 + br2, ar2 * bi1 + ai2 * br1 + bi2)

    _, _, x_re, x_im = lax.associative_scan(combine, (a_re, a_im, bu_re, bu_im), axis=1)
    y = (jnp.einsum('btgp,ghp->btgh', x_re, c_re.astype(f32))
         - jnp.einsum('btgp,ghp->btgh', x_im, c_im.astype(f32))
         + d.astype(f32) * u)
    return y


def even_layer(x, c, norm_g, ada_w, ada_b, w_in, conv_w, conv_b, rg_w, rg_b, ig_w, ig_b,
               lam, q_g, k_g, w_out):
    shift, scale, gate = ada_modulation(c, ada_w, ada_b)
    h = rms_norm(x, norm_g) * (1.0 + scale) + shift
    proj = jnp.einsum('btd,de->bte', h, w_in)
    x_lru, g_lru, q, k, v, g_att = jnp.split(proj, split_points(EVEN_SPLITS), axis=-1)
    xc = causal_depthwise_conv(x_lru, conv_w, conv_b)
    y_lru = rg_lru(xc, rg_w, rg_b, ig_w, ig_b, lam) * jax.nn.silu(g_lru.astype(jnp.float32))
    qh = rms_norm(to_heads(q, MOBA_HEADS), q_g).astype(jnp.float32).transpose(0, 2, 1, 3)
    kh = rms_norm(to_heads(k, MOBA_HEADS), k_g).astype(jnp.float32).transpose(0, 2, 1, 3)
    vh = to_heads(v, MOBA_HEADS).astype(jnp.float32).transpose(0, 2, 1, 3)
    y_att = from_heads(moba_attention(qh, kh, vh)) * jax.nn.silu(g_att.astype(jnp.float32))
    mix = jnp.concatenate([y_lru, y_att], axis=-1).astype(x.dtype)
    return x + gate * jnp.einsum('bte,ed->btd', mix, w_out)


def odd_layer(x, c, norm_g, ada_w, ada_b, w_in, q_g, k_g, lam_re, lam_im, log_step,
              b_re, b_im, c_re, c_im, d, glu_w, glu_b, w_out):
    b_, t_, _ = x.shape
    shift, scale, gate = ada_modulation(c, ada_w, ada_b)
    h = rms_norm(x, norm_g) * (1.0 + scale) + shift
    proj = jnp.einsum('btd,de->bte', h, w_in)
    q, k, v, g_sb, u, g_s5 = jnp.split(proj, split_points(ODD_SPLITS), axis=-1)
    qh = rms_norm(to_heads(q, SB_HEADS), q_g).astype(jnp.float32).transpose(0, 2, 1, 3)
    kh = rms_norm(to_heads(k, SB_HEADS), k_g).astype(jnp.float32).transpose(0, 2, 1, 3)
    vh = to_heads(v, SB_HEADS).astype(jnp.float32).transpose(0, 2, 1, 3)
    y_sb = from_heads(stick_breaking_attention(qh, kh, vh)) * jax.nn.silu(g_sb.astype(jnp.float32))
    ug = u.astype(jnp.float32).reshape(b_, t_, S5_GROUPS, S5_GROUP)
    y = s5_ssm(ug, lam_re, lam_im, log_step, b_re, b_im, c_re, c_im, d).reshape(b_, t_, S5_WIDTH)
    z = jnp.einsum('bte,ef->btf', y, glu_w.astype(jnp.float32)) + glu_b.astype(jnp.float32)
    z_val, z_gate = jnp.split(z, 2, axis=-1)
    y_s5 = z_val * jax.nn.sigmoid(z_gate) * jax.nn.silu(g_s5.astype(jnp.float32))
    mix = jnp.concatenate([y_sb, y_s5], axis=-1).astype(x.dtype)
    return x + gate * jnp.einsum('bte,ed->btd', mix, w_out)


def setup_inputs(seed: int = 0) -> dict:
    key = jax.random.key(seed)
    keys = iter(jax.random.split(key, 40))
    f32 = jnp.float32

    def nrm(shape, s):
        return jax.random.normal(next(keys), shape, f32) * s

    def gain(shape):
        return 1.0 + nrm(shape, 0.02)

    ne, no = N_EVEN, N_ODD
    x = nrm((BATCH, SEQ, D_MODEL), 1.0)
    c = nrm((BATCH, D_MODEL), 1.0)
    u_lru = jax.random.uniform(next(keys), (ne, LRU_WIDTH), f32, minval=0.9, maxval=0.999)
    a_lru = u_lru ** (1.0 / LRU_C)
    lam_im = (jnp.pi * jnp.arange(S5_STATE, dtype=f32))[None, None, :] + nrm((no, S5_GROUPS, S5_STATE), 0.01)
    return {
        'x': x,
        'c': c,
        'ev_norm': gain((ne, D_MODEL)),
        'ev_ada_w': nrm((ne, D_MODEL, 3 * D_MODEL), 0.5 * D_MODEL ** -0.5),
        'ev_ada_b': nrm((ne, 3 * D_MODEL), 0.02),
        'ev_w_in': nrm((ne, D_MODEL, EVEN_IN), D_MODEL ** -0.5),
        'ev_conv_w': nrm((ne, CONV_WIDTH, LRU_WIDTH), CONV_WIDTH ** -0.5),
        'ev_conv_b': nrm((ne, LRU_WIDTH), 0.02),
        'ev_rgate_w': nrm((ne, LRU_BLOCKS, LRU_BLOCK, LRU_BLOCK), LRU_BLOCK ** -0.5),
        'ev_rgate_b': nrm((ne, LRU_BLOCKS, LRU_BLOCK), 0.02),
        'ev_igate_w': nrm((ne, LRU_BLOCKS, LRU_BLOCK, LRU_BLOCK), LRU_BLOCK ** -0.5),
        'ev_igate_b': nrm((ne, LRU_BLOCKS, LRU_BLOCK), 0.02),
        'ev_lru_lambda': jnp.log(a_lru) - jnp.log1p(-a_lru),
        'ev_q_norm': gain((ne, HEAD_DIM)),
        'ev_k_norm': gain((ne, HEAD_DIM)),
        'ev_w_out': nrm((ne, EVEN_MIX, D_MODEL), EVEN_MIX ** -0.5),
        'od_norm': gain((no, D_MODEL)),
        'od_ada_w': nrm((no, D_MODEL, 3 * D_MODEL), 0.5 * D_MODEL ** -0.5),
        'od_ada_b': nrm((no, 3 * D_MODEL), 0.02),
        'od_w_in': nrm((no, D_MODEL, ODD_IN), D_MODEL ** -0.5),
        'od_q_norm': gain((no, HEAD_DIM)),
        'od_k_norm': gain((no, HEAD_DIM)),
        'od_s5_lambda_re': -0.5 + nrm((no, S5_GROUPS, S5_STATE), 0.01),
        'od_s5_lambda_im': lam_im,
        'od_s5_log_step': jax.random.uniform(next(keys), (no, S5_GROUPS), f32,
                                             minval=math.log(1e-3), maxval=math.log(1e-1)),
        'od_s5_b_re': nrm((no, S5_GROUPS, S5_STATE, S5_GROUP), (2 * S5_GROUP) ** -0.5),
        'od_s5_b_im': nrm((no, S5_GROUPS, S5_STATE, S5_GROUP), (2 * S5_GROUP) ** -0.5),
        'od_s5_c_re': nrm((no, S5_GROUPS, S5_GROUP, S5_STATE), S5_STATE ** -0.5),
        'od_s5_c_im': nrm((no, S5_GROUPS, S5_GROUP, S5_STATE), S5_STATE ** -0.5),
        'od_s5_d': nrm((no, S5_GROUPS, S5_GROUP), 1.0),
        'od_glu_w': nrm((no, S5_WIDTH, 2 * S5_WIDTH), S5_WIDTH ** -0.5),
        'od_glu_b': nrm((no, 2 * S5_WIDTH), 0.02),
        'od_w_out': nrm((no, ODD_MIX, D_MODEL), ODD_MIX ** -0.5),
    }


def reference(x, c, ev_norm, ev_ada_w, ev_ada_b, ev_w_in, ev_conv_w, ev_conv_b, ev_rgate_w,
              ev_rgate_b, ev_igate_w, ev_igate_b, ev_lru_lambda, ev_q_norm, ev_k_norm, ev_w_out,
              od_norm, od_ada_w, od_ada_b, od_w_in, od_q_norm, od_k_norm, od_s5_lambda_re,
              od_s5_lambda_im, od_s5_log_step, od_s5_b_re, od_s5_b_im, od_s5_c_re, od_s5_c_im,
              od_s5_d, od_glu_w, od_glu_b, od_w_out):
    for layer in range(DEPTH):
        i = layer // 2
        if layer % 2 == 0:
            x = even_layer(x, c, ev_norm[i], ev_ada_w[i], ev_ada_b[i], ev_w_in[i], ev_conv_w[i],
                           ev_conv_b[i], ev_rgate_w[i], ev_rgate_b[i], ev_igate_w[i], ev_igate_b[i],
                           ev_lru_lambda[i], ev_q_norm[i], ev_k_norm[i], ev_w_out[i])
        else:
            x = odd_layer(x, c, od_norm[i], od_ada_w[i], od_ada_b[i], od_w_in[i], od_q_norm[i],
                          od_k_norm[i], od_s5_lambda_re[i], od_s5_lambda_im[i], od_s5_log_step[i],
                          od_s5_b_re[i], od_s5_b_im[i], od_s5_c_re[i], od_s5_c_im[i], od_s5_d[i],
                          od_glu_w[i], od_glu_b[i], od_w_out[i])
    return x
```

```python
import contextlib
import numpy as np
import ml_dtypes
import concourse.bass as bass
import concourse.mybir as mybir
from concourse.bass_utils import run_bass_kernel_spmd

F32 = mybir.dt.float32
BF16 = mybir.dt.bfloat16
ALU = mybir.AluOpType
AF = mybir.ActivationFunctionType
NPBF = ml_dtypes.bfloat16

D = 1024
NB = 4
SEQ = 8192
EPS = 1e-6
BIG = 32768.0

CENG = ("pe", "act", "dve", "pool")


class Buf:
    __slots__ = ("last_w", "readers")

    def __init__(self):
        self.last_w = None
        self.readers = []


class V:
    __slots__ = ("ap", "bufs")

    def __init__(self, ap, *bufs):
        self.ap = ap
        bl = []
        for b in bufs:
            if isinstance(b, (list, tuple)):
                bl.extend(b)
            else:
                bl.append(b)
        self.bufs = bl


class Tl:
    def __init__(self, h, nreg=1):
        self.t = h
        self.b = [Buf() for _ in range(nreg)]

    def all(self):
        return V(self.t[:], self.b)


class Op:
    __slots__ = ("eng", "fn", "waits", "dwaits", "signal", "idx", "is_dma", "dma_sem", "dma_thr", "pre_wait")

    def __init__(self, eng, fn):
        self.eng = eng
        self.fn = fn
        self.waits = {}
        self.dwaits = []
        self.signal = False
        self.is_dma = False
        self.dma_sem = None
        self.dma_thr = None
        self.pre_wait = None


class Sched:
    NDMA = 16

    def __init__(self, nc):
        self.nc = nc
        self.ops = {e: [] for e in CENG + ("sp",)}
        self.known = {e: {} for e in CENG + ("sp",)}
        self.kdma = {e: set() for e in CENG + ("sp",)}
        self.ndma = {"sp": 0, "pool": 0, "act": 0}

    def _add(self, eng, fn, reads, writes, is_dma=False):
        op = Op(eng, fn)
        op.is_dma = is_dma
        lst = self.ops[eng]
        op.idx = len(lst)
        deps = []
        for b in reads:
            if b.last_w is not None:
                deps.append((b.last_w, "raw"))
        for b in writes:
            if b.last_w is not None:
                deps.append((b.last_w, "waw"))
            for r in b.readers:
                deps.append((r, "war"))
        for src, kind in deps:
            if src is op:
                continue
            if src.is_dma:
                if id(src) in self.kdma[eng]:
                    continue
                self.kdma[eng].add(id(src))
                op.dwaits.append(src)
                continue
            se = src.eng
            if se == eng and not is_dma:
                if eng == "pe":
                    continue
            if self.known[eng].get(se, -1) >= src.idx:
                continue
            if op.waits.get(se, -1) < src.idx:
                op.waits[se] = src.idx
        for k, vv in op.waits.items():
            self.known[eng][k] = max(self.known[eng].get(k, -1), vv)
        lst.append(op)
        for b in reads:
            b.readers.append(op)
        for b in writes:
            b.last_w = op
            b.readers = []
        return op

    def op(self, eng, method, **kw):
        reads, writes = [], []
        args = {}
        for k, v in kw.items():
            if isinstance(v, V):
                if k in ("out", "accum_out"):
                    writes.extend(v.bufs)
                else:
                    reads.extend(v.bufs)
                args[k] = v.ap
            else:
                args[k] = v

        def fn(e, method=method, args=args):
            return getattr(e, method)(**args)

        return self._add(eng, fn, reads, writes)

    def memset(self, eng, view, val):
        ap = view.ap
        return self._add(eng, lambda e: e.memset(ap, val), [], list(view.bufs))

    def dma(self, out, in_, q="sp", **kw):
        args = dict(out=out.ap, in_=in_.ap, **kw)

        def fn(e, args=args):
            return e.dma_start(**args)

        op = self._add(q, fn, list(in_.bufs), list(out.bufs), is_dma=True)
        k = self.ndma[q]
        self.ndma[q] += 1
        op.dma_sem = (q, k % self.NDMA)
        op.dma_thr = 16 * (k // self.NDMA + 1)
        if k >= self.NDMA:
            op.pre_wait = (op.dma_sem, 16 * (k // self.NDMA))
        return op

    def pe(self, method, **kw):
        return self.op("pe", method, **kw)

    def act(self, method, **kw):
        return self.op("act", method, **kw)

    def dve(self, method, **kw):
        return self.op("dve", method, **kw)

    def pool(self, method, **kw):
        return self.op("pool", method, **kw)

    def emit(self):
        nc = self.nc
        for e in self.ops:
            for op in self.ops[e]:
                for k, vv in op.waits.items():
                    self.ops[k][vv].signal = True
        for e in CENG:
            for op in reversed(self.ops[e]):
                if not op.is_dma:
                    op.signal = True
                    break
        cnt = {}
        for e in CENG:
            c = 0
            arr = []
            for op in self.ops[e]:
                if op.signal and not op.is_dma:
                    c += 1
                arr.append(c)
            cnt[e] = arr
        stack = contextlib.ExitStack()
        with stack:
            sems = {e: stack.enter_context(nc.semaphore("s_" + e)) for e in CENG}
            dsems = {}
            for q in ("sp", "pool", "act"):
                if self.ndma[q]:
                    for i in range(min(self.NDMA, self.ndma[q])):
                        dsems[(q, i)] = stack.enter_context(nc.semaphore("d_%s%d" % (q, i)))
            block = stack.enter_context(nc.Block())

            def run(e, name):
                for op in self.ops[name]:
                    if op.pre_wait is not None:
                        e.wait_ge(dsems[op.pre_wait[0]], op.pre_wait[1])
                    for d in op.dwaits:
                        e.wait_ge(dsems[d.dma_sem], d.dma_thr)
                    for k, vv in op.waits.items():
                        e.wait_ge(sems[k], cnt[k][vv])
                    ins = op.fn(e)
                    if op.is_dma:
                        ins.then_inc(dsems[op.dma_sem], 16)
                    elif op.signal:
                        ins.then_inc(sems[name], 1)
                if name == "sp":
                    last = {}
                    for q in self.ops:
                        for op in self.ops[q]:
                            if op.is_dma:
                                last[op.dma_sem] = op.dma_thr
                    for s, thr in last.items():
                        e.wait_ge(dsems[s], thr)
                    for k in CENG:
                        if cnt[k] and cnt[k][-1] > 0:
                            e.wait_ge(sems[k], cnt[k][-1])

            @block.tensor
            def _(e):
                run(e, "pe")

            @block.scalar
            def _(e):
                run(e, "act")

            @block.vector
            def _(e):
                run(e, "dve")

            @block.gpsimd
            def _(e):
                run(e, "pool")

            @block.sync
            def _(e):
                run(e, "sp")


class KB:
    def __init__(self):
        self.nc = bass.Bass("TRN2", target_bir_lowering=False)
        self.S = Sched(self.nc)
        self.st = contextlib.ExitStack()
        self._rr = 0

    def sb(self, name, shape, dt=F32, nreg=1):
        return Tl(self.st.enter_context(self.nc.sbuf_tensor(name, shape, dt)), nreg)

    def psum(self, name):
        return Tl(self.st.enter_context(self.nc.psum_tensor(name, [128, 512], F32)))

    def psum8(self):
        P, PR = [], []
        for i in range(4):
            t = self.st.enter_context(self.nc.psum_tensor("PP%d" % i, [128, 1024], F32))
            a, b = Tl(t[:, 0:512]), Tl(t[:, 512:1024])
            P += [a, b]
            pr = Tl(t[:, :])
            pr.b = [a.b[0], b.b[0]]
            PR.append(pr)
        return P, PR

    def din(self, name, shape, dt=F32, nreg=1):
        return Tl(self.nc.dram_tensor(name, shape, dt, kind="ExternalInput").ap(), nreg)

    def dout(self, name, shape, dt=F32, nreg=1):
        return Tl(self.nc.dram_tensor(name, shape, dt, kind="ExternalOutput").ap(), nreg)

    def dscr(self, name, shape, dt=F32, nreg=1):
        return Tl(self.nc.dram_tensor(name, shape, dt, kind="Internal").ap(), nreg)

    def ew_eng(self):
        self._rr += 1
        return ("dve", "pool")[self._rr % 2]


class Carver:
    def __init__(self, ap):
        self.t = ap
        self.off = 0

    def get(self, shape, dt=F32):
        n = int(np.prod(shape[1:]))
        nb = n * (2 if dt == F32 else 1)
        v = self.t[:, self.off:self.off + nb]
        self.off += nb
        if dt == F32:
            v = v.bitcast(F32)
        if len(shape) == 3:
            v = v.rearrange("p (a b) -> p a b", a=shape[1])
        return Tl(v)

    def reset(self):
        self.off = 0


def load_cast_weight(kb, w_dram, wbf, nk, ncols, stage, col0=0):
    S = kb.S
    for kc in range(nk):
        stg = stage[kc % len(stage)]
        S.dma(V(stg.t[:, 0:ncols], stg.b), V(w_dram.t[kc * 128:(kc + 1) * 128, col0:col0 + ncols], w_dram.b))
        S.op(kb.ew_eng(), "tensor_copy", out=V(wbf.t[:, kc, :], wbf.b), in_=V(stg.t[:, 0:ncols], stg.b))


def ada_part(kb, cl_d, adaw_d, adab_d, part, P, out_tile, adaw_sb, plus_one=False, tag=""):
    S = kb.S
    scl = kb.sb("scl%d%s" % (part, tag), [128, 8])
    cl = kb.sb("cl%d%s" % (part, tag), [128, 8])
    ab = kb.sb("ab%d%s" % (part, tag), [128, 8])
    S.dma(cl.all(), cl_d.all())
    S.dma(ab.all(), adab_d.all())
    S.act("activation", out=scl.all(), in_=cl.all(), func=AF.Silu)
    for kc in range(8):
        S.dma(V(adaw_sb.t[:, kc, :], adaw_sb.b), V(adaw_d.t[kc * 128:(kc + 1) * 128, part * 1024:(part + 1) * 1024], adaw_d.b))
    for m in range(8):
        for kc in range(8):
            S.pe("matmul", out=V(P.t[:, m:m + 1], P.b), lhsT=V(adaw_sb.t[:, kc, m * 128:(m + 1) * 128], adaw_sb.b),
                 rhs=V(scl.t[:, kc:kc + 1], scl.b), start=(kc == 0), stop=(kc == 7))
    if plus_one:
        S.dve("scalar_tensor_tensor", out=out_tile.all(), in0=V(P.t[:, 0:8], P.b), scalar=1.0, in1=ab.all(), op0=ALU.add, op1=ALU.add)
    else:
        S.dve("tensor_tensor", out=out_tile.all(), in0=V(P.t[:, 0:8], P.b), in1=ab.all(), op=ALU.add)


def build_l3(Th):
    kb = KB()
    S = kb.S
    NT = Th // 512
    x1T = kb.din("x1T", [D, Th])
    ysbT = kb.din("ysbT", [512, Th], BF16)
    yT = kb.din("yT", [512, Th], BF16)
    sg5T = kb.din("sg5T", [512, Th], BF16)
    gluw = kb.din("gluw", [512, 1024])
    glub = kb.din("glub", [128, 8])
    wout = kb.din("wout", [D, D])
    cl_d = kb.din("cl", [128, 8])
    adaw = kb.din("adaw", [D, 3 * D])
    adab = kb.din("adab_g", [128, 8])
    outT = kb.dout("outT", [D, Th], F32, nreg=NT)
    with kb.st:
        P = [kb.psum("P%d" % i) for i in range(8)]
        stage = [kb.sb("stg%d" % i, [128, 1024]) for i in range(2)]
        adaw_sb = kb.sb("adaw_sb", [128, 8, 1024])
        gate = kb.sb("gate", [128, 8])
        glub_sb = kb.sb("glub_sb", [128, 8])
        wout_bf = kb.sb("wout_bf", [128, 8, 1024], BF16)
        gluw_bf = kb.sb("gluw_bf", [128, 4, 1024], BF16)
        x1t = [kb.sb("x1t%d" % i, [128, 8, 512]) for i in range(2)]
        ot = [kb.sb("ot%d" % i, [128, 8, 512]) for i in range(2)]
        ysbt = [kb.sb("ysbt%d" % i, [128, 4, 512], BF16) for i in range(2)]
        yt = [kb.sb("yt%d" % i, [128, 4, 512], BF16) for i in range(2)]
        sgt = [kb.sb("sgt%d" % i, [128, 4, 512], BF16) for i in range(2)]
        ms5 = [kb.sb("ms5%d" % i, [128, 4, 512], BF16) for i in range(2)]
        sgm = [kb.sb("sgm%d" % i, [128, 512]) for i in range(2)]
        t1 = [kb.sb("t1%d" % i, [128, 512]) for i in range(2)]

        S.dma(glub_sb.all(), glub.all())
        ada_part(kb, cl_d, adaw, adab, 2, P[0], gate, adaw_sb)
        load_cast_weight(kb, gluw, gluw_bf, 4, 1024, stage)
        load_cast_weight(kb, wout, wout_bf, 8, 1024, stage)

        x1v = x1T.t.rearrange("(c p) t -> p c t", p=128)
        ysbv = ysbT.t.rearrange("(c p) t -> p c t", p=128)
        yv = yT.t.rearrange("(c p) t -> p c t", p=128)
        sgv = sg5T.t.rearrange("(c p) t -> p c t", p=128)
        outv = outT.t.rearrange("(c p) t -> p c t", p=128)
        for tt in range(NT):
            bi = tt % 2
            sl = slice(tt * 512, (tt + 1) * 512)
            S.dma(x1t[bi].all(), V(x1v[:, :, sl], x1T.b))
            S.dma(ysbt[bi].all(), V(ysbv[:, :, sl], ysbT.b))
            S.dma(yt[bi].all(), V(yv[:, :, sl], yT.b))
            S.dma(sgt[bi].all(), V(sgv[:, :, sl], sg5T.b))
            for i in range(4):
                pv = P[i % 2]
                pg = P[2 + i % 2]
                for e in range(4):
                    S.pe("matmul", out=pv.all(), lhsT=V(gluw_bf.t[:, e, i * 128:(i + 1) * 128], gluw_bf.b),
                         rhs=V(yt[bi].t[:, e, :], yt[bi].b), start=(e == 0), stop=(e == 3))
                for e in range(4):
                    S.pe("matmul", out=pg.all(), lhsT=V(gluw_bf.t[:, e, (4 + i) * 128:(5 + i) * 128], gluw_bf.b),
                         rhs=V(yt[bi].t[:, e, :], yt[bi].b), start=(e == 0), stop=(e == 3))
                S.act("activation", out=sgm[i % 2].all(), in_=pg.all(), func=AF.Sigmoid,
                      bias=V(glub_sb.t[:, 4 + i:5 + i], glub_sb.b))
                S.dve("scalar_tensor_tensor", out=t1[i % 2].all(), in0=pv.all(), scalar=V(glub_sb.t[:, i:i + 1], glub_sb.b),
                      in1=sgm[i % 2].all(), op0=ALU.add, op1=ALU.mult)
                S.pool("tensor_tensor", out=V(ms5[bi].t[:, i, :], ms5[bi].b), in0=t1[i % 2].all(),
                       in1=V(sgt[bi].t[:, i, :], sgt[bi].b), op=ALU.mult)
            for d in range(8):
                po = P[4 + d % 4]
                for e in range(8):
                    rhs = V(ysbt[bi].t[:, e, :], ysbt[bi].b) if e < 4 else V(ms5[bi].t[:, e - 4, :], ms5[bi].b)
                    S.pe("matmul", out=po.all(), lhsT=V(wout_bf.t[:, e, d * 128:(d + 1) * 128], wout_bf.b), rhs=rhs,
                         start=(e == 0), stop=(e == 7))
                S.dve("scalar_tensor_tensor", out=V(ot[bi].t[:, d, :], ot[bi].b), in0=po.all(),
                      scalar=V(gate.t[:, d:d + 1], gate.b), in1=V(x1t[bi].t[:, d, :], x1t[bi].b), op0=ALU.mult, op1=ALU.add)
            S.dma(V(outv[:, :, sl], outT.b[tt]), ot[bi].all(), q="pool")
        S.emit()
    return kb.nc


def lay128(v):
    return np.ascontiguousarray(np.asarray(v, np.float32).reshape(-1, 128).T)


AX = mybir.AxisListType


def fence(S):
    snap = {}
    for e in CENG:
        for op in reversed(S.ops[e]):
            if not op.is_dma:
                snap[e] = op.idx
                break
    dl = {}
    for q in S.ops:
        for op in S.ops[q]:
            if op.is_dma:
                dl[op.dma_sem] = op
    S.pending = {e: (dict(snap), list(dl.values())) for e in CENG + ("sp",)}


_orig_add = Sched._add


def _add_with_fence(self, eng, fn, reads, writes, is_dma=False):
    op = _orig_add(self, eng, fn, reads, writes, is_dma)
    pend = getattr(self, "pending", None)
    if pend and eng in pend:
        snap, dl = pend.pop(eng)
        for se, idx in snap.items():
            if se == eng:
                continue
            if self.known[eng].get(se, -1) >= idx:
                continue
            if op.waits.get(se, -1) < idx:
                op.waits[se] = idx
            self.known[eng][se] = max(self.known[eng].get(se, -1), idx)
        for d in dl:
            if d is op or id(d) in self.kdma[eng]:
                continue
            self.kdma[eng].add(id(d))
            op.dwaits.append(d)
    return op


Sched._add = _add_with_fence


def norm_tile(kb, C, xt, ht, sq, rstd, Pss, gs, sh, tmpf):
    S = kb.S
    S.act("activation", out=sq.all(), in_=xt.all(), func=AF.Square)
    for c in range(8):
        S.pe("matmul", out=Pss.all(), lhsT=C["ones_bf"].all(), rhs=V(sq.t[:, c, :], sq.b), start=(c == 0), stop=(c == 7))
    S.act("activation", out=rstd.all(), in_=Pss.all(), func=AF.Ln, scale=1.0 / D, bias=V(C["eps"].t[:, 0:1], C["eps"].b))
    S.act("activation", out=rstd.all(), in_=rstd.all(), func=AF.Exp, scale=-0.5)
    for c in range(8):
        tf = tmpf[c % 2]
        S.dve("scalar_tensor_tensor", out=tf.all(), in0=V(xt.t[:, c, :], xt.b), scalar=V(gs.t[:, c:c + 1], gs.b),
              in1=rstd.all(), op0=ALU.mult, op1=ALU.mult)
        S.act("activation", out=V(ht.t[:, c, :], ht.b), in_=tf.all(), func=AF.Identity, bias=V(sh.t[:, c:c + 1], sh.b))


def make_consts(kb, T):
    S = kb.S
    C = {}
    C["ones_bf"] = kb.sb("ones_bf", [128, 128], BF16)
    S.memset("pool", C["ones_bf"].all(), 1.0)
    C["eps"] = kb.sb("eps_c", [128, 1])
    S.memset("pool", C["eps"].all(), EPS)
    C["one"] = kb.sb("one_c", [128, 1])
    S.memset("pool", C["one"].all(), 1.0)
    return C


def headnorm(kb, C, Pin, Pss, gain, sqf, rs, outf):
    S = kb.S
    S.act("activation", out=sqf.all(), in_=Pin.all(), func=AF.Square)
    S.pe("matmul", out=Pss.all(), lhsT=C["onesbd"].all(), rhs=sqf.all(), start=True, stop=True)
    S.act("activation", out=rs.all(), in_=Pss.all(), func=AF.Ln, scale=1.0 / 64, bias=V(C["eps"].t[:, 0:1], C["eps"].b))
    S.act("activation", out=rs.all(), in_=rs.all(), func=AF.Exp, scale=-0.5)
    S.dve("scalar_tensor_tensor", out=outf.all(), in0=Pin.all(), scalar=V(gain.t[:, 0:1], gain.b), in1=rs.all(),
          op0=ALU.mult, op1=ALU.mult)


def proj_pass(kb, hT, T, wbf, chunks, P, hts, consumer, vchunk=None, vconsumer=None):
    S = kb.S
    NT = T // 512
    hv = hT.t.rearrange("c p t -> p c t")
    for tt in range(NT):
        ht = hts[tt % 2]
        S.dma(ht.all(), V(hv[:, :, tt * 512:(tt + 1) * 512], hT.b[tt]))
        for n, (slot, tag) in enumerate(chunks):
            pp = P[n % len(P)] if not isinstance(P, dict) else P[tag]
            for kc in range(8):
                S.pe("matmul", out=pp.all(), lhsT=V(wbf.t[:, slot, kc, :], wbf.b), rhs=V(ht.t[:, kc, :], ht.b),
                     start=(kc == 0), stop=(kc == 7))
            consumer(tag, tt, pp)
        if vchunk is not None:
            pp = P["v"]
            for s in range(4):
                for kc in range(8):
                    S.pe("matmul", out=V(pp.t[:, s * 128:(s + 1) * 128], pp.b), lhsT=V(ht.t[:, kc, s * 128:(s + 1) * 128], ht.b),
                         rhs=V(wbf.t[:, vchunk, kc, :], wbf.b), start=(kc == 0), stop=(kc == 7))
            vconsumer(tt, pp)


def load_w_slots(kb, win, wbf, slots, stage):
    S = kb.S
    for slot, col0 in slots:
        for kc in range(8):
            stg = stage[kc % 2]
            S.dma(V(stg.t[:, 0:128], stg.b), V(win.t[kc * 128:(kc + 1) * 128, col0:col0 + 128], win.b))
            S.op(kb.ew_eng(), "tensor_copy", out=V(wbf.t[:, slot, kc, :], wbf.b), in_=V(stg.t[:, 0:128], stg.b))


def build_l1(T, stop=99):
    kb = KB()
    S = kb.S
    NT = T // 512
    NQ = T // 128
    CS = 8192 + 32
    xT = kb.din("xT", [D, T])
    cl_d = kb.din("cl", [128, 8])
    adaw = kb.din("adaw", [D, 3 * D])
    adab_sh = kb.din("adab_sh", [128, 8])
    adab_sc = kb.din("adab_sc", [128, 8])
    normg = kb.din("normg", [128, 8])
    win = kb.din("win", [D, 1536])
    convw = kb.din("convw", [128, 2, 4])
    convb = kb.din("convb", [128, 2])
    rgw = kb.din("rgw", [128, 2, 128])
    rgb = kb.din("rgb", [128, 2])
    igw = kb.din("igw", [128, 2, 128])
    igb = kb.din("igb", [128, 2])
    lam = kb.din("lam", [128, 2])
    qg = kb.din("qg", [128, 1])
    kg = kb.din("kg", [128, 1])
    onesbd_d = kb.din("onesbd", [128, 128])
    khot_d = kb.din("khot", [32, T], BF16)
    pm_d = kb.din("pm128", [128, NQ, 32], BF16)
    own_d = kb.din("own128", [128, NQ, 32], BF16)
    cm_d = kb.din("cm", [128, 896], BF16)
    identb_d = kb.din("identb", [128, 128], BF16)
    identf_d = kb.din("identf", [128, 128])
    mixT = kb.dout("mixT", [512, T], BF16)
    hT = kb.dscr("hT", [8, 128, T], BF16, nreg=NT)
    with kb.st:
        P = [kb.psum("P%d" % i) for i in range(8)]
        C = make_consts(kb, T)
        arena = kb.sb("arena", [128, 7 * CS + 4096], BF16)

        def cbf(i, nreg=1):
            return Tl(arena.t[:, i * CS:i * CS + T], nreg)

        def cf32(i, nreg=1):
            return Tl(arena.t[:, i * CS:(i + 2) * CS].bitcast(F32), nreg)

        stage = [kb.sb("stg%d" % i, [128, 128]) for i in range(2)]
        wbf = kb.sb("wbf", [128, 4, 8, 128], BF16)
        hts = [kb.sb("hts%d" % i, [128, 8, 512], BF16) for i in range(2)]
        gs = kb.sb("gs", [128, 8])
        sh = kb.sb("shf", [128, 8])
        sc1 = kb.sb("sc1", [128, 8])
        ng = kb.sb("ng", [128, 8])
        tf = [kb.sb("tf%d" % i, [128, 512]) for i in range(4)]
        rstd = kb.sb("rstd", [128, 512])
        small = {}
        for nm, src, shp in (("convw", convw, [128, 2, 4]), ("convb", convb, [128, 2]), ("rgb", rgb, [128, 2]),
                             ("igb", igb, [128, 2]), ("lam", lam, [128, 2]), ("qg", qg, [128, 1]), ("kg", kg, [128, 1]),
                             ("onesbd", onesbd_d, [128, 128]), ("identf", identf_d, [128, 128])):
            small[nm] = kb.sb("c_" + nm, shp)
            S.dma(small[nm].all(), src.all())
        C["onesbd"] = small["onesbd"]
        cmt = kb.sb("cmt", [128, 896], BF16)
        S.dma(cmt.all(), cm_d.all())
        identb = kb.sb("identb_s", [128, 128], BF16)
        S.dma(identb.all(), identb_d.all())
        gwf = kb.sb("gwf", [128, 2, 2, 128])
        S.dma(V(gwf.t[:, 0, :, :], gwf.b), rgw.all())
        S.dma(V(gwf.t[:, 1, :, :], gwf.b), igw.all())
        gwb = kb.sb("gwb", [128, 2, 2, 128], BF16)
        S.dve("tensor_copy", out=gwb.all(), in_=gwf.all())
        spl = kb.sb("spl", [128, 2])
        nsp8 = kb.sb("nsp8", [128, 2])
        nsp16 = kb.sb("nsp16", [128, 2])
        S.act("activation", out=spl.all(), in_=small["lam"].all(), func=AF.Exp, scale=-1.0)
        S.act("activation", out=spl.all(), in_=spl.all(), func=AF.Ln, bias=V(C["one"].t[:, 0:1], C["one"].b))
        S.dve("tensor_scalar", out=nsp8.all(), in0=spl.all(), scalar1=-8.0, scalar2=None, op0=ALU.mult)
        S.dve("tensor_scalar", out=nsp16.all(), in0=spl.all(), scalar1=-16.0, scalar2=None, op0=ALU.mult)

        adaw_sb = Tl(arena.t[:, 0:2 * CS].bitcast(F32)[:, 0:8192].rearrange("p (k n) -> p k n", k=8))
        ada_part(kb, cl_d, adaw, adab_sc, 1, P[0], sc1, adaw_sb, plus_one=True, tag="a")
        ada_part(kb, cl_d, adaw, adab_sh, 0, P[1], sh, adaw_sb, tag="b")
        S.dma(ng.all(), normg.all())
        S.dve("tensor_tensor", out=gs.all(), in0=ng.all(), in1=sc1.all(), op=ALU.mult)
        xts = [Tl(arena.t[:, (2 + 2 * i) * CS:(4 + 2 * i) * CS].bitcast(F32)[:, 0:4096].rearrange("p (c t) -> p c t", c=8)) for i in range(2)]
        sq = Tl(arena.t[:, 6 * CS:6 * CS + 4096].rearrange("p (c t) -> p c t", c=8))
        xv = xT.t.rearrange("(c p) t -> p c t", p=128)
        hv = hT.t.rearrange("c p t -> p c t")
        for tt in range(NT):
            xt = xts[tt % 2]
            S.dma(xt.all(), V(xv[:, :, tt * 512:(tt + 1) * 512], xT.b))
            ht = hts[tt % 2]
            norm_tile(kb, C, xt, ht, sq, rstd, P[2 + tt % 2], gs, sh, tf)
            S.dma(V(hv[:, :, tt * 512:(tt + 1) * 512], hT.b[tt]), ht.all(), q="pool")
        fence(S)
        if stop <= 1:
            S.emit()
            return kb.nc

        xcb = [kb.sb("xcb%d" % i, [128, 512], BF16) for i in range(2)]
        for ci in range(2):
            load_w_slots(kb, win, wbf, [(0, ci * 128), (1, 256 + ci * 128)], stage)
            xl = cf32(0)
            xc = cf32(2)
            sg = cbf(4)
            ym = cbf(5)
            S.memset("pool", V(xl.t[:, 0:3], xl.b), 0.0)

            def cons(tag, tt, pp, xl=xl, sg=sg):
                sl = slice(tt * 512, (tt + 1) * 512)
                if tag == "x":
                    S.act("activation", out=V(xl.t[:, 3 + tt * 512:3 + (tt + 1) * 512], xl.b), in_=pp.all(), func=AF.Copy)
                else:
                    S.act("activation", out=V(sg.t[:, sl], sg.b), in_=pp.all(), func=AF.Silu)

            proj_pass(kb, hT, T, wbf, [(0, "x"), (1, "g")], P[0:4], hts, cons)
            cw = small["convw"]
            S.dve("tensor_scalar", out=V(xc.t[:, 0:T], xc.b), in0=V(xl.t[:, 0:T], xl.b), scalar1=V(cw.t[:, ci, 0:1], cw.b),
                  scalar2=V(small["convb"].t[:, ci:ci + 1], small["convb"].b), op0=ALU.mult, op1=ALU.add)
            for j in range(1, 4):
                S.dve("scalar_tensor_tensor", out=V(xc.t[:, 0:T], xc.b), in0=V(xl.t[:, j:j + T], xl.b),
                      scalar=V(cw.t[:, ci, j:j + 1], cw.b), in1=V(xc.t[:, 0:T], xc.b), op0=ALU.mult, op1=ALU.add)
            a = xl
            for tt in range(NT):
                sl = slice(tt * 512, (tt + 1) * 512)
                xb = xcb[tt % 2]
                S.pool("tensor_copy", out=xb.all(), in_=V(xc.t[:, sl], xc.b))
                pr, pi = P[(2 * tt) % 8], P[(2 * tt + 1) % 8]
                S.pe("matmul", out=pr.all(), lhsT=V(gwb.t[:, 0, ci, :], gwb.b), rhs=xb.all(), start=True, stop=True)
                S.pe("matmul", out=pi.all(), lhsT=V(gwb.t[:, 1, ci, :], gwb.b), rhs=xb.all(), start=True, stop=True)
                r, ig, a2, m = tf
                S.act("activation", out=r.all(), in_=pr.all(), func=AF.Sigmoid, bias=V(small["rgb"].t[:, ci:ci + 1], small["rgb"].b))
                S.act("activation", out=ig.all(), in_=pi.all(), func=AF.Sigmoid, bias=V(small["igb"].t[:, ci:ci + 1], small["igb"].b))
                S.act("activation", out=V(a.t[:, sl], a.b), in_=r.all(), func=AF.Exp, scale=V(nsp8.t[:, ci:ci + 1], nsp8.b))
                S.act("activation", out=a2.all(), in_=r.all(), func=AF.Exp, scale=V(nsp16.t[:, ci:ci + 1], nsp16.b))
                S.act("activation", out=a2.all(), in_=a2.all(), func=AF.Ln, scale=-1.0, bias=V(C["one"].t[:, 0:1], C["one"].b))
                S.act("activation", out=m.all(), in_=a2.all(), func=AF.Exp, scale=0.5)
                S.dve("tensor_tensor", out=m.all(), in0=m.all(), in1=ig.all(), op=ALU.mult)
                S.dve("tensor_tensor", out=V(xc.t[:, sl], xc.b), in0=V(xc.t[:, sl], xc.b), in1=m.all(), op=ALU.mult)
            S.dve("tensor_tensor_scan", out=V(xc.t[:, 0:T], xc.b), data0=V(a.t[:, 0:T], a.b), data1=V(xc.t[:, 0:T], xc.b),
                  initial=0.0, op0=ALU.mult, op1=ALU.add)
            S.dve("tensor_tensor", out=ym.all(), in0=V(xc.t[:, 0:T], xc.b), in1=sg.all(), op=ALU.mult)
            S.dma(V(mixT.t[ci * 128:(ci + 1) * 128, :], mixT.b), ym.all(), q="pool")
            fence(S)

        if stop <= 2:
            S.emit()
            return kb.nc
        pm = kb.sb("pm_s", [128, NQ, 32], BF16)
        own = kb.sb("own_s", [128, NQ, 32], BF16)
        S.dma(pm.all(), pm_d.all())
        S.dma(own.all(), own_d.all())
        kms = kb.sb("kms", [128, 2, 32])
        gm = kb.sb("gm", [128, 8, 32])
        top8 = kb.sb("top8", [128, 8, 8])
        thr8 = kb.sb("thr8", [128, 8])
        selB = kb.sb("selB", [128, 8, 32])
        rden = kb.sb("rden", [128, 512])
        ytmp = kb.sb("ytmp", [128, 512])
        ones64 = V(C["ones_bf"].t[:, 0:64], C["ones_bf"].b)
        At_l1 = [kb.sb("At%d" % i, [128, 512], BF16) for i in range(4)]
        for hp in range(2):
            load_w_slots(kb, win, wbf, [(0, 512 + hp * 128), (1, 768 + hp * 128), (2, 1024 + hp * 128), (3, 1280 + hp * 128)], stage)
            Qp = [Tl(arena.t[0:96, (0 + h) * CS:(0 + h) * CS + T]) for h in range(2)]
            Kp = [Tl(arena.t[0:96, (2 + h) * CS:(2 + h) * CS + T]) for h in range(2)]
            o4 = 4 * CS
            Vt = Tl(arena.t[:, o4:o4 + NQ * 192].rearrange("p (n c) -> p n c", c=192))
            sga = Tl(arena.t[:, o4 + 64 * 192:o4 + 64 * 192 + T])
            ym = Tl(arena.t[:, o4 + 64 * 192 + 8192:o4 + 64 * 192 + 8192 + T])
            S.memset("pool", V(Vt.t[:, :, 64:128], Vt.b), 1.0)
            for h in range(2):
                S.dma(V(Kp[h].t[64:96, :], Kp[h].b), khot_d.all())
            S.memset("pool", kms.all(), 0.0)
            PP = {"k": P[0], "q": P[1], "g": P[2], "v": P[3]}
            Pss, Pgate, PnT = P[4], P[5], [P[6], P[7]]

            def cons(tag, tt, pp, Qp=Qp, Kp=Kp, sga=sga):
                sl = slice(tt * 512, (tt + 1) * 512)
                if tag == "g":
                    S.act("activation", out=V(sga.t[:, sl], sga.b), in_=pp.all(), func=AF.Silu)
                    return
                sqf, rs, hf = tf[0], tf[1], tf[2]
                headnorm(kb, C, pp, Pss, small["kg" if tag == "k" else "qg"], sqf, rs, hf)
                if tag == "k":
                    for h in range(2):
                        S.op(("pool", "dve")[h], "tensor_copy", out=V(Kp[h].t[0:64, sl], Kp[h].b), in_=V(hf.t[h * 64:(h + 1) * 64, :], hf.b))
                    for h in range(2):
                        S.dve("tensor_reduce", out=V(kms.t[h * 64:(h + 1) * 64, h, 2 * tt:2 * tt + 2], kms.b),
                              in_=V(hf.t[h * 64:(h + 1) * 64, :].rearrange("p (a b) -> p a b", a=2), hf.b), axis=AX.X, op=ALU.add)
                    return
                import os
                SK = os.environ.get('SKIP', '')
                for h in range(2):
                    S.act("activation", out=V(Qp[h].t[0:64, sl], Qp[h].b), in_=V(hf.t[h * 64:(h + 1) * 64, :], hf.b), func=AF.Copy, scale=0.125)
                if 'g' in SK:
                    return
                for h in range(2):
                    for s in range(4):
                        S.pe("matmul", out=V(Pgate.t[:, (h * 4 + s) * 32:(h * 4 + s + 1) * 32], Pgate.b),
                             lhsT=V(hf.t[:, s * 128:(s + 1) * 128], hf.b), rhs=V(kms.t[:, h, :], kms.b),
                             start=True, stop=True)
                for h in range(2):
                    S.dve("tensor_tensor", out=V(gm.t[:, h * 4:(h + 1) * 4, :], gm.b),
                          in0=V(Pgate.t[:, h * 128:(h + 1) * 128].rearrange("p (s j) -> p s j", s=4), Pgate.b),
                          in1=V(pm.t[:, 4 * tt:4 * tt + 4, :], pm.b), op=ALU.add)
                if 'm' in SK:
                    return
                for hs in range(8):
                    S.dve("max", out=V(top8.t[:, hs, :], top8.b), in_=V(gm.t[:, hs, :], gm.b))
                S.dve("tensor_scalar", out=thr8.all(), in0=V(top8.t[:, :, 2], top8.b), scalar1=-5e8, scalar2=None, op0=ALU.max)
                for hs in range(8):
                    S.dve("tensor_scalar", out=V(selB.t[:, hs, :], selB.b), in0=V(gm.t[:, hs, :], gm.b),
                          scalar1=V(thr8.t[:, hs:hs + 1], thr8.b), scalar2=BIG, op0=ALU.is_ge, op1=ALU.mult)
                for h in range(2):
                    S.dve("tensor_tensor", out=V(selB.t[:, h * 4:(h + 1) * 4, :], selB.b), in0=V(selB.t[:, h * 4:(h + 1) * 4, :], selB.b),
                          in1=V(own.t[:, 4 * tt:4 * tt + 4, :], own.b), op=ALU.max)
                if 't' in SK:
                    return
                for h in range(2):
                    for s in range(4):
                        S.pe("transpose", out=V(PnT[h].t[0:32, s * 128:(s + 1) * 128], PnT[h].b), in_=V(selB.t[:, h * 4 + s, :], selB.b),
                             identity=small["identf"].all())
                    S.act("activation", out=V(Qp[h].t[64:96, sl], Qp[h].b), in_=V(PnT[h].t[0:32, :], PnT[h].b), func=AF.Identity,
                          bias=V(C["nbig"].t[0:32, 0:1], C["nbig"].b))

            def vcons(tt, pp, Vt=Vt):
                pv3 = pp.t[:].rearrange("p (n c) -> p n c", c=128)
                S.dve("tensor_copy", out=V(Vt.t[:, 4 * tt:4 * tt + 4, 0:64], Vt.b), in_=V(pv3[:, :, 0:64], pp.b))
                S.dve("tensor_copy", out=V(Vt.t[:, 4 * tt:4 * tt + 4, 128:192], Vt.b), in_=V(pv3[:, :, 64:128], pp.b))

            if "nbig" not in C:
                C["nbig"] = kb.sb("nbig", [128, 1])
                S.memset("pool", C["nbig"].all(), -BIG)
            proj_pass(kb, hT, T, wbf, [(1, "k"), (0, "q"), (3, "g")], PP, hts, cons, vchunk=2, vconsumer=vcons)
            if stop <= 3:
                S.emit()
                return kb.nc
            At = xcb + [kb.sb("At%d_%d" % (hp, i), [128, 512], BF16) for i in range(1)] if False else At_l1
            blocks = [(h, qt, kt) for h in range(2) for qt in range(NT) for kt in range(4 * qt + 4)]

            def qk_m(bi):
                h, qt, kt = blocks[bi]
                qsl = slice(qt * 512, (qt + 1) * 512)
                Z = P[bi % 4]
                diag = kt >= 4 * qt
                S.pe("matmul", out=Z.all(), lhsT=V(Kp[h].t[0:96, kt * 128:(kt + 1) * 128], Kp[h].b), rhs=V(Qp[h].t[0:96, qsl], Qp[h].b),
                     start=True, stop=not diag)
                if diag:
                    r = kt - 4 * qt
                    S.pe("matmul", out=Z.all(), lhsT=identb.all(), rhs=V(cmt.t[:, 384 - 128 * r:896 - 128 * r], cmt.b), start=False, stop=True)

            def rest_m(bi):
                h, qt, kt = blocks[bi]
                qsl = slice(qt * 512, (qt + 1) * 512)
                nk = 4 * qt + 4
                par = (h * NT + qt) % 4
                Pn = P[4 + par]
                Z = P[bi % 4]
                A = At[bi % 4]
                S.act("activation", out=A.all(), in_=Z.all(), func=AF.Exp)
                S.pe("matmul", out=Pn.all(), lhsT=V(Vt.t[:, kt, h * 64:h * 64 + 128], Vt.b), rhs=A.all(),
                     start=(kt == 0), stop=(kt == nk - 1))
                if kt == nk - 1:
                    nr = slice(h * 64, (h + 1) * 64)
                    dr = slice((1 - h) * 64, (2 - h) * 64)
                    S.dve("reciprocal", out=V(rden.t[nr, :], rden.b), in_=V(Pn.t[dr, :], Pn.b))
                    S.dve("tensor_tensor", out=V(ytmp.t[nr, :], ytmp.b), in0=V(Pn.t[nr, :], Pn.b), in1=V(rden.t[nr, :], rden.b), op=ALU.mult)
                    S.pool("tensor_tensor", out=V(ym.t[h * 64:(h + 1) * 64, qsl], ym.b), in0=V(ytmp.t[h * 64:(h + 1) * 64, :], ytmp.b),
                           in1=V(sga.t[h * 64:(h + 1) * 64, qsl], sga.b), op=ALU.mult)

            LOOK = 2
            for bi in range(min(LOOK, len(blocks))):
                qk_m(bi)
            for bi in range(len(blocks)):
                if bi + LOOK < len(blocks):
                    qk_m(bi + LOOK)
                rest_m(bi)
            S.dma(V(mixT.t[256 + hp * 128:256 + (hp + 1) * 128, :], mixT.b), ym.all(), q="pool")
            fence(S)
        S.emit()
    return kb.nc


def consts_l1(T):
    NQ = T // 128
    onesbd = np.zeros((128, 128), np.float32)
    onesbd[:64, :64] = 1.0
    onesbd[64:, 64:] = 1.0
    khot = np.zeros((32, T), np.float32)
    for jb in range(min(32, T // 256)):
        khot[jb, jb * 256:(jb + 1) * 256] = 1.0
    pm = np.zeros((128, NQ, 32), np.float32)
    own = np.zeros((128, NQ, 32), np.float32)
    for qi in range(NQ):
        qb = qi // 2
        pm[:, qi, qb:] = -1e9
        own[:, qi, qb] = BIG
    p = np.arange(128)[:, None]
    c = np.arange(896)[None, :]
    cm = np.where(p > c - 384, -BIG, 0.0).astype(np.float32)
    cms = np.where(p >= c - 384, -BIG, 0.0).astype(np.float32)
    return dict(onesbd=onesbd, khot=khot.astype(NPBF), pm128=pm.astype(NPBF), own128=own.astype(NPBF), cm=cm.astype(NPBF), cms=cms.astype(NPBF),
                identb=np.eye(128, dtype=np.float32).astype(NPBF), identf=np.eye(128, dtype=np.float32))


def bdiag2(w, g0):
    o = np.zeros((128, 128), np.float32)
    o[:64, :64] = w[g0]
    o[64:, 64:] = w[g0 + 1]
    return o


def prep_l1(inp, b, j, T, cst):
    f = np.float32
    wi = inp["ev_w_in"][0]
    cols = np.concatenate([np.arange(k * 512 + j * 256, k * 512 + (j + 1) * 256) for k in range(6)])
    lw = inp["ev_conv_w"][0][:, j * 256:(j + 1) * 256]
    m = {
        "xT": np.ascontiguousarray(inp["x"][b].T), "cl": lay128(inp["c"][b]), "adaw": inp["ev_ada_w"][0],
        "adab_sh": lay128(inp["ev_ada_b"][0][0:1024]), "adab_sc": lay128(inp["ev_ada_b"][0][1024:2048]),
        "normg": lay128(inp["ev_norm"][0]), "win": np.ascontiguousarray(wi[:, cols]),
        "convw": np.ascontiguousarray(lw.reshape(4, 2, 128).transpose(2, 1, 0)),
        "convb": np.ascontiguousarray(inp["ev_conv_b"][0][j * 256:(j + 1) * 256].reshape(2, 128).T),
        "rgw": np.stack([bdiag2(inp["ev_rgate_w"][0], j * 4 + 2 * ci) for ci in range(2)], axis=1),
        "igw": np.stack([bdiag2(inp["ev_igate_w"][0], j * 4 + 2 * ci) for ci in range(2)], axis=1),
        "rgb": np.ascontiguousarray(inp["ev_rgate_b"][0].reshape(-1)[j * 256:(j + 1) * 256].reshape(2, 128).T),
        "igb": np.ascontiguousarray(inp["ev_igate_b"][0].reshape(-1)[j * 256:(j + 1) * 256].reshape(2, 128).T),
        "lam": np.ascontiguousarray(inp["ev_lru_lambda"][0][j * 256:(j + 1) * 256].reshape(2, 128).T),
        "qg": np.tile(inp["ev_q_norm"][0], 2).reshape(128, 1), "kg": np.tile(inp["ev_k_norm"][0], 2).reshape(128, 1),
        "onesbd": cst["onesbd"], "khot": cst["khot"], "pm128": cst["pm128"], "own128": cst["own128"], "cm": cst["cm"],
        "identb": cst["identb"], "identf": cst["identf"],
    }
    return {k: np.ascontiguousarray(v) if v.dtype == NPBF else np.ascontiguousarray(v, dtype=f) for k, v in m.items()}


def build_l2(T, stop=99):
    kb = KB()
    S = kb.S
    NT = T // 512
    Tn = T // 2
    NTn = Tn // 512
    CS = 8192 + 32
    xT = kb.din("xT", [D, T])
    mix0T = kb.din("mix0T", [D, T], BF16)
    wout0 = kb.din("wout0", [D, D])
    cl_d = kb.din("cl", [128, 8])
    adaw0 = kb.din("adaw0", [D, 3 * D])
    adab_g0 = kb.din("adab_g0", [128, 8])
    adaw1 = kb.din("adaw1", [D, 3 * D])
    adab_sh = kb.din("adab_sh", [128, 8])
    adab_sc = kb.din("adab_sc", [128, 8])
    normg = kb.din("normg", [128, 8])
    win = kb.din("win", [D, 1536])
    qg = kb.din("qg", [128, 1])
    kg = kb.din("kg", [128, 1])
    onesbd_d = kb.din("onesbd", [128, 128])
    cms_d = kb.din("cms", [128, 896], BF16)
    identb_d = kb.din("identb", [128, 128], BF16)
    negtri_d = kb.din("negtri", [128, 128], BF16)
    lre_d = kb.din("lre", [128, 8])
    lim_d = kb.din("lim", [128, 8])
    lstep_d = kb.din("lstep", [128, 8])
    bre_d = kb.din("bre_l", [128, 8, 128])
    bim_d = kb.din("bim_l", [128, 8, 128])
    cre_d = kb.din("cre_p", [128, 8, 128])
    cim_d = kb.din("cim_p", [128, 8, 128])
    dvec_d = kb.din("dvec", [128, 2])
    x1T = kb.dout("x1T", [D, T], F32)
    ysbT = kb.dout("ysbT", [256, T], BF16)
    yT = kb.dout("yT", [256, T], BF16)
    sg5T = kb.dout("sg5T", [256, T], BF16)
    hT = kb.dscr("hT", [8, 128, T], BF16, nreg=NT)
    with kb.st:
        P, PR = kb.psum8()
        C = make_consts(kb, T)
        arena = kb.sb("arena", [128, 7 * CS], BF16)

        def cbf(i, nreg=1):
            return Tl(arena.t[:, i * CS:i * CS + T], nreg)

        stage = [kb.sb("stg%d" % i, [128, 128]) for i in range(2)]
        arena2 = kb.sb("arena2", [128, 17 * 1024], BF16)
        car = Carver(arena2.t)
        stage_big = [car.get([128, 1024]) for i in range(2)]
        wbf = kb.sb("wbf", [128, 4, 8, 128], BF16)
        hts = [kb.sb("hts%d" % i, [128, 8, 512], BF16) for i in range(2)]
        gs = kb.sb("gs", [128, 8])
        sh = kb.sb("shf", [128, 8])
        sc1 = kb.sb("sc1", [128, 8])
        ng = kb.sb("ng", [128, 8])
        gate0 = kb.sb("gate0", [128, 8])
        tf = [kb.sb("tf%d" % i, [128, 512]) for i in range(4)]
        rstd = kb.sb("rstd", [128, 512])
        small = {}
        for nm, src, shp in (("qg", qg, [128, 1]), ("kg", kg, [128, 1]), ("onesbd", onesbd_d, [128, 128]),
                             ("lre", lre_d, [128, 8]), ("lim", lim_d, [128, 8]), ("lstep", lstep_d, [128, 8]), ("dvec", dvec_d, [128, 2])):
            small[nm] = kb.sb("c_" + nm, shp)
            S.dma(small[nm].all(), src.all())
        C["onesbd"] = small["onesbd"]
        cmt = kb.sb("cmt", [128, 896], BF16)
        S.dma(cmt.all(), cms_d.all())
        identb = kb.sb("identb_s", [128, 128], BF16)
        S.dma(identb.all(), identb_d.all())
        negtri = kb.sb("negtri_s", [128, 128], BF16)
        S.dma(negtri.all(), negtri_d.all())
        negones = kb.sb("negones", [128, 128], BF16)
        S.memset("pool", negones.all(), -1.0)

        adaw_sb = Tl(arena.t[:, 0:2 * CS].bitcast(F32)[:, 0:8192].rearrange("p (k n) -> p k n", k=8))
        ada_part(kb, cl_d, adaw0, adab_g0, 2, P[0], gate0, adaw_sb, tag="g0")
        ada_part(kb, cl_d, adaw1, adab_sc, 1, P[1], sc1, adaw_sb, plus_one=True, tag="a")
        ada_part(kb, cl_d, adaw1, adab_sh, 0, P[2], sh, adaw_sb, tag="b")
        S.dma(ng.all(), normg.all())
        S.dve("tensor_tensor", out=gs.all(), in0=ng.all(), in1=sc1.all(), op=ALU.mult)
        fence(S)
        wout_bf = Tl(arena.t[:, 0:8192].rearrange("p (k n) -> p k n", k=8))
        load_cast_weight(kb, wout0, wout_bf, 8, 1024, stage_big)
        mts = [Tl(arena.t[:, CS + i * 4096:CS + (i + 1) * 4096].rearrange("p (c t) -> p c t", c=8)) for i in range(2)]
        xts = [Tl(arena.t[:, (2 + 2 * i) * CS:(4 + 2 * i) * CS].bitcast(F32)[:, 0:4096].rearrange("p (c t) -> p c t", c=8)) for i in range(2)]
        sq = Tl(arena.t[:, 6 * CS:6 * CS + 4096].rearrange("p (c t) -> p c t", c=8))
        xv = xT.t.rearrange("(c p) t -> p c t", p=128)
        x1v = x1T.t.rearrange("(c p) t -> p c t", p=128)
        mv = mix0T.t.rearrange("(c p) t -> p c t", p=128)
        hv = hT.t.rearrange("c p t -> p c t")
        for tt in range(NT):
            sl = slice(tt * 512, (tt + 1) * 512)
            xt = xts[tt % 2]
            mt = mts[tt % 2]
            S.dma(xt.all(), V(xv[:, :, sl], xT.b))
            S.dma(mt.all(), V(mv[:, :, sl], mix0T.b))
            for d in range(8):
                po = P[4 + d % 4]
                for e in range(8):
                    S.pe("matmul", out=po.all(), lhsT=V(wout_bf.t[:, e, d * 128:(d + 1) * 128], wout_bf.b), rhs=V(mt.t[:, e, :], mt.b),
                         start=(e == 0), stop=(e == 7))
                S.dve("scalar_tensor_tensor", out=V(xt.t[:, d, :], xt.b), in0=po.all(), scalar=V(gate0.t[:, d:d + 1], gate0.b),
                      in1=V(xt.t[:, d, :], xt.b), op0=ALU.mult, op1=ALU.add)
            S.dma(V(x1v[:, :, sl], x1T.b), xt.all(), q="pool")
            ht = hts[tt % 2]
            norm_tile(kb, C, xt, ht, sq, rstd, P[2 + tt % 2], gs, sh, tf)
            S.dma(V(hv[:, :, sl], hT.b[tt]), ht.all(), q="pool")
        fence(S)
        if stop <= 1:
            S.emit()
            return kb.nc

        car.reset()
        acc = Tl(car.get([128, 512]).t[0:64, :])
        Dt = Tl(car.get([128, 512]).t[0:64, :])
        ytmp = car.get([128, 512])
        ef = [car.get([128, 1024]) for i in range(2)]
        spb = [car.get([128, 1024], BF16) for i in range(4)]
        At = [car.get([128, 1024], BF16) for i in range(2)]
        ones64 = V(C["ones_bf"].t[:, 0:64], C["ones_bf"].b)
        for hp in range(2):
            load_w_slots(kb, win, wbf, [(0, hp * 128), (1, 256 + hp * 128), (2, 512 + hp * 128), (3, 768 + hp * 128)], stage)
            Qh = [Tl(arena.t[0:64, (0 + h) * CS:(0 + h) * CS + T]) for h in range(2)]
            Kh = [Tl(arena.t[0:64, (2 + h) * CS:(2 + h) * CS + T]) for h in range(2)]
            Vt = Tl(arena.t[:, 4 * CS:4 * CS + T].rearrange("p (n c) -> p n c", c=128))
            sga = cbf(5)
            ym = cbf(6)
            PP = {"k": P[0], "q": P[1], "g": P[2], "v": P[3]}
            Pss = P[4]

            def cons(tag, tt, pp, Qh=Qh, Kh=Kh, sga=sga):
                sl = slice(tt * 512, (tt + 1) * 512)
                if tag == "g":
                    S.act("activation", out=V(sga.t[:, sl], sga.b), in_=pp.all(), func=AF.Silu)
                    return
                sqf, rs, hf = tf[0], tf[1], tf[2]
                headnorm(kb, C, pp, Pss, small["kg" if tag == "k" else "qg"], sqf, rs, hf)
                for h in range(2):
                    if tag == "k":
                        S.op(("pool", "dve")[h], "tensor_copy", out=V(Kh[h].t[0:64, sl], Kh[h].b), in_=V(hf.t[h * 64:(h + 1) * 64, :], hf.b))
                    else:
                        S.act("activation", out=V(Qh[h].t[0:64, sl], Qh[h].b), in_=V(hf.t[h * 64:(h + 1) * 64, :], hf.b), func=AF.Copy, scale=0.125)

            def vcons(tt, pp, Vt=Vt):
                S.dve("tensor_copy", out=V(Vt.t[:, 4 * tt:4 * tt + 4, :], Vt.b), in_=V(pp.t[:].rearrange("p (n c) -> p n c", c=128), pp.b))

            proj_pass(kb, hT, T, wbf, [(1, "k"), (0, "q"), (3, "g")], PP, hts, cons, vchunk=2, vconsumer=vcons)
            groups = [(h, qt, g) for h in range(2) for qt in range(NT) for g in range(qt + 1)]
            pairs = [(gi, ph) for gi in range(len(groups)) for ph in (0, 1)]
            NPR = len(pairs)
            PO, PB = P[6], P[7]

            def pinfo(pi):
                gi, ph = pairs[pi]
                h, qt, g = groups[gi]
                return gi, ph, h, qt, g

            def spp(gi, ph):
                return spb[(gi % 2) * 2 + ph]

            def sp_tile(gi, kl):
                ph, half = (0, 3 - kl) if kl >= 2 else (1, 1 - kl)
                t = spp(gi, ph)
                return V(t.t[:, half * 512:(half + 1) * 512], t.b)

            def s1_pe(pi):
                gi, ph, h, qt, g = pinfo(pi)
                qsl = slice(qt * 512, (qt + 1) * 512)
                for half in range(2):
                    kl = 3 - 2 * ph - half
                    kt = 4 * g + kl
                    Z = P[2 * (pi % 3) + half]
                    S.pe("matmul", out=Z.all(), lhsT=V(Kh[h].t[0:64, kt * 128:(kt + 1) * 128], Kh[h].b), rhs=V(Qh[h].t[0:64, qsl], Qh[h].b),
                         start=True, stop=(g != qt))
                    if g == qt:
                        r = kt - 4 * qt
                        S.pe("matmul", out=Z.all(), lhsT=identb.all(), rhs=V(cmt.t[:, 384 - 128 * r:896 - 128 * r], cmt.b), start=False, stop=True)

            def s1_act(pi):
                gi, ph, h, qt, g = pinfo(pi)
                e = ef[pi % 2]
                S.act("activation", out=e.all(), in_=PR[pi % 3].all(), func=AF.Exp)
                S.act("activation", out=spp(gi, ph).all(), in_=e.all(), func=AF.Ln, bias=V(C["one"].t[:, 0:1], C["one"].b))

            def s2_pe(pi):
                gi, ph, h, qt, g = pinfo(pi)
                for half in range(2):
                    kl = 3 - 2 * ph - half
                    Z = P[2 * (pi % 3) + half]
                    S.pe("matmul", out=Z.all(), lhsT=negtri.all(), rhs=sp_tile(gi, kl), start=False, stop=(kl == 3), skip_group_check=True)
                    for k2 in range(kl + 1, 4):
                        S.pe("matmul", out=Z.all(), lhsT=negones.all(), rhs=sp_tile(gi, k2), start=False, stop=(k2 == 3), skip_group_check=True)

            def s2_act(pi):
                S.act("activation", out=At[pi % 2].all(), in_=PR[pi % 3].all(), func=AF.Exp)

            def s3_pe(pi):
                gi, ph, h, qt, g = pinfo(pi)
                qsl = slice(qt * 512, (qt + 1) * 512)
                A = At[pi % 2]
                for half in range(2):
                    kl = 3 - 2 * ph - half
                    kt = 4 * g + kl
                    S.pe("matmul", out=V(PO.t[0:64, :], PO.b), lhsT=V(Vt.t[:, kt, h * 64:(h + 1) * 64], Vt.b), rhs=V(A.t[:, half * 512:(half + 1) * 512], A.b),
                         start=(kl == 3), stop=(kl == 0))
                    if g > 0:
                        S.pe("matmul", out=V(PB.t[0:64, :], PB.b), lhsT=ones64, rhs=sp_tile(gi, kl), start=(kl == 3), stop=(kl == 0))
                if ph != 1:
                    return
                if g == 0:
                    S.dve("tensor_copy", out=acc.all(), in_=V(PO.t[0:64, :], PO.b))
                else:
                    S.act("activation", out=Dt.all(), in_=V(PB.t[0:64, :], PB.b), func=AF.Exp, scale=-1.0)
                    S.dve("tensor_tensor", out=acc.all(), in0=acc.all(), in1=Dt.all(), op=ALU.mult)
                    S.dve("tensor_tensor", out=acc.all(), in0=acc.all(), in1=V(PO.t[0:64, :], PO.b), op=ALU.add)
                if g == qt:
                    S.dve("tensor_copy", out=V(ytmp.t[h * 64:(h + 1) * 64, :], ytmp.b), in_=acc.all())
                    S.pool("tensor_tensor", out=V(ym.t[h * 64:(h + 1) * 64, qsl], ym.b), in0=V(ytmp.t[h * 64:(h + 1) * 64, :], ytmp.b),
                           in1=V(sga.t[h * 64:(h + 1) * 64, qsl], sga.b), op=ALU.mult)

            s1_pe(0)
            s1_pe(1)
            s1_act(0)
            for i in range(NPR + 1):
                if i + 2 < NPR:
                    s1_pe(i + 2)
                if i < NPR:
                    s2_pe(i)
                if i + 1 < NPR:
                    s1_act(i + 1)
                if i < NPR:
                    s2_act(i)
                if i >= 1:
                    s3_pe(i - 1)
            S.dma(V(ysbT.t[hp * 128:(hp + 1) * 128, :], ysbT.b), ym.all(), q="pool")
            fence(S)
        if stop <= 2:
            S.emit()
            return kb.nc
        car.reset()
        build_s5(kb, C, P, arena, CS, T, win, wbf, stage, hts, hT, small, bre_d, bim_d, cre_d, cim_d, yT, sg5T, tf, car)
        S.emit()
    return kb.nc


def build_s5(kb, C, P, arena, CS, T, win, wbf, stage, hts, hT, small, bre_d, bim_d, cre_d, cim_d, yT, sg5T, tf, car):
    S = kb.S
    NT = T // 512
    Tn = T // 2
    NTn = Tn // 512
    import math
    load_w_slots(kb, win, wbf, [(0, 1024), (1, 1152), (2, 1280), (3, 1408)], stage)
    ubf = [Tl(arena.t[:, i * CS:i * CS + T]) for i in range(2)]
    sgt = [car.get([128, 512], BF16) for i in range(2)]
    cnt = [0]

    def cons(tag, tt, pp):
        sl = slice(tt * 512, (tt + 1) * 512)
        if tag[0] == "u":
            ci = int(tag[1])
            S.act("activation", out=V(ubf[ci].t[:, sl], ubf[ci].b), in_=pp.all(), func=AF.Copy)
        else:
            ci = int(tag[1])
            st = sgt[cnt[0] % 2]
            cnt[0] += 1
            S.act("activation", out=st.all(), in_=pp.all(), func=AF.Silu)
            S.dma(V(sg5T.t[ci * 128:(ci + 1) * 128, sl], sg5T.b), st.all(), q="pool")

    proj_pass(kb, hT, T, wbf, [(0, "u0"), (1, "u1"), (2, "g0"), (3, "g1")], P[0:4], hts, cons)

    def sm(name, shape=(128, 8)):
        return kb.sb("s5_" + name, list(shape))

    lre, lim, lstep = small["lre"], small["lim"], small["lstep"]
    step, lrs, th, rho = sm("step"), sm("lrs"), sm("th"), sm("rho")
    cc, ss, t1, t2 = sm("cc"), sm("ss"), sm("t1"), sm("t2")
    hpi = sm("hpi", (128, 1))
    S.memset("pool", hpi.all(), math.pi / 2)
    S.act("activation", out=step.all(), in_=lstep.all(), func=AF.Exp)
    S.dve("tensor_tensor", out=lrs.all(), in0=lre.all(), in1=step.all(), op=ALU.mult)
    S.dve("tensor_tensor", out=th.all(), in0=lim.all(), in1=step.all(), op=ALU.mult)
    S.act("activation", out=rho.all(), in_=lrs.all(), func=AF.Exp)
    S.act("activation", out=ss.all(), in_=th.all(), func=AF.Sin, scale=1.0 / 32)
    S.act("activation", out=cc.all(), in_=th.all(), func=AF.Sin, scale=1.0 / 32, bias=V(hpi.t[:, 0:1], hpi.b))

    def dbl(c, s, ta, tb):
        S.dve("tensor_tensor", out=ta.all(), in0=c.all(), in1=c.all(), op=ALU.mult)
        S.dve("tensor_tensor", out=tb.all(), in0=s.all(), in1=s.all(), op=ALU.mult)
        S.dve("scalar_tensor_tensor", out=s.all(), in0=s.all(), scalar=2.0, in1=c.all(), op0=ALU.mult, op1=ALU.mult)
        S.dve("tensor_tensor", out=c.all(), in0=ta.all(), in1=tb.all(), op=ALU.subtract)

    for _ in range(5):
        dbl(cc, ss, t1, t2)
    abr, abi, den, fre, fim = sm("abr"), sm("abi"), sm("den"), sm("fre"), sm("fim")
    S.dve("tensor_tensor", out=abr.all(), in0=rho.all(), in1=cc.all(), op=ALU.mult)
    S.dve("tensor_scalar", out=abr.all(), in0=abr.all(), scalar1=-1.0, scalar2=None, op0=ALU.add)
    S.dve("tensor_tensor", out=abi.all(), in0=rho.all(), in1=ss.all(), op=ALU.mult)
    S.dve("tensor_tensor", out=t1.all(), in0=lre.all(), in1=lre.all(), op=ALU.mult)
    S.dve("tensor_tensor", out=t2.all(), in0=lim.all(), in1=lim.all(), op=ALU.mult)
    S.dve("tensor_tensor", out=den.all(), in0=t1.all(), in1=t2.all(), op=ALU.add)
    S.dve("reciprocal", out=den.all(), in_=den.all())
    S.dve("tensor_tensor", out=t1.all(), in0=abr.all(), in1=lre.all(), op=ALU.mult)
    S.dve("tensor_tensor", out=t2.all(), in0=abi.all(), in1=lim.all(), op=ALU.mult)
    S.dve("tensor_tensor", out=fre.all(), in0=t1.all(), in1=t2.all(), op=ALU.add)
    S.dve("tensor_tensor", out=fre.all(), in0=fre.all(), in1=den.all(), op=ALU.mult)
    S.dve("tensor_tensor", out=t1.all(), in0=abi.all(), in1=lre.all(), op=ALU.mult)
    S.dve("tensor_tensor", out=t2.all(), in0=abr.all(), in1=lim.all(), op=ALU.mult)
    S.dve("tensor_tensor", out=fim.all(), in0=t1.all(), in1=t2.all(), op=ALU.subtract)
    S.dve("tensor_tensor", out=fim.all(), in0=fim.all(), in1=den.all(), op=ALU.mult)
    nfim = sm("nfim")
    S.dve("tensor_scalar", out=nfim.all(), in0=fim.all(), scalar1=-1.0, scalar2=None, op0=ALU.mult)
    nfre = sm("nfre")
    S.dve("tensor_scalar", out=nfre.all(), in0=fre.all(), scalar1=-1.0, scalar2=None, op0=ALU.mult)
    bst = car.get([128, 8, 128])
    Bre = car.get([128, 8, 128], BF16)
    Bim = car.get([128, 8, 128], BF16)
    S.dma(bst.all(), bre_d.all())
    S.dve("tensor_copy", out=Bre.all(), in_=bst.all())
    S.dma(bst.all(), bim_d.all())
    S.dve("tensor_copy", out=Bim.all(), in_=bst.all())
    cst = car.get([128, 8, 128])
    C1 = car.get([128, 8, 128], BF16)
    C2 = car.get([128, 8, 128], BF16)
    ctm = car.get([128, 128])
    S.dma(bst.all(), cre_d.all())
    S.dma(cst.all(), cim_d.all())
    for sc in range(8):
        S.dve("tensor_scalar", out=ctm.all(), in0=V(cst.t[:, sc, :], cst.b), scalar1=V(nfim.t[:, sc:sc + 1], nfim.b), scalar2=None, op0=ALU.mult)
        S.dve("scalar_tensor_tensor", out=V(C1.t[:, sc, :], C1.b), in0=V(bst.t[:, sc, :], bst.b), scalar=V(fre.t[:, sc:sc + 1], fre.b),
              in1=ctm.all(), op0=ALU.mult, op1=ALU.add)
        S.dve("tensor_scalar", out=ctm.all(), in0=V(cst.t[:, sc, :], cst.b), scalar1=V(nfre.t[:, sc:sc + 1], nfre.b), scalar2=None, op0=ALU.mult)
        S.dve("scalar_tensor_tensor", out=V(C2.t[:, sc, :], C2.b), in0=V(bst.t[:, sc, :], bst.b), scalar=V(nfim.t[:, sc:sc + 1], nfim.b),
              in1=ctm.all(), op0=ALU.mult, op1=ALU.add)

    def f32v(i0, n):
        return arena.t[:, i0 * CS:(i0 + 1) * CS].bitcast(F32)[:, 0:n]

    tabC = Tl(f32v(2, Tn))
    tabS = Tl(f32v(3, Tn))
    wre = Tl(f32v(4, Tn))
    wim = Tl(f32v(5, Tn))
    tmpA = Tl(arena.t[:, 6 * CS:7 * CS].bitcast(F32)[:, 0:max(Tn // 2, 512)])
    tmpB = Tl(arena.t[:, 6 * CS:7 * CS].bitcast(F32)[:, 2048:2048 + max(Tn // 2, 512)])
    En = [kb.sb("s5_En%d" % i, [128, 2]) for i in range(2)]
    e1, e2 = sm("e1", (128, 1)), sm("e2", (128, 1))
    ini = sm("ini", (128, 2))
    rt = [car.get([128, 512]) for i in range(4)]
    xr = [car.get([128, 512], BF16) for i in range(2)]
    xi = [car.get([128, 512], BF16) for i in range(2)]
    yt = [car.get([128, 512], BF16) for i in range(2)]
    ny = 0
    for sc in range(8):
        uc, r0 = sc // 4, 32 * (sc % 4)
        S.memset("pool", V(tabC.t[:, 0:1], tabC.b), 1.0)
        S.memset("pool", V(tabS.t[:, 0:1], tabS.b), 0.0)
        cur = En[0]
        S.dve("tensor_copy", out=V(cur.t[:, 0:1], cur.b), in_=V(cc.t[:, sc:sc + 1], cc.b))
        S.dve("tensor_copy", out=V(cur.t[:, 1:2], cur.b), in_=V(ss.t[:, sc:sc + 1], ss.b))
        n = 1
        k = 0
        while n < Tn:
            cn = V(cur.t[:, 0:1], cur.b)
            sn = V(cur.t[:, 1:2], cur.b)
            S.dve("tensor_scalar", out=V(tmpA.t[:, 0:n], tmpA.b), in0=V(tabS.t[:, 0:n], tabS.b), scalar1=sn, scalar2=None, op0=ALU.mult)
            S.dve("scalar_tensor_tensor", out=V(tabC.t[:, n:2 * n], tabC.b), in0=V(tabC.t[:, 0:n], tabC.b), scalar=cn,
                  in1=V(tmpA.t[:, 0:n], tmpA.b), op0=ALU.mult, op1=ALU.subtract)
            S.pool("tensor_scalar", out=V(tmpB.t[:, 0:n], tmpB.b), in0=V(tabC.t[:, 0:n], tabC.b), scalar1=sn, scalar2=0.0, op0=ALU.mult, op1=ALU.add)
            S.dve("scalar_tensor_tensor", out=V(tabS.t[:, n:2 * n], tabS.b), in0=V(tabS.t[:, 0:n], tabS.b), scalar=cn,
                  in1=V(tmpB.t[:, 0:n], tmpB.b), op0=ALU.mult, op1=ALU.add)
            nxt = En[(k + 1) % 2]
            S.dve("tensor_tensor", out=e1.all(), in0=cn, in1=cn, op=ALU.mult)
            S.dve("tensor_tensor", out=e2.all(), in0=sn, in1=sn, op=ALU.mult)
            S.dve("tensor_tensor", out=V(nxt.t[:, 0:1], nxt.b), in0=e1.all(), in1=e2.all(), op=ALU.subtract)
            S.dve("scalar_tensor_tensor", out=V(nxt.t[:, 1:2], nxt.b), in0=sn, scalar=2.0, in1=cn, op0=ALU.mult, op1=ALU.mult)
            cur = nxt
            k += 1
            n *= 2
        ETn = cur
        for hh in range(2):
            for i in range(NTn):
                ls = slice(i * 512, (i + 1) * 512)
                gsl = slice(hh * Tn + i * 512, hh * Tn + (i + 1) * 512)
                Pr, Pi = P[(2 * i) % 4], P[(2 * i + 1) % 4]
                S.pe("matmul", out=Pr.all(), lhsT=V(Bre.t[:, sc, :], Bre.b), rhs=V(ubf[uc].t[:, gsl], ubf[uc].b), start=True, stop=True)
                S.pe("matmul", out=Pi.all(), lhsT=V(Bim.t[:, sc, :], Bim.b), rhs=V(ubf[uc].t[:, gsl], ubf[uc].b), start=True, stop=True)
                a0, a1, a2, a3 = rt
                S.dve("tensor_tensor", out=a0.all(), in0=Pr.all(), in1=V(tabC.t[:, ls], tabC.b), op=ALU.mult)
                S.dve("tensor_tensor", out=a1.all(), in0=Pi.all(), in1=V(tabS.t[:, ls], tabS.b), op=ALU.mult)
                S.pool("tensor_tensor", out=V(wre.t[:, ls], wre.b), in0=a0.all(), in1=a1.all(), op=ALU.add)
                S.dve("tensor_tensor", out=a2.all(), in0=Pi.all(), in1=V(tabC.t[:, ls], tabC.b), op=ALU.mult)
                S.dve("tensor_tensor", out=a3.all(), in0=Pr.all(), in1=V(tabS.t[:, ls], tabS.b), op=ALU.mult)
                S.pool("tensor_tensor", out=V(wim.t[:, ls], wim.b), in0=a2.all(), in1=a3.all(), op=ALU.subtract)
            rho_b = V(rho.t[:, sc:sc + 1].to_broadcast([128, Tn]), rho.b)
            if hh == 0:
                i_re, i_im = 0.0, 0.0
            else:
                i_re, i_im = V(ini.t[:, 0:1], ini.b), V(ini.t[:, 1:2], ini.b)
            S.dve("tensor_tensor_scan", out=wre.all(), data0=rho_b, data1=wre.all(), initial=i_re, op0=ALU.mult, op1=ALU.add)
            S.dve("tensor_tensor_scan", out=wim.all(), data0=rho_b, data1=wim.all(), initial=i_im, op0=ALU.mult, op1=ALU.add)
            if hh == 0:
                cT, sT = V(ETn.t[:, 0:1], ETn.b), V(ETn.t[:, 1:2], ETn.b)
                lr, li = V(wre.t[:, Tn - 1:Tn], wre.b), V(wim.t[:, Tn - 1:Tn], wim.b)
                S.dve("tensor_tensor", out=e1.all(), in0=li, in1=sT, op=ALU.mult)
                S.dve("scalar_tensor_tensor", out=V(ini.t[:, 0:1], ini.b), in0=lr, scalar=cT, in1=e1.all(), op0=ALU.mult, op1=ALU.subtract)
                S.dve("tensor_tensor", out=e2.all(), in0=li, in1=cT, op=ALU.mult)
                S.dve("scalar_tensor_tensor", out=V(ini.t[:, 1:2], ini.b), in0=lr, scalar=sT, in1=e2.all(), op0=ALU.mult, op1=ALU.add)
            for i in range(NTn):
                ls = slice(i * 512, (i + 1) * 512)
                gsl = slice(hh * Tn + i * 512, hh * Tn + (i + 1) * 512)
                a0, a1, a2, a3 = rt
                X, Xi = xr[i % 2], xi[i % 2]
                S.pool("tensor_tensor", out=a0.all(), in0=V(wre.t[:, ls], wre.b), in1=V(tabC.t[:, ls], tabC.b), op=ALU.mult)
                S.pool("tensor_tensor", out=a1.all(), in0=V(wim.t[:, ls], wim.b), in1=V(tabS.t[:, ls], tabS.b), op=ALU.mult)
                S.dve("tensor_tensor", out=X.all(), in0=a0.all(), in1=a1.all(), op=ALU.subtract)
                S.pool("tensor_tensor", out=a2.all(), in0=V(wre.t[:, ls], wre.b), in1=V(tabS.t[:, ls], tabS.b), op=ALU.mult)
                S.dve("tensor_tensor", out=a3.all(), in0=V(wim.t[:, ls], wim.b), in1=V(tabC.t[:, ls], tabC.b), op=ALU.mult)
                S.dve("tensor_tensor", out=Xi.all(), in0=a2.all(), in1=a3.all(), op=ALU.add)
                Py = P[4 + i % 2]
                S.pe("matmul", out=Py.all(), lhsT=V(C1.t[:, sc, :], C1.b), rhs=X.all(), start=True, stop=False)
                S.pe("matmul", out=Py.all(), lhsT=V(C2.t[:, sc, :], C2.b), rhs=Xi.all(), start=False, stop=True)
                Y = yt[ny % 2]
                ny += 1
                S.dve("scalar_tensor_tensor", out=V(Y.t[r0:r0 + 32, :], Y.b), in0=V(ubf[uc].t[r0:r0 + 32, gsl], ubf[uc].b),
                      scalar=V(small["dvec"].t[r0:r0 + 32, uc:uc + 1], small["dvec"].b), in1=V(Py.t[r0:r0 + 32, :], Py.b), op0=ALU.mult, op1=ALU.add)
                S.dma(V(yT.t[uc * 128 + r0:uc * 128 + r0 + 32, gsl], yT.b), V(Y.t[r0:r0 + 32, :], Y.b), q="pool")


def prep_l2(inp, b, j, T, cst, mix0T_b):
    f = np.float32
    wi = inp["od_w_in"][0]
    cols = np.concatenate([np.arange(k * 512 + j * 256, k * 512 + (j + 1) * 256) for k in range(6)])
    G0 = 16 * j
    lre = np.zeros((128, 8), f)
    lim = np.zeros((128, 8), f)
    lstep = np.zeros((128, 8), f)
    bre = np.zeros((128, 8, 128), f)
    bim = np.zeros((128, 8, 128), f)
    cre = np.zeros((128, 8, 128), f)
    cim = np.zeros((128, 8, 128), f)
    for sc in range(8):
        r0 = 32 * (sc % 4)
        for gl in range(2):
            g = G0 + 2 * sc + gl
            lre[gl * 64:(gl + 1) * 64, sc] = inp["od_s5_lambda_re"][0][g]
            lim[gl * 64:(gl + 1) * 64, sc] = inp["od_s5_lambda_im"][0][g]
            lstep[gl * 64:(gl + 1) * 64, sc] = inp["od_s5_log_step"][0][g]
            bre[r0 + gl * 16:r0 + (gl + 1) * 16, sc, gl * 64:(gl + 1) * 64] = inp["od_s5_b_re"][0][g].T
            bim[r0 + gl * 16:r0 + (gl + 1) * 16, sc, gl * 64:(gl + 1) * 64] = inp["od_s5_b_im"][0][g].T
            cre[gl * 64:(gl + 1) * 64, sc, r0 + gl * 16:r0 + (gl + 1) * 16] = inp["od_s5_c_re"][0][g].T
            cim[gl * 64:(gl + 1) * 64, sc, r0 + gl * 16:r0 + (gl + 1) * 16] = inp["od_s5_c_im"][0][g].T
    dflat = inp["od_s5_d"][0].reshape(-1)[j * 256:(j + 1) * 256]
    p = np.arange(128)[:, None]
    s_ = np.arange(128)[None, :]
    negtri = np.where(p >= s_, -1.0, 0.0).astype(f)
    m = {
        "xT": np.ascontiguousarray(inp["x"][b].T), "mix0T": mix0T_b, "wout0": inp["ev_w_out"][0], "cl": lay128(inp["c"][b]),
        "adaw0": inp["ev_ada_w"][0], "adab_g0": lay128(inp["ev_ada_b"][0][2048:]), "adaw1": inp["od_ada_w"][0],
        "adab_sh": lay128(inp["od_ada_b"][0][0:1024]), "adab_sc": lay128(inp["od_ada_b"][0][1024:2048]),
        "normg": lay128(inp["od_norm"][0]), "win": np.ascontiguousarray(wi[:, cols]),
        "qg": np.tile(inp["od_q_norm"][0], 2).reshape(128, 1), "kg": np.tile(inp["od_k_norm"][0], 2).reshape(128, 1),
        "onesbd": cst["onesbd"], "cms": cst["cms"], "identb": cst["identb"], "negtri": negtri.astype(NPBF),
        "lre": lre, "lim": lim, "lstep": lstep, "bre_l": bre, "bim_l": bim, "cre_p": cre, "cim_p": cim,
        "dvec": np.ascontiguousarray(dflat.reshape(2, 128).T),
    }
    return {k: np.ascontiguousarray(v) if v.dtype == NPBF else np.ascontiguousarray(v, dtype=f) for k, v in m.items()}


_CACHE = {}


def _get(name, fn, *a):
    key = (name,) + a
    if key not in _CACHE:
        _CACHE[key] = fn(*a)
    return _CACHE[key]


def kernel(**inp):
    inp = {k: np.asarray(v) for k, v in inp.items()}
    T = inp["x"].shape[1]
    cst = consts_l1(T)
    cores = [(b, j) for b in range(NB) for j in range(2)]
    ids = list(range(8))
    nc1 = build_l1(T)
    r1 = run_bass_kernel_spmd(nc1, [prep_l1(inp, b, j, T, cst) for (b, j) in cores], core_ids=ids).results
    mix0T = []
    for b in range(NB):
        a, c = r1[2 * b]["mixT"], r1[2 * b + 1]["mixT"]
        mix0T.append(np.ascontiguousarray(np.concatenate([a[0:256], c[0:256], a[256:512], c[256:512]], axis=0)))
    del r1
    nc2 = build_l2(T)
    r2 = run_bass_kernel_spmd(nc2, [prep_l2(inp, b, j, T, cst, mix0T[b]) for (b, j) in cores], core_ids=ids).results
    Th = T // 2
    nc3 = build_l3(Th)
    maps = []
    for (b, j) in cores:
        sl = slice(j * Th, (j + 1) * Th)
        cat = lambda nm: np.ascontiguousarray(np.concatenate([r2[2 * b][nm][:, sl], r2[2 * b + 1][nm][:, sl]], axis=0))
        maps.append({
            "x1T": np.ascontiguousarray(r2[2 * b]["x1T"][:, sl]), "ysbT": cat("ysbT"), "yT": cat("yT"), "sg5T": cat("sg5T"),
            "gluw": np.ascontiguousarray(inp["od_glu_w"][0], dtype=np.float32), "glub": lay128(inp["od_glu_b"][0]),
            "wout": np.ascontiguousarray(inp["od_w_out"][0], dtype=np.float32), "cl": lay128(inp["c"][b]),
            "adaw": np.ascontiguousarray(inp["od_ada_w"][0], dtype=np.float32), "adab_g": lay128(inp["od_ada_b"][0][2048:]),
        })
    del r2
    r3 = run_bass_kernel_spmd(nc3, maps, core_ids=ids).results
    out = np.empty((NB, T, D), np.float32)
    for i, (b, j) in enumerate(cores):
        out[b, j * Th:(j + 1) * Th, :] = r3[i]["outT"].T
    return out
```

```python
import contextlib
import numpy as np
import ml_dtypes
import concourse.bass as bass
import concourse.mybir as mybir
from concourse.bass_utils import run_bass_kernel_spmd

F32 = mybir.dt.float32
BF16 = mybir.dt.bfloat16
ALU = mybir.AluOpType
AF = mybir.ActivationFunctionType
NPBF = ml_dtypes.bfloat16

D = 1024
NB = 4
SEQ = 8192
EPS = 1e-6
BIG = 32768.0

CENG = ("pe", "act", "dve", "pool")


class Buf:
    __slots__ = ("last_w", "readers")

    def __init__(self):
        self.last_w = None
        self.readers = []


class V:
    __slots__ = ("ap", "bufs")

    def __init__(self, ap, *bufs):
        self.ap = ap
        bl = []
        for b in bufs:
            if isinstance(b, (list, tuple)):
                bl.extend(b)
            else:
                bl.append(b)
        self.bufs = bl


class Tl:
    def __init__(self, h, nreg=1):
        self.t = h
        self.b = [Buf() for _ in range(nreg)]

    def all(self):
        return V(self.t[:], self.b)


class Op:
    __slots__ = ("eng", "fn", "waits", "dwaits", "signal", "idx", "is_dma", "dma_sem", "dma_thr", "pre_wait")

    def __init__(self, eng, fn):
        self.eng = eng
        self.fn = fn
        self.waits = {}
        self.dwaits = []
        self.signal = False
        self.is_dma = False
        self.dma_sem = None
        self.dma_thr = None
        self.pre_wait = None


class Sched:
    NDMA = 16

    def __init__(self, nc):
        self.nc = nc
        self.ops = {e: [] for e in CENG + ("sp",)}
        self.known = {e: {} for e in CENG + ("sp",)}
        self.kdma = {e: set() for e in CENG + ("sp",)}
        self.ndma = {"sp": 0, "pool": 0, "act": 0}

    def _add(self, eng, fn, reads, writes, is_dma=False):
        op = Op(eng, fn)
        op.is_dma = is_dma
        lst = self.ops[eng]
        op.idx = len(lst)
        deps = []
        for b in reads:
            if b.last_w is not None:
                deps.append((b.last_w, "raw"))
        for b in writes:
            if b.last_w is not None:
                deps.append((b.last_w, "waw"))
            for r in b.readers:
                deps.append((r, "war"))
        for src, kind in deps:
            if src is op:
                continue
            if src.is_dma:
                if id(src) in self.kdma[eng]:
                    continue
                self.kdma[eng].add(id(src))
                op.dwaits.append(src)
                continue
            se = src.eng
            if se == eng and not is_dma:
                if eng == "pe":
                    continue
            if self.known[eng].get(se, -1) >= src.idx:
                continue
            if op.waits.get(se, -1) < src.idx:
                op.waits[se] = src.idx
        for k, vv in op.waits.items():
            self.known[eng][k] = max(self.known[eng].get(k, -1), vv)
        lst.append(op)
        for b in reads:
            b.readers.append(op)
        for b in writes:
            b.last_w = op
            b.readers = []
        return op

    def op(self, eng, method, **kw):
        reads, writes = [], []
        args = {}
        for k, v in kw.items():
            if isinstance(v, V):
                if k in ("out", "accum_out"):
                    writes.extend(v.bufs)
                else:
                    reads.extend(v.bufs)
                args[k] = v.ap
            else:
                args[k] = v

        def fn(e, method=method, args=args):
            return getattr(e, method)(**args)

        return self._add(eng, fn, reads, writes)

    def memset(self, eng, view, val):
        ap = view.ap
        return self._add(eng, lambda e: e.memset(ap, val), [], list(view.bufs))

    def dma(self, out, in_, q="sp", **kw):
        args = dict(out=out.ap, in_=in_.ap, **kw)

        def fn(e, args=args):
            return e.dma_start(**args)

        op = self._add(q, fn, list(in_.bufs), list(out.bufs), is_dma=True)
        k = self.ndma[q]
        self.ndma[q] += 1
        op.dma_sem = (q, k % self.NDMA)
        op.dma_thr = 16 * (k // self.NDMA + 1)
        if k >= self.NDMA:
            op.pre_wait = (op.dma_sem, 16 * (k // self.NDMA))
        return op

    def pe(self, method, **kw):
        return self.op("pe", method, **kw)

    def act(self, method, **kw):
        return self.op("act", method, **kw)

    def dve(self, method, **kw):
        return self.op("dve", method, **kw)

    def pool(self, method, **kw):
        return self.op("pool", method, **kw)

    def emit(self):
        nc = self.nc
        for e in self.ops:
            for op in self.ops[e]:
                for k, vv in op.waits.items():
                    self.ops[k][vv].signal = True
        for e in CENG:
            for op in reversed(self.ops[e]):
                if not op.is_dma:
                    op.signal = True
                    break
        cnt = {}
        for e in CENG:
            c = 0
            arr = []
            for op in self.ops[e]:
                if op.signal and not op.is_dma:
                    c += 1
                arr.append(c)
            cnt[e] = arr
        stack = contextlib.ExitStack()
        with stack:
            sems = {e: stack.enter_context(nc.semaphore("s_" + e)) for e in CENG}
            dsems = {}
            for q in ("sp", "pool", "act"):
                if self.ndma[q]:
                    for i in range(min(self.NDMA, self.ndma[q])):
                        dsems[(q, i)] = stack.enter_context(nc.semaphore("d_%s%d" % (q, i)))
            block = stack.enter_context(nc.Block())

            def run(e, name):
                for op in self.ops[name]:
                    if op.pre_wait is not None:
                        e.wait_ge(dsems[op.pre_wait[0]], op.pre_wait[1])
                    for d in op.dwaits:
                        e.wait_ge(dsems[d.dma_sem], d.dma_thr)
                    for k, vv in op.waits.items():
                        e.wait_ge(sems[k], cnt[k][vv])
                    ins = op.fn(e)
                    if op.is_dma:
                        ins.then_inc(dsems[op.dma_sem], 16)
                    elif op.signal:
                        ins.then_inc(sems[name], 1)
                if name == "sp":
                    last = {}
                    for q in self.ops:
                        for op in self.ops[q]:
                            if op.is_dma:
                                last[op.dma_sem] = op.dma_thr
                    for s, thr in last.items():
                        e.wait_ge(dsems[s], thr)
                    for k in CENG:
                        if cnt[k] and cnt[k][-1] > 0:
                            e.wait_ge(sems[k], cnt[k][-1])

            @block.tensor
            def _(e):
                run(e, "pe")

            @block.scalar
            def _(e):
                run(e, "act")

            @block.vector
            def _(e):
                run(e, "dve")

            @block.gpsimd
            def _(e):
                run(e, "pool")

            @block.sync
            def _(e):
                run(e, "sp")


class KB:
    def __init__(self):
        self.nc = bass.Bass("TRN2", target_bir_lowering=False)
        self.S = Sched(self.nc)
        self.st = contextlib.ExitStack()
        self._rr = 0

    def sb(self, name, shape, dt=F32, nreg=1):
        return Tl(self.st.enter_context(self.nc.sbuf_tensor(name, shape, dt)), nreg)

    def psum(self, name):
        return Tl(self.st.enter_context(self.nc.psum_tensor(name, [128, 512], F32)))

    def psum8(self):
        P, PR = [], []
        for i in range(4):
            t = self.st.enter_context(self.nc.psum_tensor("PP%d" % i, [128, 1024], F32))
            a, b = Tl(t[:, 0:512]), Tl(t[:, 512:1024])
            P += [a, b]
            pr = Tl(t[:, :])
            pr.b = [a.b[0], b.b[0]]
            PR.append(pr)
        return P, PR

    def din(self, name, shape, dt=F32, nreg=1):
        return Tl(self.nc.dram_tensor(name, shape, dt, kind="ExternalInput").ap(), nreg)

    def dout(self, name, shape, dt=F32, nreg=1):
        return Tl(self.nc.dram_tensor(name, shape, dt, kind="ExternalOutput").ap(), nreg)

    def dscr(self, name, shape, dt=F32, nreg=1):
        return Tl(self.nc.dram_tensor(name, shape, dt, kind="Internal").ap(), nreg)

    def ew_eng(self):
        self._rr += 1
        return ("dve", "pool")[self._rr % 2]


class Carver:
    def __init__(self, ap):
        self.t = ap
        self.off = 0

    def get(self, shape, dt=F32):
        n = int(np.prod(shape[1:]))
        nb = n * (2 if dt == F32 else 1)
        v = self.t[:, self.off:self.off + nb]
        self.off += nb
        if dt == F32:
            v = v.bitcast(F32)
        if len(shape) == 3:
            v = v.rearrange("p (a b) -> p a b", a=shape[1])
        return Tl(v)

    def reset(self):
        self.off = 0


def load_cast_weight(kb, w_dram, wbf, nk, ncols, stage, col0=0):
    S = kb.S
    for kc in range(nk):
        stg = stage[kc % len(stage)]
        S.dma(V(stg.t[:, 0:ncols], stg.b), V(w_dram.t[kc * 128:(kc + 1) * 128, col0:col0 + ncols], w_dram.b))
        S.op(kb.ew_eng(), "tensor_copy", out=V(wbf.t[:, kc, :], wbf.b), in_=V(stg.t[:, 0:ncols], stg.b))


def ada_part(kb, cl_d, adaw_d, adab_d, part, P, out_tile, adaw_sb, plus_one=False, tag=""):
    S = kb.S
    scl = kb.sb("scl%d%s" % (part, tag), [128, 8])
    cl = kb.sb("cl%d%s" % (part, tag), [128, 8])
    ab = kb.sb("ab%d%s" % (part, tag), [128, 8])
    S.dma(cl.all(), cl_d.all())
    S.dma(ab.all(), adab_d.all())
    S.act("activation", out=scl.all(), in_=cl.all(), func=AF.Silu)
    for kc in range(8):
        S.dma(V(adaw_sb.t[:, kc, :], adaw_sb.b), V(adaw_d.t[kc * 128:(kc + 1) * 128, part * 1024:(part + 1) * 1024], adaw_d.b))
    for m in range(8):
        for kc in range(8):
            S.pe("matmul", out=V(P.t[:, m:m + 1], P.b), lhsT=V(adaw_sb.t[:, kc, m * 128:(m + 1) * 128], adaw_sb.b),
                 rhs=V(scl.t[:, kc:kc + 1], scl.b), start=(kc == 0), stop=(kc == 7))
    if plus_one:
        S.dve("scalar_tensor_tensor", out=out_tile.all(), in0=V(P.t[:, 0:8], P.b), scalar=1.0, in1=ab.all(), op0=ALU.add, op1=ALU.add)
    else:
        S.dve("tensor_tensor", out=out_tile.all(), in0=V(P.t[:, 0:8], P.b), in1=ab.all(), op=ALU.add)


def build_l3(Th):
    kb = KB()
    S = kb.S
    NT = Th // 512
    x1T = kb.din("x1T", [D, Th])
    ysbT = kb.din("ysbT", [512, Th], BF16)
    yT = kb.din("yT", [512, Th], BF16)
    sg5T = kb.din("sg5T", [512, Th], BF16)
    gluw = kb.din("gluw", [512, 1024])
    glub = kb.din("glub", [128, 8])
    wout = kb.din("wout", [D, D])
    cl_d = kb.din("cl", [128, 8])
    adaw = kb.din("adaw", [D, 3 * D])
    adab = kb.din("adab_g", [128, 8])
    outT = kb.dout("outT", [D, Th], F32, nreg=NT)
    with kb.st:
        P = [kb.psum("P%d" % i) for i in range(8)]
        stage = [kb.sb("stg%d" % i, [128, 1024]) for i in range(2)]
        adaw_sb = kb.sb("adaw_sb", [128, 8, 1024])
        gate = kb.sb("gate", [128, 8])
        glub_sb = kb.sb("glub_sb", [128, 8])
        wout_bf = kb.sb("wout_bf", [128, 8, 1024], BF16)
        gluw_bf = kb.sb("gluw_bf", [128, 4, 1024], BF16)
        x1t = [kb.sb("x1t%d" % i, [128, 8, 512]) for i in range(2)]
        ot = [kb.sb("ot%d" % i, [128, 8, 512]) for i in range(2)]
        ysbt = [kb.sb("ysbt%d" % i, [128, 4, 512], BF16) for i in range(2)]
        yt = [kb.sb("yt%d" % i, [128, 4, 512], BF16) for i in range(2)]
        sgt = [kb.sb("sgt%d" % i, [128, 4, 512], BF16) for i in range(2)]
        ms5 = [kb.sb("ms5%d" % i, [128, 4, 512], BF16) for i in range(2)]
        sgm = [kb.sb("sgm%d" % i, [128, 512]) for i in range(2)]
        t1 = [kb.sb("t1%d" % i, [128, 512]) for i in range(2)]

        S.dma(glub_sb.all(), glub.all())
        ada_part(kb, cl_d, adaw, adab, 2, P[0], gate, adaw_sb)
        load_cast_weight(kb, gluw, gluw_bf, 4, 1024, stage)
        load_cast_weight(kb, wout, wout_bf, 8, 1024, stage)

        x1v = x1T.t.rearrange("(c p) t -> p c t", p=128)
        ysbv = ysbT.t.rearrange("(c p) t -> p c t", p=128)
        yv = yT.t.rearrange("(c p) t -> p c t", p=128)
        sgv = sg5T.t.rearrange("(c p) t -> p c t", p=128)
        outv = outT.t.rearrange("(c p) t -> p c t", p=128)
        for tt in range(NT):
            bi = tt % 2
            sl = slice(tt * 512, (tt + 1) * 512)
            S.dma(x1t[bi].all(), V(x1v[:, :, sl], x1T.b))
            S.dma(ysbt[bi].all(), V(ysbv[:, :, sl], ysbT.b))
            S.dma(yt[bi].all(), V(yv[:, :, sl], yT.b))
            S.dma(sgt[bi].all(), V(sgv[:, :, sl], sg5T.b))
            for i in range(4):
                pv = P[i % 2]
                pg = P[2 + i % 2]
                for e in range(4):
                    S.pe("matmul", out=pv.all(), lhsT=V(gluw_bf.t[:, e, i * 128:(i + 1) * 128], gluw_bf.b),
                         rhs=V(yt[bi].t[:, e, :], yt[bi].b), start=(e == 0), stop=(e == 3))
                for e in range(4):
                    S.pe("matmul", out=pg.all(), lhsT=V(gluw_bf.t[:, e, (4 + i) * 128:(5 + i) * 128], gluw_bf.b),
                         rhs=V(yt[bi].t[:, e, :], yt[bi].b), start=(e == 0), stop=(e == 3))
                S.act("activation", out=sgm[i % 2].all(), in_=pg.all(), func=AF.Sigmoid,
                      bias=V(glub_sb.t[:, 4 + i:5 + i], glub_sb.b))
                S.dve("scalar_tensor_tensor", out=t1[i % 2].all(), in0=pv.all(), scalar=V(glub_sb.t[:, i:i + 1], glub_sb.b),
                      in1=sgm[i % 2].all(), op0=ALU.add, op1=ALU.mult)
                S.pool("tensor_tensor", out=V(ms5[bi].t[:, i, :], ms5[bi].b), in0=t1[i % 2].all(),
                       in1=V(sgt[bi].t[:, i, :], sgt[bi].b), op=ALU.mult)
            for d in range(8):
                po = P[4 + d % 4]
                for e in range(8):
                    rhs = V(ysbt[bi].t[:, e, :], ysbt[bi].b) if e < 4 else V(ms5[bi].t[:, e - 4, :], ms5[bi].b)
                    S.pe("matmul", out=po.all(), lhsT=V(wout_bf.t[:, e, d * 128:(d + 1) * 128], wout_bf.b), rhs=rhs,
                         start=(e == 0), stop=(e == 7))
                S.dve("scalar_tensor_tensor", out=V(ot[bi].t[:, d, :], ot[bi].b), in0=po.all(),
                      scalar=V(gate.t[:, d:d + 1], gate.b), in1=V(x1t[bi].t[:, d, :], x1t[bi].b), op0=ALU.mult, op1=ALU.add)
            S.dma(V(outv[:, :, sl], outT.b[tt]), ot[bi].all(), q="pool")
        S.emit()
    return kb.nc


def lay128(v):
    return np.ascontiguousarray(np.asarray(v, np.float32).reshape(-1, 128).T)


AX = mybir.AxisListType


def fence(S):
    snap = {}
    for e in CENG:
        for op in reversed(S.ops[e]):
            if not op.is_dma:
                snap[e] = op.idx
                break
    dl = {}
    for q in S.ops:
        for op in S.ops[q]:
            if op.is_dma:
                dl[op.dma_sem] = op
    S.pending = {e: (dict(snap), list(dl.values())) for e in CENG + ("sp",)}


_orig_add = Sched._add


def _add_with_fence(self, eng, fn, reads, writes, is_dma=False):
    op = _orig_add(self, eng, fn, reads, writes, is_dma)
    pend = getattr(self, "pending", None)
    if pend and eng in pend:
        snap, dl = pend.pop(eng)
        for se, idx in snap.items():
            if se == eng:
                continue
            if self.known[eng].get(se, -1) >= idx:
                continue
            if op.waits.get(se, -1) < idx:
                op.waits[se] = idx
            self.known[eng][se] = max(self.known[eng].get(se, -1), idx)
        for d in dl:
            if d is op or id(d) in self.kdma[eng]:
                continue
            self.kdma[eng].add(id(d))
            op.dwaits.append(d)
    return op


Sched._add = _add_with_fence


def norm_tile(kb, C, xt, ht, sq, rstd, Pss, gs, sh, tmpf):
    S = kb.S
    S.act("activation", out=sq.all(), in_=xt.all(), func=AF.Square)
    for c in range(8):
        S.pe("matmul", out=Pss.all(), lhsT=C["ones_bf"].all(), rhs=V(sq.t[:, c, :], sq.b), start=(c == 0), stop=(c == 7))
    S.act("activation", out=rstd.all(), in_=Pss.all(), func=AF.Ln, scale=1.0 / D, bias=V(C["eps"].t[:, 0:1], C["eps"].b))
    S.act("activation", out=rstd.all(), in_=rstd.all(), func=AF.Exp, scale=-0.5)
    for c in range(8):
        tf = tmpf[c % 2]
        S.dve("scalar_tensor_tensor", out=tf.all(), in0=V(xt.t[:, c, :], xt.b), scalar=V(gs.t[:, c:c + 1], gs.b),
              in1=rstd.all(), op0=ALU.mult, op1=ALU.mult)
        S.act("activation", out=V(ht.t[:, c, :], ht.b), in_=tf.all(), func=AF.Identity, bias=V(sh.t[:, c:c + 1], sh.b))


def make_consts(kb, T):
    S = kb.S
    C = {}
    C["ones_bf"] = kb.sb("ones_bf", [128, 128], BF16)
    S.memset("pool", C["ones_bf"].all(), 1.0)
    C["eps"] = kb.sb("eps_c", [128, 1])
    S.memset("pool", C["eps"].all(), EPS)
    C["one"] = kb.sb("one_c", [128, 1])
    S.memset("pool", C["one"].all(), 1.0)
    return C


def headnorm(kb, C, Pin, Pss, gain, sqf, rs, outf):
    S = kb.S
    S.act("activation", out=sqf.all(), in_=Pin.all(), func=AF.Square)
    S.pe("matmul", out=Pss.all(), lhsT=C["onesbd"].all(), rhs=sqf.all(), start=True, stop=True)
    S.act("activation", out=rs.all(), in_=Pss.all(), func=AF.Ln, scale=1.0 / 64, bias=V(C["eps"].t[:, 0:1], C["eps"].b))
    S.act("activation", out=rs.all(), in_=rs.all(), func=AF.Exp, scale=-0.5)
    S.dve("scalar_tensor_tensor", out=outf.all(), in0=Pin.all(), scalar=V(gain.t[:, 0:1], gain.b), in1=rs.all(),
          op0=ALU.mult, op1=ALU.mult)


def proj_pass(kb, hT, T, wbf, chunks, P, hts, consumer, vchunk=None, vconsumer=None):
    S = kb.S
    NT = T // 512
    hv = hT.t.rearrange("c p t -> p c t")
    for tt in range(NT):
        ht = hts[tt % 2]
        S.dma(ht.all(), V(hv[:, :, tt * 512:(tt + 1) * 512], hT.b[tt]))
        for n, (slot, tag) in enumerate(chunks):
            pp = P[n % len(P)] if not isinstance(P, dict) else P[tag]
            for kc in range(8):
                S.pe("matmul", out=pp.all(), lhsT=V(wbf.t[:, slot, kc, :], wbf.b), rhs=V(ht.t[:, kc, :], ht.b),
                     start=(kc == 0), stop=(kc == 7))
            consumer(tag, tt, pp)
        if vchunk is not None:
            pp = P["v"]
            for s in range(4):
                for kc in range(8):
                    S.pe("matmul", out=V(pp.t[:, s * 128:(s + 1) * 128], pp.b), lhsT=V(ht.t[:, kc, s * 128:(s + 1) * 128], ht.b),
                         rhs=V(wbf.t[:, vchunk, kc, :], wbf.b), start=(kc == 0), stop=(kc == 7))
            vconsumer(tt, pp)


def load_w_slots(kb, win, wbf, slots, stage):
    S = kb.S
    for slot, col0 in slots:
        for kc in range(8):
            stg = stage[kc % 2]
            S.dma(V(stg.t[:, 0:128], stg.b), V(win.t[kc * 128:(kc + 1) * 128, col0:col0 + 128], win.b))
            S.op(kb.ew_eng(), "tensor_copy", out=V(wbf.t[:, slot, kc, :], wbf.b), in_=V(stg.t[:, 0:128], stg.b))


def build_l1(T, stop=99):
    kb = KB()
    S = kb.S
    NT = T // 512
    NQ = T // 128
    CS = 8192 + 32
    xT = kb.din("xT", [D, T])
    cl_d = kb.din("cl", [128, 8])
    adaw = kb.din("adaw", [D, 3 * D])
    adab_sh = kb.din("adab_sh", [128, 8])
    adab_sc = kb.din("adab_sc", [128, 8])
    normg = kb.din("normg", [128, 8])
    win = kb.din("win", [D, 1536])
    convw = kb.din("convw", [128, 2, 4])
    convb = kb.din("convb", [128, 2])
    rgw = kb.din("rgw", [128, 2, 128])
    rgb = kb.din("rgb", [128, 2])
    igw = kb.din("igw", [128, 2, 128])
    igb = kb.din("igb", [128, 2])
    lam = kb.din("lam", [128, 2])
    qg = kb.din("qg", [128, 1])
    kg = kb.din("kg", [128, 1])
    onesbd_d = kb.din("onesbd", [128, 128])
    khot_d = kb.din("khot", [32, T], BF16)
    pm_d = kb.din("pm128", [128, NQ, 32], BF16)
    own_d = kb.din("own128", [128, NQ, 32], BF16)
    cm_d = kb.din("cm", [128, 896], BF16)
    identb_d = kb.din("identb", [128, 128], BF16)
    identf_d = kb.din("identf", [128, 128])
    mixT = kb.dout("mixT", [512, T], BF16)
    hT = kb.dscr("hT", [8, 128, T], BF16, nreg=NT)
    with kb.st:
        P = [kb.psum("P%d" % i) for i in range(8)]
        C = make_consts(kb, T)
        arena = kb.sb("arena", [128, 7 * CS + 4096], BF16)

        def cbf(i, nreg=1):
            return Tl(arena.t[:, i * CS:i * CS + T], nreg)

        def cf32(i, nreg=1):
            return Tl(arena.t[:, i * CS:(i + 2) * CS].bitcast(F32), nreg)

        stage = [kb.sb("stg%d" % i, [128, 128]) for i in range(2)]
        wbf = kb.sb("wbf", [128, 4, 8, 128], BF16)
        hts = [kb.sb("hts%d" % i, [128, 8, 512], BF16) for i in range(2)]
        gs = kb.sb("gs", [128, 8])
        sh = kb.sb("shf", [128, 8])
        sc1 = kb.sb("sc1", [128, 8])
        ng = kb.sb("ng", [128, 8])
        tf = [kb.sb("tf%d" % i, [128, 512]) for i in range(4)]
        rstd = kb.sb("rstd", [128, 512])
        small = {}
        for nm, src, shp in (("convw", convw, [128, 2, 4]), ("convb", convb, [128, 2]), ("rgb", rgb, [128, 2]),
                             ("igb", igb, [128, 2]), ("lam", lam, [128, 2]), ("qg", qg, [128, 1]), ("kg", kg, [128, 1]),
                             ("onesbd", onesbd_d, [128, 128]), ("identf", identf_d, [128, 128])):
            small[nm] = kb.sb("c_" + nm, shp)
            S.dma(small[nm].all(), src.all())
        C["onesbd"] = small["onesbd"]
        cmt = kb.sb("cmt", [128, 896], BF16)
        S.dma(cmt.all(), cm_d.all())
        identb = kb.sb("identb_s", [128, 128], BF16)
        S.dma(identb.all(), identb_d.all())
        gwf = kb.sb("gwf", [128, 2, 2, 128])
        S.dma(V(gwf.t[:, 0, :, :], gwf.b), rgw.all())
        S.dma(V(gwf.t[:, 1, :, :], gwf.b), igw.all())
        gwb = kb.sb("gwb", [128, 2, 2, 128], BF16)
        S.dve("tensor_copy", out=gwb.all(), in_=gwf.all())
        spl = kb.sb("spl", [128, 2])
        nsp8 = kb.sb("nsp8", [128, 2])
        nsp16 = kb.sb("nsp16", [128, 2])
        S.act("activation", out=spl.all(), in_=small["lam"].all(), func=AF.Exp, scale=-1.0)
        S.act("activation", out=spl.all(), in_=spl.all(), func=AF.Ln, bias=V(C["one"].t[:, 0:1], C["one"].b))
        S.dve("tensor_scalar", out=nsp8.all(), in0=spl.all(), scalar1=-8.0, scalar2=None, op0=ALU.mult)
        S.dve("tensor_scalar", out=nsp16.all(), in0=spl.all(), scalar1=-16.0, scalar2=None, op0=ALU.mult)

        adaw_sb = Tl(arena.t[:, 0:2 * CS].bitcast(F32)[:, 0:8192].rearrange("p (k n) -> p k n", k=8))
        ada_part(kb, cl_d, adaw, adab_sc, 1, P[0], sc1, adaw_sb, plus_one=True, tag="a")
        ada_part(kb, cl_d, adaw, adab_sh, 0, P[1], sh, adaw_sb, tag="b")
        S.dma(ng.all(), normg.all())
        S.dve("tensor_tensor", out=gs.all(), in0=ng.all(), in1=sc1.all(), op=ALU.mult)
        xts = [Tl(arena.t[:, (2 + 2 * i) * CS:(4 + 2 * i) * CS].bitcast(F32)[:, 0:4096].rearrange("p (c t) -> p c t", c=8)) for i in range(2)]
        sq = Tl(arena.t[:, 6 * CS:6 * CS + 4096].rearrange("p (c t) -> p c t", c=8))
        xv = xT.t.rearrange("(c p) t -> p c t", p=128)
        hv = hT.t.rearrange("c p t -> p c t")
        for tt in range(NT):
            xt = xts[tt % 2]
            S.dma(xt.all(), V(xv[:, :, tt * 512:(tt + 1) * 512], xT.b))
            ht = hts[tt % 2]
            norm_tile(kb, C, xt, ht, sq, rstd, P[2 + tt % 2], gs, sh, tf)
            S.dma(V(hv[:, :, tt * 512:(tt + 1) * 512], hT.b[tt]), ht.all(), q="pool")
        fence(S)
        if stop <= 1:
            S.emit()
            return kb.nc

        xcb = [kb.sb("xcb%d" % i, [128, 512], BF16) for i in range(2)]
        for ci in range(2):
            load_w_slots(kb, win, wbf, [(0, ci * 128), (1, 256 + ci * 128)], stage)
            xl = cf32(0)
            xc = cf32(2)
            sg = cbf(4)
            ym = cbf(5)
            S.memset("pool", V(xl.t[:, 0:3], xl.b), 0.0)

            def cons(tag, tt, pp, xl=xl, sg=sg):
                sl = slice(tt * 512, (tt + 1) * 512)
                if tag == "x":
                    S.act("activation", out=V(xl.t[:, 3 + tt * 512:3 + (tt + 1) * 512], xl.b), in_=pp.all(), func=AF.Copy)
                else:
                    S.act("activation", out=V(sg.t[:, sl], sg.b), in_=pp.all(), func=AF.Silu)

            proj_pass(kb, hT, T, wbf, [(0, "x"), (1, "g")], P[0:4], hts, cons)
            cw = small["convw"]
            S.dve("tensor_scalar", out=V(xc.t[:, 0:T], xc.b), in0=V(xl.t[:, 0:T], xl.b), scalar1=V(cw.t[:, ci, 0:1], cw.b),
                  scalar2=V(small["convb"].t[:, ci:ci + 1], small["convb"].b), op0=ALU.mult, op1=ALU.add)
            for j in range(1, 4):
                S.dve("scalar_tensor_tensor", out=V(xc.t[:, 0:T], xc.b), in0=V(xl.t[:, j:j + T], xl.b),
                      scalar=V(cw.t[:, ci, j:j + 1], cw.b), in1=V(xc.t[:, 0:T], xc.b), op0=ALU.mult, op1=ALU.add)
            a = xl
            for tt in range(NT):
                sl = slice(tt * 512, (tt + 1) * 512)
                xb = xcb[tt % 2]
                S.pool("tensor_copy", out=xb.all(), in_=V(xc.t[:, sl], xc.b))
                pr, pi = P[(2 * tt) % 8], P[(2 * tt + 1) % 8]
                S.pe("matmul", out=pr.all(), lhsT=V(gwb.t[:, 0, ci, :], gwb.b), rhs=xb.all(), start=True, stop=True)
                S.pe("matmul", out=pi.all(), lhsT=V(gwb.t[:, 1, ci, :], gwb.b), rhs=xb.all(), start=True, stop=True)
                r, ig, a2, m = tf
                S.act("activation", out=r.all(), in_=pr.all(), func=AF.Sigmoid, bias=V(small["rgb"].t[:, ci:ci + 1], small["rgb"].b))
                S.act("activation", out=ig.all(), in_=pi.all(), func=AF.Sigmoid, bias=V(small["igb"].t[:, ci:ci + 1], small["igb"].b))
                S.act("activation", out=V(a.t[:, sl], a.b), in_=r.all(), func=AF.Exp, scale=V(nsp8.t[:, ci:ci + 1], nsp8.b))
                S.act("activation", out=a2.all(), in_=r.all(), func=AF.Exp, scale=V(nsp16.t[:, ci:ci + 1], nsp16.b))
                S.act("activation", out=a2.all(), in_=a2.all(), func=AF.Ln, scale=-1.0, bias=V(C["one"].t[:, 0:1], C["one"].b))
                S.act("activation", out=m.all(), in_=a2.all(), func=AF.Exp, scale=0.5)
                S.dve("tensor_tensor", out=m.all(), in0=m.all(), in1=ig.all(), op=ALU.mult)
                S.dve("tensor_tensor", out=V(xc.t[:, sl], xc.b), in0=V(xc.t[:, sl], xc.b), in1=m.all(), op=ALU.mult)
            S.dve("tensor_tensor_scan", out=V(xc.t[:, 0:T], xc.b), data0=V(a.t[:, 0:T], a.b), data1=V(xc.t[:, 0:T], xc.b),
                  initial=0.0, op0=ALU.mult, op1=ALU.add)
            S.dve("tensor_tensor", out=ym.all(), in0=V(xc.t[:, 0:T], xc.b), in1=sg.all(), op=ALU.mult)
            S.dma(V(mixT.t[ci * 128:(ci + 1) * 128, :], mixT.b), ym.all(), q="pool")
            fence(S)

        if stop <= 2:
            S.emit()
            return kb.nc
        pm = kb.sb("pm_s", [128, NQ, 32], BF16)
        own = kb.sb("own_s", [128, NQ, 32], BF16)
        S.dma(pm.all(), pm_d.all())
        S.dma(own.all(), own_d.all())
        kms = kb.sb("kms", [128, 2, 32])
        gm = kb.sb("gm", [128, 8, 32])
        top8 = kb.sb("top8", [128, 8, 8])
        thr8 = kb.sb("thr8", [128, 8])
        selB = kb.sb("selB", [128, 8, 32])
        rden = kb.sb("rden", [128, 512])
        ytmp = kb.sb("ytmp", [128, 512])
        ones64 = V(C["ones_bf"].t[:, 0:64], C["ones_bf"].b)
        At_l1 = [kb.sb("At%d" % i, [128, 512], BF16) for i in range(4)]
        for hp in range(2):
            load_w_slots(kb, win, wbf, [(0, 512 + hp * 128), (1, 768 + hp * 128), (2, 1024 + hp * 128), (3, 1280 + hp * 128)], stage)
            Qp = [Tl(arena.t[0:96, (0 + h) * CS:(0 + h) * CS + T]) for h in range(2)]
            Kp = [Tl(arena.t[0:96, (2 + h) * CS:(2 + h) * CS + T]) for h in range(2)]
            o4 = 4 * CS
            Vt = Tl(arena.t[:, o4:o4 + NQ * 192].rearrange("p (n c) -> p n c", c=192))
            sga = Tl(arena.t[:, o4 + 64 * 192:o4 + 64 * 192 + T])
            ym = Tl(arena.t[:, o4 + 64 * 192 + 8192:o4 + 64 * 192 + 8192 + T])
            S.memset("pool", V(Vt.t[:, :, 64:128], Vt.b), 1.0)
            for h in range(2):
                S.dma(V(Kp[h].t[64:96, :], Kp[h].b), khot_d.all())
            S.memset("pool", kms.all(), 0.0)
            PP = {"k": P[0], "q": P[1], "g": P[2], "v": P[3]}
            Pss, Pgate, PnT = P[4], P[5], [P[6], P[7]]

            def cons(tag, tt, pp, Qp=Qp, Kp=Kp, sga=sga):
                sl = slice(tt * 512, (tt + 1) * 512)
                if tag == "g":
                    S.act("activation", out=V(sga.t[:, sl], sga.b), in_=pp.all(), func=AF.Silu)
                    return
                sqf, rs, hf = tf[0], tf[1], tf[2]
                headnorm(kb, C, pp, Pss, small["kg" if tag == "k" else "qg"], sqf, rs, hf)
                if tag == "k":
                    for h in range(2):
                        S.op(("pool", "dve")[h], "tensor_copy", out=V(Kp[h].t[0:64, sl], Kp[h].b), in_=V(hf.t[h * 64:(h + 1) * 64, :], hf.b))
                    for h in range(2):
                        S.dve("tensor_reduce", out=V(kms.t[h * 64:(h + 1) * 64, h, 2 * tt:2 * tt + 2], kms.b),
                              in_=V(hf.t[h * 64:(h + 1) * 64, :].rearrange("p (a b) -> p a b", a=2), hf.b), axis=AX.X, op=ALU.add)
                    return
                import os
                SK = os.environ.get('SKIP', '')
                for h in range(2):
                    S.act("activation", out=V(Qp[h].t[0:64, sl], Qp[h].b), in_=V(hf.t[h * 64:(h + 1) * 64, :], hf.b), func=AF.Copy, scale=0.125)
                if 'g' in SK:
                    return
                for h in range(2):
                    for s in range(4):
                        S.pe("matmul", out=V(Pgate.t[:, (h * 4 + s) * 32:(h * 4 + s + 1) * 32], Pgate.b),
                             lhsT=V(hf.t[:, s * 128:(s + 1) * 128], hf.b), rhs=V(kms.t[:, h, :], kms.b),
                             start=True, stop=True)
                for h in range(2):
                    S.dve("tensor_tensor", out=V(gm.t[:, h * 4:(h + 1) * 4, :], gm.b),
                          in0=V(Pgate.t[:, h * 128:(h + 1) * 128].rearrange("p (s j) -> p s j", s=4), Pgate.b),
                          in1=V(pm.t[:, 4 * tt:4 * tt + 4, :], pm.b), op=ALU.add)
                if 'm' in SK:
                    return
                for hs in range(8):
                    S.dve("max", out=V(top8.t[:, hs, :], top8.b), in_=V(gm.t[:, hs, :], gm.b))
                S.dve("tensor_scalar", out=thr8.all(), in0=V(top8.t[:, :, 2], top8.b), scalar1=-5e8, scalar2=None, op0=ALU.max)
                for hs in range(8):
                    S.dve("tensor_scalar", out=V(selB.t[:, hs, :], selB.b), in0=V(gm.t[:, hs, :], gm.b),
                          scalar1=V(thr8.t[:, hs:hs + 1], thr8.b), scalar2=BIG, op0=ALU.is_ge, op1=ALU.mult)
                for h in range(2):
                    S.dve("tensor_tensor", out=V(selB.t[:, h * 4:(h + 1) * 4, :], selB.b), in0=V(selB.t[:, h * 4:(h + 1) * 4, :], selB.b),
                          in1=V(own.t[:, 4 * tt:4 * tt + 4, :], own.b), op=ALU.max)
                if 't' in SK:
                    return
                for h in range(2):
                    for s in range(4):
                        S.pe("transpose", out=V(PnT[h].t[0:32, s * 128:(s + 1) * 128], PnT[h].b), in_=V(selB.t[:, h * 4 + s, :], selB.b),
                             identity=small["identf"].all())
                    S.act("activation", out=V(Qp[h].t[64:96, sl], Qp[h].b), in_=V(PnT[h].t[0:32, :], PnT[h].b), func=AF.Identity,
                          bias=V(C["nbig"].t[0:32, 0:1], C["nbig"].b))

            def vcons(tt, pp, Vt=Vt):
                pv3 = pp.t[:].rearrange("p (n c) -> p n c", c=128)
                S.dve("tensor_copy", out=V(Vt.t[:, 4 * tt:4 * tt + 4, 0:64], Vt.b), in_=V(pv3[:, :, 0:64], pp.b))
                S.dve("tensor_copy", out=V(Vt.t[:, 4 * tt:4 * tt + 4, 128:192], Vt.b), in_=V(pv3[:, :, 64:128], pp.b))

            if "nbig" not in C:
                C["nbig"] = kb.sb("nbig", [128, 1])
                S.memset("pool", C["nbig"].all(), -BIG)
            proj_pass(kb, hT, T, wbf, [(1, "k"), (0, "q"), (3, "g")], PP, hts, cons, vchunk=2, vconsumer=vcons)
            if stop <= 3:
                S.emit()
                return kb.nc
            At = xcb + [kb.sb("At%d_%d" % (hp, i), [128, 512], BF16) for i in range(1)] if False else At_l1
            blocks = [(h, qt, kt) for h in range(2) for qt in range(NT) for kt in range(4 * qt + 4)]

            def qk_m(bi):
                h, qt, kt = blocks[bi]
                qsl = slice(qt * 512, (qt + 1) * 512)
                Z = P[bi % 4]
                diag = kt >= 4 * qt
                S.pe("matmul", out=Z.all(), lhsT=V(Kp[h].t[0:96, kt * 128:(kt + 1) * 128], Kp[h].b), rhs=V(Qp[h].t[0:96, qsl], Qp[h].b),
                     start=True, stop=not diag)
                if diag:
                    r = kt - 4 * qt
                    S.pe("matmul", out=Z.all(), lhsT=identb.all(), rhs=V(cmt.t[:, 384 - 128 * r:896 - 128 * r], cmt.b), start=False, stop=True)

            def rest_m(bi):
                h, qt, kt = blocks[bi]
                qsl = slice(qt * 512, (qt + 1) * 512)
                nk = 4 * qt + 4
                par = (h * NT + qt) % 4
                Pn = P[4 + par]
                Z = P[bi % 4]
                A = At[bi % 4]
                S.act("activation", out=A.all(), in_=Z.all(), func=AF.Exp)
                S.pe("matmul", out=Pn.all(), lhsT=V(Vt.t[:, kt, h * 64:h * 64 + 128], Vt.b), rhs=A.all(),
                     start=(kt == 0), stop=(kt == nk - 1))
                if kt == nk - 1:
                    nr = slice(h * 64, (h + 1) * 64)
                    dr = slice((1 - h) * 64, (2 - h) * 64)
                    S.dve("reciprocal", out=V(rden.t[nr, :], rden.b), in_=V(Pn.t[dr, :], Pn.b))
                    S.dve("tensor_tensor", out=V(ytmp.t[nr, :], ytmp.b), in0=V(Pn.t[nr, :], Pn.b), in1=V(rden.t[nr, :], rden.b), op=ALU.mult)
                    S.pool("tensor_tensor", out=V(ym.t[h * 64:(h + 1) * 64, qsl], ym.b), in0=V(ytmp.t[h * 64:(h + 1) * 64, :], ytmp.b),
                           in1=V(sga.t[h * 64:(h + 1) * 64, qsl], sga.b), op=ALU.mult)

            LOOK = 2
            for bi in range(min(LOOK, len(blocks))):
                qk_m(bi)
            for bi in range(len(blocks)):
                if bi + LOOK < len(blocks):
                    qk_m(bi + LOOK)
                rest_m(bi)
            S.dma(V(mixT.t[256 + hp * 128:256 + (hp + 1) * 128, :], mixT.b), ym.all(), q="pool")
            fence(S)
        S.emit()
    return kb.nc


def consts_l1(T):
    NQ = T // 128
    onesbd = np.zeros((128, 128), np.float32)
    onesbd[:64, :64] = 1.0
    onesbd[64:, 64:] = 1.0
    khot = np.zeros((32, T), np.float32)
    for jb in range(min(32, T // 256)):
        khot[jb, jb * 256:(jb + 1) * 256] = 1.0
    pm = np.zeros((128, NQ, 32), np.float32)
    own = np.zeros((128, NQ, 32), np.float32)
    for qi in range(NQ):
        qb = qi // 2
        pm[:, qi, qb:] = -1e9
        own[:, qi, qb] = BIG
    p = np.arange(128)[:, None]
    c = np.arange(896)[None, :]
    cm = np.where(p > c - 384, -BIG, 0.0).astype(np.float32)
    cms = np.where(p >= c - 384, -BIG, 0.0).astype(np.float32)
    return dict(onesbd=onesbd, khot=khot.astype(NPBF), pm128=pm.astype(NPBF), own128=own.astype(NPBF), cm=cm.astype(NPBF), cms=cms.astype(NPBF),
                identb=np.eye(128, dtype=np.float32).astype(NPBF), identf=np.eye(128, dtype=np.float32))


def bdiag2(w, g0):
    o = np.zeros((128, 128), np.float32)
    o[:64, :64] = w[g0]
    o[64:, 64:] = w[g0 + 1]
    return o


def prep_l1(inp, b, j, T, cst):
    f = np.float32
    wi = inp["ev_w_in"][0]
    cols = np.concatenate([np.arange(k * 512 + j * 256, k * 512 + (j + 1) * 256) for k in range(6)])
    lw = inp["ev_conv_w"][0][:, j * 256:(j + 1) * 256]
    m = {
        "xT": np.ascontiguousarray(inp["x"][b].T), "cl": lay128(inp["c"][b]), "adaw": inp["ev_ada_w"][0],
        "adab_sh": lay128(inp["ev_ada_b"][0][0:1024]), "adab_sc": lay128(inp["ev_ada_b"][0][1024:2048]),
        "normg": lay128(inp["ev_norm"][0]), "win": np.ascontiguousarray(wi[:, cols]),
        "convw": np.ascontiguousarray(lw.reshape(4, 2, 128).transpose(2, 1, 0)),
        "convb": np.ascontiguousarray(inp["ev_conv_b"][0][j * 256:(j + 1) * 256].reshape(2, 128).T),
        "rgw": np.stack([bdiag2(inp["ev_rgate_w"][0], j * 4 + 2 * ci) for ci in range(2)], axis=1),
        "igw": np.stack([bdiag2(inp["ev_igate_w"][0], j * 4 + 2 * ci) for ci in range(2)], axis=1),
        "rgb": np.ascontiguousarray(inp["ev_rgate_b"][0].reshape(-1)[j * 256:(j + 1) * 256].reshape(2, 128).T),
        "igb": np.ascontiguousarray(inp["ev_igate_b"][0].reshape(-1)[j * 256:(j + 1) * 256].reshape(2, 128).T),
        "lam": np.ascontiguousarray(inp["ev_lru_lambda"][0][j * 256:(j + 1) * 256].reshape(2, 128).T),
        "qg": np.tile(inp["ev_q_norm"][0], 2).reshape(128, 1), "kg": np.tile(inp["ev_k_norm"][0], 2).reshape(128, 1),
        "onesbd": cst["onesbd"], "khot": cst["khot"], "pm128": cst["pm128"], "own128": cst["own128"], "cm": cst["cm"],
        "identb": cst["identb"], "identf": cst["identf"],
    }
    return {k: np.ascontiguousarray(v) if v.dtype == NPBF else np.ascontiguousarray(v, dtype=f) for k, v in m.items()}


def build_l2(T, stop=99):
    kb = KB()
    S = kb.S
    NT = T // 512
    Tn = T // 2
    NTn = Tn // 512
    CS = 8192 + 32
    xT = kb.din("xT", [D, T])
    mix0T = kb.din("mix0T", [D, T], BF16)
    wout0 = kb.din("wout0", [D, D])
    cl_d = kb.din("cl", [128, 8])
    adaw0 = kb.din("adaw0", [D, 3 * D])
    adab_g0 = kb.din("adab_g0", [128, 8])
    adaw1 = kb.din("adaw1", [D, 3 * D])
    adab_sh = kb.din("adab_sh", [128, 8])
    adab_sc = kb.din("adab_sc", [128, 8])
    normg = kb.din("normg", [128, 8])
    win = kb.din("win", [D, 1536])
    qg = kb.din("qg", [128, 1])
    kg = kb.din("kg", [128, 1])
    onesbd_d = kb.din("onesbd", [128, 128])
    cms_d = kb.din("cms", [128, 896], BF16)
    identb_d = kb.din("identb", [128, 128], BF16)
    negtri_d = kb.din("negtri", [128, 128], BF16)
    lre_d = kb.din("lre", [128, 8])
    lim_d = kb.din("lim", [128, 8])
    lstep_d = kb.din("lstep", [128, 8])
    bre_d = kb.din("bre_l", [128, 8, 128])
    bim_d = kb.din("bim_l", [128, 8, 128])
    cre_d = kb.din("cre_p", [128, 8, 128])
    cim_d = kb.din("cim_p", [128, 8, 128])
    dvec_d = kb.din("dvec", [128, 2])
    x1T = kb.dout("x1T", [D, T], F32)
    ysbT = kb.dout("ysbT", [256, T], BF16)
    yT = kb.dout("yT", [256, T], BF16)
    sg5T = kb.dout("sg5T", [256, T], BF16)
    hT = kb.dscr("hT", [8, 128, T], BF16, nreg=NT)
    with kb.st:
        P, PR = kb.psum8()
        C = make_consts(kb, T)
        arena = kb.sb("arena", [128, 7 * CS], BF16)

        def cbf(i, nreg=1):
            return Tl(arena.t[:, i * CS:i * CS + T], nreg)

        stage = [kb.sb("stg%d" % i, [128, 128]) for i in range(2)]
        arena2 = kb.sb("arena2", [128, 21 * 1024], BF16)
        car = Carver(arena2.t)
        stage_big = [car.get([128, 1024]) for i in range(2)]
        wbf = kb.sb("wbf", [128, 4, 8, 128], BF16)
        hts = [kb.sb("hts%d" % i, [128, 8, 512], BF16) for i in range(2)]
        gs = kb.sb("gs", [128, 8])
        sh = kb.sb("shf", [128, 8])
        sc1 = kb.sb("sc1", [128, 8])
        ng = kb.sb("ng", [128, 8])
        gate0 = kb.sb("gate0", [128, 8])
        tf = [kb.sb("tf%d" % i, [128, 512]) for i in range(4)]
        rstd = kb.sb("rstd", [128, 512])
        small = {}
        for nm, src, shp in (("qg", qg, [128, 1]), ("kg", kg, [128, 1]), ("onesbd", onesbd_d, [128, 128]),
                             ("lre", lre_d, [128, 8]), ("lim", lim_d, [128, 8]), ("lstep", lstep_d, [128, 8]), ("dvec", dvec_d, [128, 2])):
            small[nm] = kb.sb("c_" + nm, shp)
            S.dma(small[nm].all(), src.all())
        C["onesbd"] = small["onesbd"]
        cmt = kb.sb("cmt", [128, 896], BF16)
        S.dma(cmt.all(), cms_d.all())
        identb = kb.sb("identb_s", [128, 128], BF16)
        S.dma(identb.all(), identb_d.all())
        negtri = kb.sb("negtri_s", [128, 128], BF16)
        S.dma(negtri.all(), negtri_d.all())
        negones = kb.sb("negones", [128, 128], BF16)
        S.memset("pool", negones.all(), -1.0)

        adaw_sb = Tl(arena.t[:, 0:2 * CS].bitcast(F32)[:, 0:8192].rearrange("p (k n) -> p k n", k=8))
        ada_part(kb, cl_d, adaw0, adab_g0, 2, P[0], gate0, adaw_sb, tag="g0")
        ada_part(kb, cl_d, adaw1, adab_sc, 1, P[1], sc1, adaw_sb, plus_one=True, tag="a")
        ada_part(kb, cl_d, adaw1, adab_sh, 0, P[2], sh, adaw_sb, tag="b")
        S.dma(ng.all(), normg.all())
        S.dve("tensor_tensor", out=gs.all(), in0=ng.all(), in1=sc1.all(), op=ALU.mult)
        fence(S)
        wout_bf = Tl(arena.t[:, 0:8192].rearrange("p (k n) -> p k n", k=8))
        load_cast_weight(kb, wout0, wout_bf, 8, 1024, stage_big)
        mts = [Tl(arena.t[:, CS + i * 4096:CS + (i + 1) * 4096].rearrange("p (c t) -> p c t", c=8)) for i in range(2)]
        xts = [Tl(arena.t[:, (2 + 2 * i) * CS:(4 + 2 * i) * CS].bitcast(F32)[:, 0:4096].rearrange("p (c t) -> p c t", c=8)) for i in range(2)]
        sq = Tl(arena.t[:, 6 * CS:6 * CS + 4096].rearrange("p (c t) -> p c t", c=8))
        xv = xT.t.rearrange("(c p) t -> p c t", p=128)
        x1v = x1T.t.rearrange("(c p) t -> p c t", p=128)
        mv = mix0T.t.rearrange("(c p) t -> p c t", p=128)
        hv = hT.t.rearrange("c p t -> p c t")
        for tt in range(NT):
            sl = slice(tt * 512, (tt + 1) * 512)
            xt = xts[tt % 2]
            mt = mts[tt % 2]
            S.dma(xt.all(), V(xv[:, :, sl], xT.b))
            S.dma(mt.all(), V(mv[:, :, sl], mix0T.b))
            for d in range(8):
                po = P[4 + d % 4]
                for e in range(8):
                    S.pe("matmul", out=po.all(), lhsT=V(wout_bf.t[:, e, d * 128:(d + 1) * 128], wout_bf.b), rhs=V(mt.t[:, e, :], mt.b),
                         start=(e == 0), stop=(e == 7))
                S.dve("scalar_tensor_tensor", out=V(xt.t[:, d, :], xt.b), in0=po.all(), scalar=V(gate0.t[:, d:d + 1], gate0.b),
                      in1=V(xt.t[:, d, :], xt.b), op0=ALU.mult, op1=ALU.add)
            S.dma(V(x1v[:, :, sl], x1T.b), xt.all(), q="pool")
            ht = hts[tt % 2]
            norm_tile(kb, C, xt, ht, sq, rstd, P[2 + tt % 2], gs, sh, tf)
            S.dma(V(hv[:, :, sl], hT.b[tt]), ht.all(), q="pool")
        fence(S)
        if stop <= 1:
            S.emit()
            return kb.nc

        car.reset()
        acc = Tl(car.get([128, 512]).t[0:64, :])
        Dt = Tl(car.get([128, 512]).t[0:64, :])
        ytmp = car.get([128, 512])
        ef = [car.get([128, 1024]) for i in range(2)]
        spb = [car.get([128, 1024], BF16) for i in range(4)]
        At = [car.get([128, 1024], BF16) for i in range(2)]
        ones64 = V(C["ones_bf"].t[:, 0:64], C["ones_bf"].b)
        for hp in range(2):
            load_w_slots(kb, win, wbf, [(0, hp * 128), (1, 256 + hp * 128), (2, 512 + hp * 128), (3, 768 + hp * 128)], stage)
            Qh = [Tl(arena.t[0:64, (0 + h) * CS:(0 + h) * CS + T]) for h in range(2)]
            Kh = [Tl(arena.t[0:64, (2 + h) * CS:(2 + h) * CS + T]) for h in range(2)]
            Vt = Tl(arena.t[:, 4 * CS:4 * CS + T].rearrange("p (n c) -> p n c", c=128))
            sga = cbf(5)
            ym = cbf(6)
            PP = {"k": P[0], "q": P[1], "g": P[2], "v": P[3]}
            Pss = P[4]

            def cons(tag, tt, pp, Qh=Qh, Kh=Kh, sga=sga):
                sl = slice(tt * 512, (tt + 1) * 512)
                if tag == "g":
                    S.act("activation", out=V(sga.t[:, sl], sga.b), in_=pp.all(), func=AF.Silu)
                    return
                sqf, rs, hf = tf[0], tf[1], tf[2]
                headnorm(kb, C, pp, Pss, small["kg" if tag == "k" else "qg"], sqf, rs, hf)
                for h in range(2):
                    if tag == "k":
                        S.op(("pool", "dve")[h], "tensor_copy", out=V(Kh[h].t[0:64, sl], Kh[h].b), in_=V(hf.t[h * 64:(h + 1) * 64, :], hf.b))
                    else:
                        S.act("activation", out=V(Qh[h].t[0:64, sl], Qh[h].b), in_=V(hf.t[h * 64:(h + 1) * 64, :], hf.b), func=AF.Copy, scale=0.125)

            def vcons(tt, pp, Vt=Vt):
                S.dve("tensor_copy", out=V(Vt.t[:, 4 * tt:4 * tt + 4, :], Vt.b), in_=V(pp.t[:].rearrange("p (n c) -> p n c", c=128), pp.b))

            proj_pass(kb, hT, T, wbf, [(1, "k"), (0, "q"), (3, "g")], PP, hts, cons, vchunk=2, vconsumer=vcons)
            groups = [(h, qt, g) for h in range(2) for qt in range(NT) for g in range(qt + 1)]
            pairs = [(gi, ph) for gi in range(len(groups)) for ph in (0, 1)]
            NPR = len(pairs)
            PO, PB = P[6], P[7]

            def pinfo(pi):
                gi, ph = pairs[pi]
                h, qt, g = groups[gi]
                return gi, ph, h, qt, g

            def spp(gi, ph):
                return spb[(gi % 2) * 2 + ph]

            def sp_tile(gi, kl):
                ph, half = (0, 3 - kl) if kl >= 2 else (1, 1 - kl)
                t = spp(gi, ph)
                return V(t.t[:, half * 512:(half + 1) * 512], t.b)

            def s1_pe(pi):
                gi, ph, h, qt, g = pinfo(pi)
                qsl = slice(qt * 512, (qt + 1) * 512)
                for half in range(2):
                    kl = 3 - 2 * ph - half
                    kt = 4 * g + kl
                    Z = P[2 * (pi % 3) + half]
                    S.pe("matmul", out=Z.all(), lhsT=V(Kh[h].t[0:64, kt * 128:(kt + 1) * 128], Kh[h].b), rhs=V(Qh[h].t[0:64, qsl], Qh[h].b),
                         start=True, stop=(g != qt))
                    if g == qt:
                        r = kt - 4 * qt
                        S.pe("matmul", out=Z.all(), lhsT=identb.all(), rhs=V(cmt.t[:, 384 - 128 * r:896 - 128 * r], cmt.b), start=False, stop=True)

            def s1_act(pi):
                gi, ph, h, qt, g = pinfo(pi)
                e = ef[pi % 2]
                S.act("activation", out=e.all(), in_=PR[pi % 3].all(), func=AF.Exp)
                S.act("activation", out=spp(gi, ph).all(), in_=e.all(), func=AF.Ln, bias=V(C["one"].t[:, 0:1], C["one"].b))

            def s2_pe(pi):
                gi, ph, h, qt, g = pinfo(pi)
                for half in range(2):
                    kl = 3 - 2 * ph - half
                    Z = P[2 * (pi % 3) + half]
                    S.pe("matmul", out=Z.all(), lhsT=negtri.all(), rhs=sp_tile(gi, kl), start=False, stop=(kl == 3), skip_group_check=True)
                    for k2 in range(kl + 1, 4):
                        S.pe("matmul", out=Z.all(), lhsT=negones.all(), rhs=sp_tile(gi, k2), start=False, stop=(k2 == 3), skip_group_check=True)

            def s2_act(pi):
                S.act("activation", out=At[pi % 2].all(), in_=PR[pi % 3].all(), func=AF.Exp)

            def s3_pe(pi):
                gi, ph, h, qt, g = pinfo(pi)
                qsl = slice(qt * 512, (qt + 1) * 512)
                A = At[pi % 2]
                for half in range(2):
                    kl = 3 - 2 * ph - half
                    kt = 4 * g + kl
                    S.pe("matmul", out=V(PO.t[0:64, :], PO.b), lhsT=V(Vt.t[:, kt, h * 64:(h + 1) * 64], Vt.b), rhs=V(A.t[:, half * 512:(half + 1) * 512], A.b),
                         start=(kl == 3), stop=(kl == 0))
                    if g > 0:
                        S.pe("matmul", out=V(PB.t[0:64, :], PB.b), lhsT=ones64, rhs=sp_tile(gi, kl), start=(kl == 3), stop=(kl == 0))
                if ph != 1:
                    return
                if g == 0:
                    S.dve("tensor_copy", out=acc.all(), in_=V(PO.t[0:64, :], PO.b))
                else:
                    S.act("activation", out=Dt.all(), in_=V(PB.t[0:64, :], PB.b), func=AF.Exp, scale=-1.0)
                    S.dve("tensor_tensor", out=acc.all(), in0=acc.all(), in1=Dt.all(), op=ALU.mult)
                    S.dve("tensor_tensor", out=acc.all(), in0=acc.all(), in1=V(PO.t[0:64, :], PO.b), op=ALU.add)
                if g == qt:
                    S.dve("tensor_copy", out=V(ytmp.t[h * 64:(h + 1) * 64, :], ytmp.b), in_=acc.all())
                    S.pool("tensor_tensor", out=V(ym.t[h * 64:(h + 1) * 64, qsl], ym.b), in0=V(ytmp.t[h * 64:(h + 1) * 64, :], ytmp.b),
                           in1=V(sga.t[h * 64:(h + 1) * 64, qsl], sga.b), op=ALU.mult)

            s1_pe(0)
            s1_pe(1)
            s1_act(0)
            for i in range(NPR + 1):
                if i + 2 < NPR:
                    s1_pe(i + 2)
                if i < NPR:
                    s2_pe(i)
                if i + 1 < NPR:
                    s1_act(i + 1)
                if i < NPR:
                    s2_act(i)
                if i >= 1:
                    s3_pe(i - 1)
            S.dma(V(ysbT.t[hp * 128:(hp + 1) * 128, :], ysbT.b), ym.all(), q="pool")
            fence(S)
        if stop <= 2:
            S.emit()
            return kb.nc
        car.reset()
        build_s5(kb, C, P, arena, CS, T, win, wbf, stage, hts, hT, small, bre_d, bim_d, cre_d, cim_d, yT, sg5T, tf, car)
        S.emit()
    return kb.nc


def build_s5(kb, C, P, arena, CS, T, win, wbf, stage, hts, hT, small, bre_d, bim_d, cre_d, cim_d, yT, sg5T, tf, car):
    S = kb.S
    NT = T // 512
    Tn = T // 2
    NTn = Tn // 512
    import math
    load_w_slots(kb, win, wbf, [(0, 1024), (1, 1152), (2, 1280), (3, 1408)], stage)
    ubf = [Tl(arena.t[:, i * CS:i * CS + T]) for i in range(2)]
    sgt = [car.get([128, 512], BF16) for i in range(2)]
    cnt = [0]

    def cons(tag, tt, pp):
        sl = slice(tt * 512, (tt + 1) * 512)
        if tag[0] == "u":
            ci = int(tag[1])
            S.act("activation", out=V(ubf[ci].t[:, sl], ubf[ci].b), in_=pp.all(), func=AF.Copy)
        else:
            ci = int(tag[1])
            st = sgt[cnt[0] % 2]
            cnt[0] += 1
            S.act("activation", out=st.all(), in_=pp.all(), func=AF.Silu)
            S.dma(V(sg5T.t[ci * 128:(ci + 1) * 128, sl], sg5T.b), st.all(), q="pool")

    proj_pass(kb, hT, T, wbf, [(0, "u0"), (1, "u1"), (2, "g0"), (3, "g1")], P[0:4], hts, cons)

    def sm(name, shape=(128, 8)):
        return kb.sb("s5_" + name, list(shape))

    lre, lim, lstep = small["lre"], small["lim"], small["lstep"]
    step, lrs, th, rho = sm("step"), sm("lrs"), sm("th"), sm("rho")
    cc, ss, t1, t2 = sm("cc"), sm("ss"), sm("t1"), sm("t2")
    hpi = sm("hpi", (128, 1))
    S.memset("pool", hpi.all(), math.pi / 2)
    S.act("activation", out=step.all(), in_=lstep.all(), func=AF.Exp)
    S.dve("tensor_tensor", out=lrs.all(), in0=lre.all(), in1=step.all(), op=ALU.mult)
    S.dve("tensor_tensor", out=th.all(), in0=lim.all(), in1=step.all(), op=ALU.mult)
    S.act("activation", out=rho.all(), in_=lrs.all(), func=AF.Exp)
    S.act("activation", out=ss.all(), in_=th.all(), func=AF.Sin, scale=1.0 / 32)
    S.act("activation", out=cc.all(), in_=th.all(), func=AF.Sin, scale=1.0 / 32, bias=V(hpi.t[:, 0:1], hpi.b))

    def dbl(c, s, ta, tb):
        S.dve("tensor_tensor", out=ta.all(), in0=c.all(), in1=c.all(), op=ALU.mult)
        S.dve("tensor_tensor", out=tb.all(), in0=s.all(), in1=s.all(), op=ALU.mult)
        S.dve("scalar_tensor_tensor", out=s.all(), in0=s.all(), scalar=2.0, in1=c.all(), op0=ALU.mult, op1=ALU.mult)
        S.dve("tensor_tensor", out=c.all(), in0=ta.all(), in1=tb.all(), op=ALU.subtract)

    for _ in range(5):
        dbl(cc, ss, t1, t2)
    abr, abi, den, fre, fim = sm("abr"), sm("abi"), sm("den"), sm("fre"), sm("fim")
    S.dve("tensor_tensor", out=abr.all(), in0=rho.all(), in1=cc.all(), op=ALU.mult)
    S.dve("tensor_scalar", out=abr.all(), in0=abr.all(), scalar1=-1.0, scalar2=None, op0=ALU.add)
    S.dve("tensor_tensor", out=abi.all(), in0=rho.all(), in1=ss.all(), op=ALU.mult)
    S.dve("tensor_tensor", out=t1.all(), in0=lre.all(), in1=lre.all(), op=ALU.mult)
    S.dve("tensor_tensor", out=t2.all(), in0=lim.all(), in1=lim.all(), op=ALU.mult)
    S.dve("tensor_tensor", out=den.all(), in0=t1.all(), in1=t2.all(), op=ALU.add)
    S.dve("reciprocal", out=den.all(), in_=den.all())
    S.dve("tensor_tensor", out=t1.all(), in0=abr.all(), in1=lre.all(), op=ALU.mult)
    S.dve("tensor_tensor", out=t2.all(), in0=abi.all(), in1=lim.all(), op=ALU.mult)
    S.dve("tensor_tensor", out=fre.all(), in0=t1.all(), in1=t2.all(), op=ALU.add)
    S.dve("tensor_tensor", out=fre.all(), in0=fre.all(), in1=den.all(), op=ALU.mult)
    S.dve("tensor_tensor", out=t1.all(), in0=abi.all(), in1=lre.all(), op=ALU.mult)
    S.dve("tensor_tensor", out=t2.all(), in0=abr.all(), in1=lim.all(), op=ALU.mult)
    S.dve("tensor_tensor", out=fim.all(), in0=t1.all(), in1=t2.all(), op=ALU.subtract)
    S.dve("tensor_tensor", out=fim.all(), in0=fim.all(), in1=den.all(), op=ALU.mult)
    nfim = sm("nfim")
    S.dve("tensor_scalar", out=nfim.all(), in0=fim.all(), scalar1=-1.0, scalar2=None, op0=ALU.mult)
    nfre = sm("nfre")
    S.dve("tensor_scalar", out=nfre.all(), in0=fre.all(), scalar1=-1.0, scalar2=None, op0=ALU.mult)
    bst = car.get([128, 8, 128])
    Bre = car.get([128, 8, 128], BF16)
    Bim = car.get([128, 8, 128], BF16)
    S.dma(bst.all(), bre_d.all())
    S.dve("tensor_copy", out=Bre.all(), in_=bst.all())
    S.dma(bst.all(), bim_d.all())
    S.dve("tensor_copy", out=Bim.all(), in_=bst.all())
    cst = car.get([128, 8, 128])
    C1 = car.get([128, 8, 128], BF16)
    C2 = car.get([128, 8, 128], BF16)
    ctm = car.get([128, 128])
    S.dma(bst.all(), cre_d.all())
    S.dma(cst.all(), cim_d.all())
    for sc in range(8):
        S.dve("tensor_scalar", out=ctm.all(), in0=V(cst.t[:, sc, :], cst.b), scalar1=V(nfim.t[:, sc:sc + 1], nfim.b), scalar2=None, op0=ALU.mult)
        S.dve("scalar_tensor_tensor", out=V(C1.t[:, sc, :], C1.b), in0=V(bst.t[:, sc, :], bst.b), scalar=V(fre.t[:, sc:sc + 1], fre.b),
              in1=ctm.all(), op0=ALU.mult, op1=ALU.add)
        S.dve("tensor_scalar", out=ctm.all(), in0=V(cst.t[:, sc, :], cst.b), scalar1=V(nfre.t[:, sc:sc + 1], nfre.b), scalar2=None, op0=ALU.mult)
        S.dve("scalar_tensor_tensor", out=V(C2.t[:, sc, :], C2.b), in0=V(bst.t[:, sc, :], bst.b), scalar=V(nfim.t[:, sc:sc + 1], nfim.b),
              in1=ctm.all(), op0=ALU.mult, op1=ALU.add)

    def f32v(i0, n):
        return arena.t[:, i0 * CS:(i0 + 1) * CS].bitcast(F32)[:, 0:n]

    Tq = max(512, T // 4)
    NSEG = T // Tq
    NTq = Tq // 512
    c2 = arena.t[:, 2 * CS:3 * CS].bitcast(F32)
    c3 = arena.t[:, 3 * CS:4 * CS].bitcast(F32)
    c4 = arena.t[:, 4 * CS:5 * CS].bitcast(F32)
    c5 = arena.t[:, 5 * CS:6 * CS].bitcast(F32)
    tabC = Tl(c2[:, 0:Tq])
    tabS = Tl(c2[:, 2048:2048 + Tq])
    wre = [Tl(c3[:, 0:Tq]), Tl(c4[:, 0:Tq])]
    wim = [Tl(c3[:, 2048:2048 + Tq]), Tl(c4[:, 2048:2048 + Tq])]
    tmpA = Tl(c5[:, 0:max(Tq // 2, 512)])
    tmpB = Tl(c5[:, 2048:2048 + max(Tq // 2, 512)])
    En = [kb.sb("s5_En%d" % i, [128, 2]) for i in range(2)]
    e1, e2 = sm("e1", (128, 1)), sm("e2", (128, 1))
    ini = [sm("ini0", (128, 2)), sm("ini1", (128, 2))]
    rt = [car.get([128, 512]) for i in range(4)]
    ro = [car.get([128, 512]) for i in range(4)]
    xr = [car.get([128, 512], BF16) for i in range(2)]
    xi = [car.get([128, 512], BF16) for i in range(2)]
    yt = [car.get([128, 512], BF16) for i in range(2)]
    cnt = {"y": 0, "p": 0}
    for sc in range(8):
        uc, r0 = sc // 4, 32 * (sc % 4)
        S.memset("pool", V(tabC.t[:, 0:1], tabC.b), 1.0)
        S.memset("pool", V(tabS.t[:, 0:1], tabS.b), 0.0)
        cur = En[0]
        S.dve("tensor_copy", out=V(cur.t[:, 0:1], cur.b), in_=V(cc.t[:, sc:sc + 1], cc.b))
        S.dve("tensor_copy", out=V(cur.t[:, 1:2], cur.b), in_=V(ss.t[:, sc:sc + 1], ss.b))
        n = 1
        k = 0
        while n < Tq:
            cn = V(cur.t[:, 0:1], cur.b)
            sn = V(cur.t[:, 1:2], cur.b)
            S.dve("tensor_scalar", out=V(tmpA.t[:, 0:n], tmpA.b), in0=V(tabS.t[:, 0:n], tabS.b), scalar1=sn, scalar2=None, op0=ALU.mult)
            S.dve("scalar_tensor_tensor", out=V(tabC.t[:, n:2 * n], tabC.b), in0=V(tabC.t[:, 0:n], tabC.b), scalar=cn,
                  in1=V(tmpA.t[:, 0:n], tmpA.b), op0=ALU.mult, op1=ALU.subtract)
            S.pool("tensor_scalar", out=V(tmpB.t[:, 0:n], tmpB.b), in0=V(tabC.t[:, 0:n], tabC.b), scalar1=sn, scalar2=0.0, op0=ALU.mult, op1=ALU.add)
            S.dve("scalar_tensor_tensor", out=V(tabS.t[:, n:2 * n], tabS.b), in0=V(tabS.t[:, 0:n], tabS.b), scalar=cn,
                  in1=V(tmpB.t[:, 0:n], tmpB.b), op0=ALU.mult, op1=ALU.add)
            nxt = En[(k + 1) % 2]
            S.dve("tensor_tensor", out=e1.all(), in0=cn, in1=cn, op=ALU.mult)
            S.dve("tensor_tensor", out=e2.all(), in0=sn, in1=sn, op=ALU.mult)
            S.dve("tensor_tensor", out=V(nxt.t[:, 0:1], nxt.b), in0=e1.all(), in1=e2.all(), op=ALU.subtract)
            S.dve("scalar_tensor_tensor", out=V(nxt.t[:, 1:2], nxt.b), in0=sn, scalar=2.0, in1=cn, op0=ALU.mult, op1=ALU.mult)
            cur = nxt
            k += 1
            n *= 2
        ETn = cur

        def rotin_tile(seg, i, sc=sc, uc=uc):
            ls = slice(i * 512, (i + 1) * 512)
            gsl = slice(seg * Tq + i * 512, seg * Tq + (i + 1) * 512)
            Pr, Pi = P[(2 * cnt["p"]) % 4], P[(2 * cnt["p"] + 1) % 4]
            cnt["p"] += 1
            W, Wi = wre[seg % 2], wim[seg % 2]
            S.pe("matmul", out=Pr.all(), lhsT=V(Bre.t[:, sc, :], Bre.b), rhs=V(ubf[uc].t[:, gsl], ubf[uc].b), start=True, stop=True)
            S.pe("matmul", out=Pi.all(), lhsT=V(Bim.t[:, sc, :], Bim.b), rhs=V(ubf[uc].t[:, gsl], ubf[uc].b), start=True, stop=True)
            a0, a1, a2, a3 = rt
            S.dve("tensor_tensor", out=a0.all(), in0=Pr.all(), in1=V(tabC.t[:, ls], tabC.b), op=ALU.mult)
            S.dve("tensor_tensor", out=a1.all(), in0=Pi.all(), in1=V(tabS.t[:, ls], tabS.b), op=ALU.mult)
            S.pool("tensor_tensor", out=V(W.t[:, ls], W.b), in0=a0.all(), in1=a1.all(), op=ALU.add)
            S.dve("tensor_tensor", out=a2.all(), in0=Pi.all(), in1=V(tabC.t[:, ls], tabC.b), op=ALU.mult)
            S.dve("tensor_tensor", out=a3.all(), in0=Pr.all(), in1=V(tabS.t[:, ls], tabS.b), op=ALU.mult)
            S.pool("tensor_tensor", out=V(Wi.t[:, ls], Wi.b), in0=a2.all(), in1=a3.all(), op=ALU.subtract)

        def rotout_tile(seg, i, sc=sc, uc=uc, r0=r0):
            ls = slice(i * 512, (i + 1) * 512)
            gsl = slice(seg * Tq + i * 512, seg * Tq + (i + 1) * 512)
            W, Wi = wre[seg % 2], wim[seg % 2]
            a0, a1, a2, a3 = ro
            X, Xi = xr[i % 2], xi[i % 2]
            S.pool("tensor_tensor", out=a0.all(), in0=V(W.t[:, ls], W.b), in1=V(tabC.t[:, ls], tabC.b), op=ALU.mult)
            S.pool("tensor_tensor", out=a1.all(), in0=V(Wi.t[:, ls], Wi.b), in1=V(tabS.t[:, ls], tabS.b), op=ALU.mult)
            S.dve("tensor_tensor", out=X.all(), in0=a0.all(), in1=a1.all(), op=ALU.subtract)
            S.pool("tensor_tensor", out=a2.all(), in0=V(W.t[:, ls], W.b), in1=V(tabS.t[:, ls], tabS.b), op=ALU.mult)
            S.dve("tensor_tensor", out=a3.all(), in0=V(Wi.t[:, ls], Wi.b), in1=V(tabC.t[:, ls], tabC.b), op=ALU.mult)
            S.dve("tensor_tensor", out=Xi.all(), in0=a2.all(), in1=a3.all(), op=ALU.add)
            Py = P[4 + i % 2]
            S.pe("matmul", out=Py.all(), lhsT=V(C1.t[:, sc, :], C1.b), rhs=X.all(), start=True, stop=False)
            S.pe("matmul", out=Py.all(), lhsT=V(C2.t[:, sc, :], C2.b), rhs=Xi.all(), start=False, stop=True)
            Y = yt[cnt["y"] % 2]
            cnt["y"] += 1
            S.dve("scalar_tensor_tensor", out=V(Y.t[r0:r0 + 32, :], Y.b), in0=V(ubf[uc].t[r0:r0 + 32, gsl], ubf[uc].b),
                  scalar=V(small["dvec"].t[r0:r0 + 32, uc:uc + 1], small["dvec"].b), in1=V(Py.t[r0:r0 + 32, :], Py.b), op0=ALU.mult, op1=ALU.add)
            S.dma(V(yT.t[uc * 128 + r0:uc * 128 + r0 + 32, gsl], yT.b), V(Y.t[r0:r0 + 32, :], Y.b), q="pool")

        rho_b = V(rho.t[:, sc:sc + 1].to_broadcast([128, Tq]), rho.b)
        cT, sT = V(ETn.t[:, 0:1], ETn.b), V(ETn.t[:, 1:2], ETn.b)
        for i in range(NTq):
            rotin_tile(0, i)
        for seg in range(NSEG):
            W, Wi = wre[seg % 2], wim[seg % 2]
            if seg == 0:
                i_re, i_im = 0.0, 0.0
            else:
                iv = ini[seg % 2]
                i_re, i_im = V(iv.t[:, 0:1], iv.b), V(iv.t[:, 1:2], iv.b)
            S.dve("tensor_tensor_scan", out=W.all(), data0=rho_b, data1=W.all(), initial=i_re, op0=ALU.mult, op1=ALU.add)
            S.dve("tensor_tensor_scan", out=Wi.all(), data0=rho_b, data1=Wi.all(), initial=i_im, op0=ALU.mult, op1=ALU.add)
            if seg + 1 < NSEG:
                iv = ini[(seg + 1) % 2]
                lr, li = V(W.t[:, Tq - 1:Tq], W.b), V(Wi.t[:, Tq - 1:Tq], Wi.b)
                S.dve("tensor_tensor", out=e1.all(), in0=li, in1=sT, op=ALU.mult)
                S.dve("scalar_tensor_tensor", out=V(iv.t[:, 0:1], iv.b), in0=lr, scalar=cT, in1=e1.all(), op0=ALU.mult, op1=ALU.subtract)
                S.dve("tensor_tensor", out=e2.all(), in0=li, in1=cT, op=ALU.mult)
                S.dve("scalar_tensor_tensor", out=V(iv.t[:, 1:2], iv.b), in0=lr, scalar=sT, in1=e2.all(), op0=ALU.mult, op1=ALU.add)
            for i in range(NTq):
                if seg + 1 < NSEG:
                    rotin_tile(seg + 1, i)
                rotout_tile(seg, i)


def prep_l2(inp, b, j, T, cst, mix0T_b):
    f = np.float32
    wi = inp["od_w_in"][0]
    cols = np.concatenate([np.arange(k * 512 + j * 256, k * 512 + (j + 1) * 256) for k in range(6)])
    G0 = 16 * j
    lre = np.zeros((128, 8), f)
    lim = np.zeros((128, 8), f)
    lstep = np.zeros((128, 8), f)
    bre = np.zeros((128, 8, 128), f)
    bim = np.zeros((128, 8, 128), f)
    cre = np.zeros((128, 8, 128), f)
    cim = np.zeros((128, 8, 128), f)
    for sc in range(8):
        r0 = 32 * (sc % 4)
        for gl in range(2):
            g = G0 + 2 * sc + gl
            lre[gl * 64:(gl + 1) * 64, sc] = inp["od_s5_lambda_re"][0][g]
            lim[gl * 64:(gl + 1) * 64, sc] = inp["od_s5_lambda_im"][0][g]
            lstep[gl * 64:(gl + 1) * 64, sc] = inp["od_s5_log_step"][0][g]
            bre[r0 + gl * 16:r0 + (gl + 1) * 16, sc, gl * 64:(gl + 1) * 64] = inp["od_s5_b_re"][0][g].T
            bim[r0 + gl * 16:r0 + (gl + 1) * 16, sc, gl * 64:(gl + 1) * 64] = inp["od_s5_b_im"][0][g].T
            cre[gl * 64:(gl + 1) * 64, sc, r0 + gl * 16:r0 + (gl + 1) * 16] = inp["od_s5_c_re"][0][g].T
            cim[gl * 64:(gl + 1) * 64, sc, r0 + gl * 16:r0 + (gl + 1) * 16] = inp["od_s5_c_im"][0][g].T
    dflat = inp["od_s5_d"][0].reshape(-1)[j * 256:(j + 1) * 256]
    p = np.arange(128)[:, None]
    s_ = np.arange(128)[None, :]
    negtri = np.where(p >= s_, -1.0, 0.0).astype(f)
    m = {
        "xT": np.ascontiguousarray(inp["x"][b].T), "mix0T": mix0T_b, "wout0": inp["ev_w_out"][0], "cl": lay128(inp["c"][b]),
        "adaw0": inp["ev_ada_w"][0], "adab_g0": lay128(inp["ev_ada_b"][0][2048:]), "adaw1": inp["od_ada_w"][0],
        "adab_sh": lay128(inp["od_ada_b"][0][0:1024]), "adab_sc": lay128(inp["od_ada_b"][0][1024:2048]),
        "normg": lay128(inp["od_norm"][0]), "win": np.ascontiguousarray(wi[:, cols]),
        "qg": np.tile(inp["od_q_norm"][0], 2).reshape(128, 1), "kg": np.tile(inp["od_k_norm"][0], 2).reshape(128, 1),
        "onesbd": cst["onesbd"], "cms": cst["cms"], "identb": cst["identb"], "negtri": negtri.astype(NPBF),
        "lre": lre, "lim": lim, "lstep": lstep, "bre_l": bre, "bim_l": bim, "cre_p": cre, "cim_p": cim,
        "dvec": np.ascontiguousarray(dflat.reshape(2, 128).T),
    }
    return {k: np.ascontiguousarray(v) if v.dtype == NPBF else np.ascontiguousarray(v, dtype=f) for k, v in m.items()}


_CACHE = {}


def _get(name, fn, *a):
    key = (name,) + a
    if key not in _CACHE:
        _CACHE[key] = fn(*a)
    return _CACHE[key]


def kernel(**inp):
    inp = {k: np.asarray(v) for k, v in inp.items()}
    T = inp["x"].shape[1]
    cst = consts_l1(T)
    cores = [(b, j) for b in range(NB) for j in range(2)]
    ids = list(range(8))
    nc1 = build_l1(T)
    r1 = run_bass_kernel_spmd(nc1, [prep_l1(inp, b, j, T, cst) for (b, j) in cores], core_ids=ids).results
    mix0T = []
    for b in range(NB):
        a, c = r1[2 * b]["mixT"], r1[2 * b + 1]["mixT"]
        mix0T.append(np.ascontiguousarray(np.concatenate([a[0:256], c[0:256], a[256:512], c[256:512]], axis=0)))
    del r1
    nc2 = build_l2(T)
    r2 = run_bass_kernel_spmd(nc2, [prep_l2(inp, b, j, T, cst, mix0T[b]) for (b, j) in cores], core_ids=ids).results
    Th = T // 2
    nc3 = build_l3(Th)
    maps = []
    for (b, j) in cores:
        sl = slice(j * Th, (j + 1) * Th)
        cat = lambda nm: np.ascontiguousarray(np.concatenate([r2[2 * b][nm][:, sl], r2[2 * b + 1][nm][:, sl]], axis=0))
        maps.append({
            "x1T": np.ascontiguousarray(r2[2 * b]["x1T"][:, sl]), "ysbT": cat("ysbT"), "yT": cat("yT"), "sg5T": cat("sg5T"),
            "gluw": np.ascontiguousarray(inp["od_glu_w"][0], dtype=np.float32), "glub": lay128(inp["od_glu_b"][0]),
            "wout": np.ascontiguousarray(inp["od_w_out"][0], dtype=np.float32), "cl": lay128(inp["c"][b]),
            "adaw": np.ascontiguousarray(inp["od_ada_w"][0], dtype=np.float32), "adab_g": lay128(inp["od_ada_b"][0][2048:]),
        })
    del r2
    r3 = run_bass_kernel_spmd(nc3, maps, core_ids=ids).results
    out = np.empty((NB, T, D), np.float32)
    for i, (b, j) in enumerate(cores):
        out[b, j * Th:(j + 1) * Th, :] = r3[i]["outT"].T
    return out
```

```python
import contextlib
import numpy as np
import ml_dtypes
import concourse.bass as bass
import concourse.mybir as mybir
from concourse.bass_utils import run_bass_kernel_spmd

F32 = mybir.dt.float32
BF16 = mybir.dt.bfloat16
ALU = mybir.AluOpType
AF = mybir.ActivationFunctionType
NPBF = ml_dtypes.bfloat16

D = 1024
NB = 4
SEQ = 8192
EPS = 1e-6
BIG = 32768.0

CENG = ("pe", "act", "dve", "pool")


class Buf:
    __slots__ = ("last_w", "readers")

    def __init__(self):
        self.last_w = None
        self.readers = []


class V:
    __slots__ = ("ap", "bufs")

    def __init__(self, ap, *bufs):
        self.ap = ap
        bl = []
        for b in bufs:
            if isinstance(b, (list, tuple)):
                bl.extend(b)
            else:
                bl.append(b)
        self.bufs = bl


class Tl:
    def __init__(self, h, nreg=1):
        self.t = h
        self.b = [Buf() for _ in range(nreg)]

    def all(self):
        return V(self.t[:], self.b)


class Op:
    __slots__ = ("eng", "fn", "waits", "dwaits", "signal", "idx", "is_dma", "dma_sem", "dma_thr", "pre_wait")

    def __init__(self, eng, fn):
        self.eng = eng
        self.fn = fn
        self.waits = {}
        self.dwaits = []
        self.signal = False
        self.is_dma = False
        self.dma_sem = None
        self.dma_thr = None
        self.pre_wait = None


class Sched:
    NDMA = 16

    def __init__(self, nc):
        self.nc = nc
        self.ops = {e: [] for e in CENG + ("sp",)}
        self.known = {e: {} for e in CENG + ("sp",)}
        self.kdma = {e: set() for e in CENG + ("sp",)}
        self.ndma = {"sp": 0, "pool": 0, "act": 0}

    def _add(self, eng, fn, reads, writes, is_dma=False):
        op = Op(eng, fn)
        op.is_dma = is_dma
        lst = self.ops[eng]
        op.idx = len(lst)
        deps = []
        for b in reads:
            if b.last_w is not None:
                deps.append((b.last_w, "raw"))
        for b in writes:
            if b.last_w is not None:
                deps.append((b.last_w, "waw"))
            for r in b.readers:
                deps.append((r, "war"))
        for src, kind in deps:
            if src is op:
                continue
            if src.is_dma:
                if id(src) in self.kdma[eng]:
                    continue
                self.kdma[eng].add(id(src))
                op.dwaits.append(src)
                continue
            se = src.eng
            if se == eng and not is_dma:
                if eng == "pe":
                    continue
            if self.known[eng].get(se, -1) >= src.idx:
                continue
            if op.waits.get(se, -1) < src.idx:
                op.waits[se] = src.idx
        for k, vv in op.waits.items():
            self.known[eng][k] = max(self.known[eng].get(k, -1), vv)
        lst.append(op)
        for b in reads:
            b.readers.append(op)
        for b in writes:
            b.last_w = op
            b.readers = []
        return op

    def op(self, eng, method, **kw):
        reads, writes = [], []
        args = {}
        for k, v in kw.items():
            if isinstance(v, V):
                if k in ("out", "accum_out"):
                    writes.extend(v.bufs)
                else:
                    reads.extend(v.bufs)
                args[k] = v.ap
            else:
                args[k] = v

        def fn(e, method=method, args=args):
            return getattr(e, method)(**args)

        return self._add(eng, fn, reads, writes)

    def memset(self, eng, view, val):
        ap = view.ap
        return self._add(eng, lambda e: e.memset(ap, val), [], list(view.bufs))

    def dma(self, out, in_, q="sp", **kw):
        args = dict(out=out.ap, in_=in_.ap, **kw)

        def fn(e, args=args):
            return e.dma_start(**args)

        op = self._add(q, fn, list(in_.bufs), list(out.bufs), is_dma=True)
        k = self.ndma[q]
        self.ndma[q] += 1
        op.dma_sem = (q, k % self.NDMA)
        op.dma_thr = 16 * (k // self.NDMA + 1)
        if k >= self.NDMA:
            op.pre_wait = (op.dma_sem, 16 * (k // self.NDMA))
        return op

    def pe(self, method, **kw):
        return self.op("pe", method, **kw)

    def act(self, method, **kw):
        return self.op("act", method, **kw)

    def dve(self, method, **kw):
        return self.op("dve", method, **kw)

    def pool(self, method, **kw):
        return self.op("pool", method, **kw)

    def emit(self):
        nc = self.nc
        for e in self.ops:
            for op in self.ops[e]:
                for k, vv in op.waits.items():
                    self.ops[k][vv].signal = True
        for e in CENG:
            for op in reversed(self.ops[e]):
                if not op.is_dma:
                    op.signal = True
                    break
        cnt = {}
        for e in CENG:
            c = 0
            arr = []
            for op in self.ops[e]:
                if op.signal and not op.is_dma:
                    c += 1
                arr.append(c)
            cnt[e] = arr
        stack = contextlib.ExitStack()
        with stack:
            sems = {e: stack.enter_context(nc.semaphore("s_" + e)) for e in CENG}
            dsems = {}
            for q in ("sp", "pool", "act"):
                if self.ndma[q]:
                    for i in range(min(self.NDMA, self.ndma[q])):
                        dsems[(q, i)] = stack.enter_context(nc.semaphore("d_%s%d" % (q, i)))
            block = stack.enter_context(nc.Block())

            def run(e, name):
                for op in self.ops[name]:
                    if op.pre_wait is not None:
                        e.wait_ge(dsems[op.pre_wait[0]], op.pre_wait[1])
                    for d in op.dwaits:
                        e.wait_ge(dsems[d.dma_sem], d.dma_thr)
                    for k, vv in op.waits.items():
                        e.wait_ge(sems[k], cnt[k][vv])
                    ins = op.fn(e)
                    if op.is_dma:
                        ins.then_inc(dsems[op.dma_sem], 16)
                    elif op.signal:
                        ins.then_inc(sems[name], 1)
                if name == "sp":
                    last = {}
                    for q in self.ops:
                        for op in self.ops[q]:
                            if op.is_dma:
                                last[op.dma_sem] = op.dma_thr
                    for s, thr in last.items():
                        e.wait_ge(dsems[s], thr)
                    for k in CENG:
                        if cnt[k] and cnt[k][-1] > 0:
                            e.wait_ge(sems[k], cnt[k][-1])

            @block.tensor
            def _(e):
                run(e, "pe")

            @block.scalar
            def _(e):
                run(e, "act")

            @block.vector
            def _(e):
                run(e, "dve")

            @block.gpsimd
            def _(e):
                run(e, "pool")

            @block.sync
            def _(e):
                run(e, "sp")


class KB:
    def __init__(self):
        self.nc = bass.Bass("TRN2", target_bir_lowering=False)
        self.S = Sched(self.nc)
        self.st = contextlib.ExitStack()
        self._rr = 0

    def sb(self, name, shape, dt=F32, nreg=1):
        return Tl(self.st.enter_context(self.nc.sbuf_tensor(name, shape, dt)), nreg)

    def psum(self, name):
        return Tl(self.st.enter_context(self.nc.psum_tensor(name, [128, 512], F32)))

    def psum8(self):
        P, PR = [], []
        for i in range(4):
            t = self.st.enter_context(self.nc.psum_tensor("PP%d" % i, [128, 1024], F32))
            a, b = Tl(t[:, 0:512]), Tl(t[:, 512:1024])
            P += [a, b]
            pr = Tl(t[:, :])
            pr.b = [a.b[0], b.b[0]]
            PR.append(pr)
        return P, PR

    def din(self, name, shape, dt=F32, nreg=1):
        return Tl(self.nc.dram_tensor(name, shape, dt, kind="ExternalInput").ap(), nreg)

    def dout(self, name, shape, dt=F32, nreg=1):
        return Tl(self.nc.dram_tensor(name, shape, dt, kind="ExternalOutput").ap(), nreg)

    def dscr(self, name, shape, dt=F32, nreg=1):
        return Tl(self.nc.dram_tensor(name, shape, dt, kind="Internal").ap(), nreg)

    def ew_eng(self):
        self._rr += 1
        return ("dve", "pool")[self._rr % 2]


class Carver:
    def __init__(self, ap):
        self.t = ap
        self.off = 0

    def get(self, shape, dt=F32):
        n = int(np.prod(shape[1:]))
        nb = n * (2 if dt == F32 else 1)
        v = self.t[:, self.off:self.off + nb]
        self.off += nb
        if dt == F32:
            v = v.bitcast(F32)
        if len(shape) == 3:
            v = v.rearrange("p (a b) -> p a b", a=shape[1])
        return Tl(v)

    def reset(self):
        self.off = 0


def load_cast_weight(kb, w_dram, wbf, nk, ncols, stage, col0=0):
    S = kb.S
    for kc in range(nk):
        stg = stage[kc % len(stage)]
        S.dma(V(stg.t[:, 0:ncols], stg.b), V(w_dram.t[kc * 128:(kc + 1) * 128, col0:col0 + ncols], w_dram.b))
        S.op(kb.ew_eng(), "tensor_copy", out=V(wbf.t[:, kc, :], wbf.b), in_=V(stg.t[:, 0:ncols], stg.b))


def ada_part(kb, cl_d, adaw_d, adab_d, part, P, out_tile, adaw_sb, plus_one=False, tag=""):
    S = kb.S
    scl = kb.sb("scl%d%s" % (part, tag), [128, 8])
    cl = kb.sb("cl%d%s" % (part, tag), [128, 8])
    ab = kb.sb("ab%d%s" % (part, tag), [128, 8])
    S.dma(cl.all(), cl_d.all())
    S.dma(ab.all(), adab_d.all())
    S.act("activation", out=scl.all(), in_=cl.all(), func=AF.Silu)
    for kc in range(8):
        S.dma(V(adaw_sb.t[:, kc, :], adaw_sb.b), V(adaw_d.t[kc * 128:(kc + 1) * 128, part * 1024:(part + 1) * 1024], adaw_d.b))
    for m in range(8):
        for kc in range(8):
            S.pe("matmul", out=V(P.t[:, m:m + 1], P.b), lhsT=V(adaw_sb.t[:, kc, m * 128:(m + 1) * 128], adaw_sb.b),
                 rhs=V(scl.t[:, kc:kc + 1], scl.b), start=(kc == 0), stop=(kc == 7))
    if plus_one:
        S.dve("scalar_tensor_tensor", out=out_tile.all(), in0=V(P.t[:, 0:8], P.b), scalar=1.0, in1=ab.all(), op0=ALU.add, op1=ALU.add)
    else:
        S.dve("tensor_tensor", out=out_tile.all(), in0=V(P.t[:, 0:8], P.b), in1=ab.all(), op=ALU.add)


def build_l3(Th):
    kb = KB()
    S = kb.S
    NT = Th // 512
    x1T = kb.din("x1T", [D, Th])
    ysbT = kb.din("ysbT", [512, Th], BF16)
    yT = kb.din("yT", [512, Th], BF16)
    sg5T = kb.din("sg5T", [512, Th], BF16)
    gluw = kb.din("gluw", [512, 1024])
    glub = kb.din("glub", [128, 8])
    wout = kb.din("wout", [D, D])
    cl_d = kb.din("cl", [128, 8])
    adaw = kb.din("adaw", [D, 3 * D])
    adab = kb.din("adab_g", [128, 8])
    outT = kb.dout("outT", [D, Th], F32, nreg=NT)
    with kb.st:
        P = [kb.psum("P%d" % i) for i in range(8)]
        stage = [kb.sb("stg%d" % i, [128, 1024]) for i in range(2)]
        adaw_sb = kb.sb("adaw_sb", [128, 8, 1024])
        gate = kb.sb("gate", [128, 8])
        glub_sb = kb.sb("glub_sb", [128, 8])
        wout_bf = kb.sb("wout_bf", [128, 8, 1024], BF16)
        gluw_bf = kb.sb("gluw_bf", [128, 4, 1024], BF16)
        x1t = [kb.sb("x1t%d" % i, [128, 8, 512]) for i in range(2)]
        ot = [kb.sb("ot%d" % i, [128, 8, 512]) for i in range(2)]
        ysbt = [kb.sb("ysbt%d" % i, [128, 4, 512], BF16) for i in range(2)]
        yt = [kb.sb("yt%d" % i, [128, 4, 512], BF16) for i in range(2)]
        sgt = [kb.sb("sgt%d" % i, [128, 4, 512], BF16) for i in range(2)]
        ms5 = [kb.sb("ms5%d" % i, [128, 4, 512], BF16) for i in range(2)]
        sgm = [kb.sb("sgm%d" % i, [128, 512]) for i in range(2)]
        t1 = [kb.sb("t1%d" % i, [128, 512]) for i in range(2)]

        S.dma(glub_sb.all(), glub.all())
        ada_part(kb, cl_d, adaw, adab, 2, P[0], gate, adaw_sb)
        load_cast_weight(kb, gluw, gluw_bf, 4, 1024, stage)
        load_cast_weight(kb, wout, wout_bf, 8, 1024, stage)

        x1v = x1T.t.rearrange("(c p) t -> p c t", p=128)
        ysbv = ysbT.t.rearrange("(c p) t -> p c t", p=128)
        yv = yT.t.rearrange("(c p) t -> p c t", p=128)
        sgv = sg5T.t.rearrange("(c p) t -> p c t", p=128)
        outv = outT.t.rearrange("(c p) t -> p c t", p=128)
        for tt in range(NT):
            bi = tt % 2
            sl = slice(tt * 512, (tt + 1) * 512)
            S.dma(x1t[bi].all(), V(x1v[:, :, sl], x1T.b))
            S.dma(ysbt[bi].all(), V(ysbv[:, :, sl], ysbT.b))
            S.dma(yt[bi].all(), V(yv[:, :, sl], yT.b))
            S.dma(sgt[bi].all(), V(sgv[:, :, sl], sg5T.b))
            for i in range(4):
                pv = P[i % 2]
                pg = P[2 + i % 2]
                for e in range(4):
                    S.pe("matmul", out=pv.all(), lhsT=V(gluw_bf.t[:, e, i * 128:(i + 1) * 128], gluw_bf.b),
                         rhs=V(yt[bi].t[:, e, :], yt[bi].b), start=(e == 0), stop=(e == 3))
                for e in range(4):
                    S.pe("matmul", out=pg.all(), lhsT=V(gluw_bf.t[:, e, (4 + i) * 128:(5 + i) * 128], gluw_bf.b),
                         rhs=V(yt[bi].t[:, e, :], yt[bi].b), start=(e == 0), stop=(e == 3))
                S.act("activation", out=sgm[i % 2].all(), in_=pg.all(), func=AF.Sigmoid,
                      bias=V(glub_sb.t[:, 4 + i:5 + i], glub_sb.b))
                S.dve("scalar_tensor_tensor", out=t1[i % 2].all(), in0=pv.all(), scalar=V(glub_sb.t[:, i:i + 1], glub_sb.b),
                      in1=sgm[i % 2].all(), op0=ALU.add, op1=ALU.mult)
                S.pool("tensor_tensor", out=V(ms5[bi].t[:, i, :], ms5[bi].b), in0=t1[i % 2].all(),
                       in1=V(sgt[bi].t[:, i, :], sgt[bi].b), op=ALU.mult)
            for d in range(8):
                po = P[4 + d % 4]
                for e in range(8):
                    rhs = V(ysbt[bi].t[:, e, :], ysbt[bi].b) if e < 4 else V(ms5[bi].t[:, e - 4, :], ms5[bi].b)
                    S.pe("matmul", out=po.all(), lhsT=V(wout_bf.t[:, e, d * 128:(d + 1) * 128], wout_bf.b), rhs=rhs,
                         start=(e == 0), stop=(e == 7))
                S.dve("scalar_tensor_tensor", out=V(ot[bi].t[:, d, :], ot[bi].b), in0=po.all(),
                      scalar=V(gate.t[:, d:d + 1], gate.b), in1=V(x1t[bi].t[:, d, :], x1t[bi].b), op0=ALU.mult, op1=ALU.add)
            S.dma(V(outv[:, :, sl], outT.b[tt]), ot[bi].all(), q="pool")
        S.emit()
    return kb.nc


def lay128(v):
    return np.ascontiguousarray(np.asarray(v, np.float32).reshape(-1, 128).T)


AX = mybir.AxisListType


def fence(S):
    snap = {}
    for e in CENG:
        for op in reversed(S.ops[e]):
            if not op.is_dma:
                snap[e] = op.idx
                break
    dl = {}
    for q in S.ops:
        for op in S.ops[q]:
            if op.is_dma:
                dl[op.dma_sem] = op
    S.pending = {e: (dict(snap), list(dl.values())) for e in CENG + ("sp",)}


_orig_add = Sched._add


def _add_with_fence(self, eng, fn, reads, writes, is_dma=False):
    op = _orig_add(self, eng, fn, reads, writes, is_dma)
    pend = getattr(self, "pending", None)
    if pend and eng in pend:
        snap, dl = pend.pop(eng)
        for se, idx in snap.items():
            if se == eng:
                continue
            if self.known[eng].get(se, -1) >= idx:
                continue
            if op.waits.get(se, -1) < idx:
                op.waits[se] = idx
            self.known[eng][se] = max(self.known[eng].get(se, -1), idx)
        for d in dl:
            if d is op or id(d) in self.kdma[eng]:
                continue
            self.kdma[eng].add(id(d))
            op.dwaits.append(d)
    return op


Sched._add = _add_with_fence


def norm_tile(kb, C, xt, ht, sq, rstd, Pss, gs, sh, tmpf):
    S = kb.S
    S.act("activation", out=sq.all(), in_=xt.all(), func=AF.Square)
    for c in range(8):
        S.pe("matmul", out=Pss.all(), lhsT=C["ones_bf"].all(), rhs=V(sq.t[:, c, :], sq.b), start=(c == 0), stop=(c == 7))
    S.act("activation", out=rstd.all(), in_=Pss.all(), func=AF.Ln, scale=1.0 / D, bias=V(C["eps"].t[:, 0:1], C["eps"].b))
    S.act("activation", out=rstd.all(), in_=rstd.all(), func=AF.Exp, scale=-0.5)
    for c in range(8):
        tf = tmpf[c % 2]
        S.dve("scalar_tensor_tensor", out=tf.all(), in0=V(xt.t[:, c, :], xt.b), scalar=V(gs.t[:, c:c + 1], gs.b),
              in1=rstd.all(), op0=ALU.mult, op1=ALU.mult)
        S.act("activation", out=V(ht.t[:, c, :], ht.b), in_=tf.all(), func=AF.Identity, bias=V(sh.t[:, c:c + 1], sh.b))


def make_consts(kb, T):
    S = kb.S
    C = {}
    C["ones_bf"] = kb.sb("ones_bf", [128, 128], BF16)
    S.memset("pool", C["ones_bf"].all(), 1.0)
    C["eps"] = kb.sb("eps_c", [128, 1])
    S.memset("pool", C["eps"].all(), EPS)
    C["one"] = kb.sb("one_c", [128, 1])
    S.memset("pool", C["one"].all(), 1.0)
    return C


def headnorm(kb, C, Pin, Pss, gain, sqf, rs, outf):
    S = kb.S
    S.act("activation", out=sqf.all(), in_=Pin.all(), func=AF.Square)
    S.pe("matmul", out=Pss.all(), lhsT=C["onesbd"].all(), rhs=sqf.all(), start=True, stop=True)
    S.act("activation", out=rs.all(), in_=Pss.all(), func=AF.Ln, scale=1.0 / 64, bias=V(C["eps"].t[:, 0:1], C["eps"].b))
    S.act("activation", out=rs.all(), in_=rs.all(), func=AF.Exp, scale=-0.5)
    S.dve("scalar_tensor_tensor", out=outf.all(), in0=Pin.all(), scalar=V(gain.t[:, 0:1], gain.b), in1=rs.all(),
          op0=ALU.mult, op1=ALU.mult)


def proj_pass(kb, hT, T, wbf, chunks, P, hts, consumer, vchunk=None, vconsumer=None):
    S = kb.S
    NT = T // 512
    hv = hT.t.rearrange("c p t -> p c t")
    for tt in range(NT):
        ht = hts[tt % 2]
        S.dma(ht.all(), V(hv[:, :, tt * 512:(tt + 1) * 512], hT.b[tt]))
        for n, (slot, tag) in enumerate(chunks):
            pp = P[n % len(P)] if not isinstance(P, dict) else P[tag]
            for kc in range(8):
                S.pe("matmul", out=pp.all(), lhsT=V(wbf.t[:, slot, kc, :], wbf.b), rhs=V(ht.t[:, kc, :], ht.b),
                     start=(kc == 0), stop=(kc == 7))
            consumer(tag, tt, pp)
        if vchunk is not None:
            pp = P["v"]
            for s in range(4):
                for kc in range(8):
                    S.pe("matmul", out=V(pp.t[:, s * 128:(s + 1) * 128], pp.b), lhsT=V(ht.t[:, kc, s * 128:(s + 1) * 128], ht.b),
                         rhs=V(wbf.t[:, vchunk, kc, :], wbf.b), start=(kc == 0), stop=(kc == 7))
            vconsumer(tt, pp)


def load_w_slots(kb, win, wbf, slots, stage):
    S = kb.S
    for slot, col0 in slots:
        for kc in range(8):
            stg = stage[kc % 2]
            S.dma(V(stg.t[:, 0:128], stg.b), V(win.t[kc * 128:(kc + 1) * 128, col0:col0 + 128], win.b))
            S.op(kb.ew_eng(), "tensor_copy", out=V(wbf.t[:, slot, kc, :], wbf.b), in_=V(stg.t[:, 0:128], stg.b))


def build_l1(T, stop=99):
    kb = KB()
    S = kb.S
    NT = T // 512
    NQ = T // 128
    CS = 8192 + 32
    xT = kb.din("xT", [D, T])
    cl_d = kb.din("cl", [128, 8])
    adaw = kb.din("adaw", [D, 3 * D])
    adab_sh = kb.din("adab_sh", [128, 8])
    adab_sc = kb.din("adab_sc", [128, 8])
    normg = kb.din("normg", [128, 8])
    win = kb.din("win", [D, 1536])
    convw = kb.din("convw", [128, 2, 4])
    convb = kb.din("convb", [128, 2])
    rgw = kb.din("rgw", [128, 2, 128])
    rgb = kb.din("rgb", [128, 2])
    igw = kb.din("igw", [128, 2, 128])
    igb = kb.din("igb", [128, 2])
    lam = kb.din("lam", [128, 2])
    qg = kb.din("qg", [128, 1])
    kg = kb.din("kg", [128, 1])
    onesbd_d = kb.din("onesbd", [128, 128])
    khot_d = kb.din("khot", [32, T], BF16)
    pm_d = kb.din("pm128", [128, NQ, 32], BF16)
    own_d = kb.din("own128", [128, NQ, 32], BF16)
    cm_d = kb.din("cm", [128, 896], BF16)
    identb_d = kb.din("identb", [128, 128], BF16)
    identf_d = kb.din("identf", [128, 128])
    mixT = kb.dout("mixT", [512, T], BF16)
    hT = kb.dscr("hT", [8, 128, T], BF16, nreg=NT)
    with kb.st:
        P = [kb.psum("P%d" % i) for i in range(8)]
        C = make_consts(kb, T)
        arena = kb.sb("arena", [128, 7 * CS + 4096], BF16)

        def cbf(i, nreg=1):
            return Tl(arena.t[:, i * CS:i * CS + T], nreg)

        def cf32(i, nreg=1):
            return Tl(arena.t[:, i * CS:(i + 2) * CS].bitcast(F32), nreg)

        stage = [kb.sb("stg%d" % i, [128, 128]) for i in range(2)]
        wbf = kb.sb("wbf", [128, 4, 8, 128], BF16)
        hts = [kb.sb("hts%d" % i, [128, 8, 512], BF16) for i in range(2)]
        gs = kb.sb("gs", [128, 8])
        sh = kb.sb("shf", [128, 8])
        sc1 = kb.sb("sc1", [128, 8])
        ng = kb.sb("ng", [128, 8])
        tf = [kb.sb("tf%d" % i, [128, 512]) for i in range(4)]
        rstd = kb.sb("rstd", [128, 512])
        small = {}
        for nm, src, shp in (("convw", convw, [128, 2, 4]), ("convb", convb, [128, 2]), ("rgb", rgb, [128, 2]),
                             ("igb", igb, [128, 2]), ("lam", lam, [128, 2]), ("qg", qg, [128, 1]), ("kg", kg, [128, 1]),
                             ("onesbd", onesbd_d, [128, 128]), ("identf", identf_d, [128, 128])):
            small[nm] = kb.sb("c_" + nm, shp)
            S.dma(small[nm].all(), src.all())
        C["onesbd"] = small["onesbd"]
        cmt = kb.sb("cmt", [128, 896], BF16)
        S.dma(cmt.all(), cm_d.all())
        identb = kb.sb("identb_s", [128, 128], BF16)
        S.dma(identb.all(), identb_d.all())
        gwf = kb.sb("gwf", [128, 2, 2, 128])
        S.dma(V(gwf.t[:, 0, :, :], gwf.b), rgw.all())
        S.dma(V(gwf.t[:, 1, :, :], gwf.b), igw.all())
        gwb = kb.sb("gwb", [128, 2, 2, 128], BF16)
        S.dve("tensor_copy", out=gwb.all(), in_=gwf.all())
        spl = kb.sb("spl", [128, 2])
        nsp8 = kb.sb("nsp8", [128, 2])
        nsp16 = kb.sb("nsp16", [128, 2])
        S.act("activation", out=spl.all(), in_=small["lam"].all(), func=AF.Exp, scale=-1.0)
        S.act("activation", out=spl.all(), in_=spl.all(), func=AF.Ln, bias=V(C["one"].t[:, 0:1], C["one"].b))
        S.dve("tensor_scalar", out=nsp8.all(), in0=spl.all(), scalar1=-8.0, scalar2=None, op0=ALU.mult)
        S.dve("tensor_scalar", out=nsp16.all(), in0=spl.all(), scalar1=-16.0, scalar2=None, op0=ALU.mult)

        adaw_sb = Tl(arena.t[:, 0:2 * CS].bitcast(F32)[:, 0:8192].rearrange("p (k n) -> p k n", k=8))
        ada_part(kb, cl_d, adaw, adab_sc, 1, P[0], sc1, adaw_sb, plus_one=True, tag="a")
        ada_part(kb, cl_d, adaw, adab_sh, 0, P[1], sh, adaw_sb, tag="b")
        S.dma(ng.all(), normg.all())
        S.dve("tensor_tensor", out=gs.all(), in0=ng.all(), in1=sc1.all(), op=ALU.mult)
        xts = [Tl(arena.t[:, (2 + 2 * i) * CS:(4 + 2 * i) * CS].bitcast(F32)[:, 0:4096].rearrange("p (c t) -> p c t", c=8)) for i in range(2)]
        sq = Tl(arena.t[:, 6 * CS:6 * CS + 4096].rearrange("p (c t) -> p c t", c=8))
        xv = xT.t.rearrange("(c p) t -> p c t", p=128)
        hv = hT.t.rearrange("c p t -> p c t")
        for tt in range(NT):
            xt = xts[tt % 2]
            S.dma(xt.all(), V(xv[:, :, tt * 512:(tt + 1) * 512], xT.b))
            ht = hts[tt % 2]
            norm_tile(kb, C, xt, ht, sq, rstd, P[2 + tt % 2], gs, sh, tf)
            S.dma(V(hv[:, :, tt * 512:(tt + 1) * 512], hT.b[tt]), ht.all(), q="pool")
        fence(S)
        if stop <= 1:
            S.emit()
            return kb.nc

        xcb = [kb.sb("xcb%d" % i, [128, 512], BF16) for i in range(2)]
        for ci in range(2):
            load_w_slots(kb, win, wbf, [(0, ci * 128), (1, 256 + ci * 128)], stage)
            xl = cf32(0)
            xc = cf32(2)
            sg = cbf(4)
            ym = cbf(5)
            S.memset("pool", V(xl.t[:, 0:3], xl.b), 0.0)

            def cons(tag, tt, pp, xl=xl, sg=sg):
                sl = slice(tt * 512, (tt + 1) * 512)
                if tag == "x":
                    S.act("activation", out=V(xl.t[:, 3 + tt * 512:3 + (tt + 1) * 512], xl.b), in_=pp.all(), func=AF.Copy)
                else:
                    S.act("activation", out=V(sg.t[:, sl], sg.b), in_=pp.all(), func=AF.Silu)

            proj_pass(kb, hT, T, wbf, [(0, "x"), (1, "g")], P[0:4], hts, cons)
            cw = small["convw"]
            S.dve("tensor_scalar", out=V(xc.t[:, 0:T], xc.b), in0=V(xl.t[:, 0:T], xl.b), scalar1=V(cw.t[:, ci, 0:1], cw.b),
                  scalar2=V(small["convb"].t[:, ci:ci + 1], small["convb"].b), op0=ALU.mult, op1=ALU.add)
            for j in range(1, 4):
                S.dve("scalar_tensor_tensor", out=V(xc.t[:, 0:T], xc.b), in0=V(xl.t[:, j:j + T], xl.b),
                      scalar=V(cw.t[:, ci, j:j + 1], cw.b), in1=V(xc.t[:, 0:T], xc.b), op0=ALU.mult, op1=ALU.add)
            a = xl
            for tt in range(NT):
                sl = slice(tt * 512, (tt + 1) * 512)
                xb = xcb[tt % 2]
                S.pool("tensor_copy", out=xb.all(), in_=V(xc.t[:, sl], xc.b))
                pr, pi = P[(2 * tt) % 8], P[(2 * tt + 1) % 8]
                S.pe("matmul", out=pr.all(), lhsT=V(gwb.t[:, 0, ci, :], gwb.b), rhs=xb.all(), start=True, stop=True)
                S.pe("matmul", out=pi.all(), lhsT=V(gwb.t[:, 1, ci, :], gwb.b), rhs=xb.all(), start=True, stop=True)
                r, ig, a2, m = tf
                S.act("activation", out=r.all(), in_=pr.all(), func=AF.Sigmoid, bias=V(small["rgb"].t[:, ci:ci + 1], small["rgb"].b))
                S.act("activation", out=ig.all(), in_=pi.all(), func=AF.Sigmoid, bias=V(small["igb"].t[:, ci:ci + 1], small["igb"].b))
                S.act("activation", out=V(a.t[:, sl], a.b), in_=r.all(), func=AF.Exp, scale=V(nsp8.t[:, ci:ci + 1], nsp8.b))
                S.act("activation", out=a2.all(), in_=r.all(), func=AF.Exp, scale=V(nsp16.t[:, ci:ci + 1], nsp16.b))
                S.act("activation", out=a2.all(), in_=a2.all(), func=AF.Ln, scale=-1.0, bias=V(C["one"].t[:, 0:1], C["one"].b))
                S.act("activation", out=m.all(), in_=a2.all(), func=AF.Exp, scale=0.5)
                S.dve("tensor_tensor", out=m.all(), in0=m.all(), in1=ig.all(), op=ALU.mult)
                S.dve("tensor_tensor", out=V(xc.t[:, sl], xc.b), in0=V(xc.t[:, sl], xc.b), in1=m.all(), op=ALU.mult)
            S.dve("tensor_tensor_scan", out=V(xc.t[:, 0:T], xc.b), data0=V(a.t[:, 0:T], a.b), data1=V(xc.t[:, 0:T], xc.b),
                  initial=0.0, op0=ALU.mult, op1=ALU.add)
            S.dve("tensor_tensor", out=ym.all(), in0=V(xc.t[:, 0:T], xc.b), in1=sg.all(), op=ALU.mult)
            S.dma(V(mixT.t[ci * 128:(ci + 1) * 128, :], mixT.b), ym.all(), q="pool")
            fence(S)

        if stop <= 2:
            S.emit()
            return kb.nc
        pm = kb.sb("pm_s", [128, NQ, 32], BF16)
        own = kb.sb("own_s", [128, NQ, 32], BF16)
        S.dma(pm.all(), pm_d.all())
        S.dma(own.all(), own_d.all())
        kms = kb.sb("kms", [128, 2, 32])
        gm = kb.sb("gm", [128, 8, 32])
        top8 = kb.sb("top8", [128, 8, 8])
        thr8 = kb.sb("thr8", [128, 8])
        selB = kb.sb("selB", [128, 8, 32])
        rden = kb.sb("rden", [128, 512])
        ytmp = kb.sb("ytmp", [128, 512])
        ones64 = V(C["ones_bf"].t[:, 0:64], C["ones_bf"].b)
        At_l1 = [kb.sb("At%d" % i, [128, 512], BF16) for i in range(4)]
        for hp in range(2):
            load_w_slots(kb, win, wbf, [(0, 512 + hp * 128), (1, 768 + hp * 128), (2, 1024 + hp * 128), (3, 1280 + hp * 128)], stage)
            Qp = [Tl(arena.t[0:96, (0 + h) * CS:(0 + h) * CS + T]) for h in range(2)]
            Kp = [Tl(arena.t[0:96, (2 + h) * CS:(2 + h) * CS + T]) for h in range(2)]
            o4 = 4 * CS
            Vt = Tl(arena.t[:, o4:o4 + NQ * 192].rearrange("p (n c) -> p n c", c=192))
            sga = Tl(arena.t[:, o4 + 64 * 192:o4 + 64 * 192 + T])
            ym = Tl(arena.t[:, o4 + 64 * 192 + 8192:o4 + 64 * 192 + 8192 + T])
            S.memset("pool", V(Vt.t[:, :, 64:128], Vt.b), 1.0)
            for h in range(2):
                S.dma(V(Kp[h].t[64:96, :], Kp[h].b), khot_d.all())
            S.memset("pool", kms.all(), 0.0)
            PP = {"k": P[0], "q": P[1], "g": P[2], "v": P[3]}
            Pss, Pgate, PnT = P[4], P[5], [P[6], P[7]]

            def cons(tag, tt, pp, Qp=Qp, Kp=Kp, sga=sga):
                sl = slice(tt * 512, (tt + 1) * 512)
                if tag == "g":
                    S.act("activation", out=V(sga.t[:, sl], sga.b), in_=pp.all(), func=AF.Silu)
                    return
                sqf, rs, hf = tf[0], tf[1], tf[2]
                headnorm(kb, C, pp, Pss, small["kg" if tag == "k" else "qg"], sqf, rs, hf)
                if tag == "k":
                    for h in range(2):
                        S.op(("pool", "dve")[h], "tensor_copy", out=V(Kp[h].t[0:64, sl], Kp[h].b), in_=V(hf.t[h * 64:(h + 1) * 64, :], hf.b))
                    for h in range(2):
                        S.dve("tensor_reduce", out=V(kms.t[h * 64:(h + 1) * 64, h, 2 * tt:2 * tt + 2], kms.b),
                              in_=V(hf.t[h * 64:(h + 1) * 64, :].rearrange("p (a b) -> p a b", a=2), hf.b), axis=AX.X, op=ALU.add)
                    return
                for h in range(2):
                    S.act("activation", out=V(Qp[h].t[0:64, sl], Qp[h].b), in_=V(hf.t[h * 64:(h + 1) * 64, :], hf.b), func=AF.Copy, scale=0.125)
                for h in range(2):
                    for s in range(4):
                        S.pe("matmul", out=V(Pgate.t[:, (h * 4 + s) * 32:(h * 4 + s + 1) * 32], Pgate.b),
                             lhsT=V(hf.t[:, s * 128:(s + 1) * 128], hf.b), rhs=V(kms.t[:, h, :], kms.b),
                             start=True, stop=True)
                for h in range(2):
                    S.dve("tensor_tensor", out=V(gm.t[:, h * 4:(h + 1) * 4, :], gm.b),
                          in0=V(Pgate.t[:, h * 128:(h + 1) * 128].rearrange("p (s j) -> p s j", s=4), Pgate.b),
                          in1=V(pm.t[:, 4 * tt:4 * tt + 4, :], pm.b), op=ALU.add)
                for hs in range(8):
                    S.dve("max", out=V(top8.t[:, hs, :], top8.b), in_=V(gm.t[:, hs, :], gm.b))
                S.dve("tensor_scalar", out=thr8.all(), in0=V(top8.t[:, :, 2], top8.b), scalar1=-5e8, scalar2=None, op0=ALU.max)
                for hs in range(8):
                    S.dve("tensor_scalar", out=V(selB.t[:, hs, :], selB.b), in0=V(gm.t[:, hs, :], gm.b),
                          scalar1=V(thr8.t[:, hs:hs + 1], thr8.b), scalar2=BIG, op0=ALU.is_ge, op1=ALU.mult)
                for h in range(2):
                    S.dve("tensor_tensor", out=V(selB.t[:, h * 4:(h + 1) * 4, :], selB.b), in0=V(selB.t[:, h * 4:(h + 1) * 4, :], selB.b),
                          in1=V(own.t[:, 4 * tt:4 * tt + 4, :], own.b), op=ALU.max)
                for h in range(2):
                    for s in range(4):
                        S.pe("transpose", out=V(PnT[h].t[0:32, s * 128:(s + 1) * 128], PnT[h].b), in_=V(selB.t[:, h * 4 + s, :], selB.b),
                             identity=small["identf"].all())
                    S.act("activation", out=V(Qp[h].t[64:96, sl], Qp[h].b), in_=V(PnT[h].t[0:32, :], PnT[h].b), func=AF.Identity,
                          bias=V(C["nbig"].t[0:32, 0:1], C["nbig"].b))

            def vcons(tt, pp, Vt=Vt):
                pv3 = pp.t[:].rearrange("p (n c) -> p n c", c=128)
                S.dve("tensor_copy", out=V(Vt.t[:, 4 * tt:4 * tt + 4, 0:64], Vt.b), in_=V(pv3[:, :, 0:64], pp.b))
                S.dve("tensor_copy", out=V(Vt.t[:, 4 * tt:4 * tt + 4, 128:192], Vt.b), in_=V(pv3[:, :, 64:128], pp.b))

            if "nbig" not in C:
                C["nbig"] = kb.sb("nbig", [128, 1])
                S.memset("pool", C["nbig"].all(), -BIG)
            proj_pass(kb, hT, T, wbf, [(1, "k"), (0, "q"), (3, "g")], PP, hts, cons, vchunk=2, vconsumer=vcons)
            if stop <= 3:
                S.emit()
                return kb.nc
            At = xcb + [kb.sb("At%d_%d" % (hp, i), [128, 512], BF16) for i in range(1)] if False else At_l1
            blocks = [(h, qt, kt) for h in range(2) for qt in range(NT) for kt in range(4 * qt + 4)]

            def qk_m(bi):
                h, qt, kt = blocks[bi]
                qsl = slice(qt * 512, (qt + 1) * 512)
                Z = P[bi % 4]
                diag = kt >= 4 * qt
                S.pe("matmul", out=Z.all(), lhsT=V(Kp[h].t[0:96, kt * 128:(kt + 1) * 128], Kp[h].b), rhs=V(Qp[h].t[0:96, qsl], Qp[h].b),
                     start=True, stop=not diag)
                if diag:
                    r = kt - 4 * qt
                    S.pe("matmul", out=Z.all(), lhsT=identb.all(), rhs=V(cmt.t[:, 384 - 128 * r:896 - 128 * r], cmt.b), start=False, stop=True)

            def rest_m(bi):
                h, qt, kt = blocks[bi]
                qsl = slice(qt * 512, (qt + 1) * 512)
                nk = 4 * qt + 4
                par = (h * NT + qt) % 4
                Pn = P[4 + par]
                Z = P[bi % 4]
                A = At[bi % 4]
                S.act("activation", out=A.all(), in_=Z.all(), func=AF.Exp)
                S.pe("matmul", out=Pn.all(), lhsT=V(Vt.t[:, kt, h * 64:h * 64 + 128], Vt.b), rhs=A.all(),
                     start=(kt == 0), stop=(kt == nk - 1))
                if kt == nk - 1:
                    nr = slice(h * 64, (h + 1) * 64)
                    dr = slice((1 - h) * 64, (2 - h) * 64)
                    S.dve("reciprocal", out=V(rden.t[nr, :], rden.b), in_=V(Pn.t[dr, :], Pn.b))
                    S.dve("tensor_tensor", out=V(ytmp.t[nr, :], ytmp.b), in0=V(Pn.t[nr, :], Pn.b), in1=V(rden.t[nr, :], rden.b), op=ALU.mult)
                    S.pool("tensor_tensor", out=V(ym.t[h * 64:(h + 1) * 64, qsl], ym.b), in0=V(ytmp.t[h * 64:(h + 1) * 64, :], ytmp.b),
                           in1=V(sga.t[h * 64:(h + 1) * 64, qsl], sga.b), op=ALU.mult)

            LOOK = 2
            for bi in range(min(LOOK, len(blocks))):
                qk_m(bi)
            for bi in range(len(blocks)):
                if bi + LOOK < len(blocks):
                    qk_m(bi + LOOK)
                rest_m(bi)
            S.dma(V(mixT.t[256 + hp * 128:256 + (hp + 1) * 128, :], mixT.b), ym.all(), q="pool")
            fence(S)
        S.emit()
    return kb.nc


def consts_l1(T):
    NQ = T // 128
    onesbd = np.zeros((128, 128), np.float32)
    onesbd[:64, :64] = 1.0
    onesbd[64:, 64:] = 1.0
    khot = np.zeros((32, T), np.float32)
    for jb in range(min(32, T // 256)):
        khot[jb, jb * 256:(jb + 1) * 256] = 1.0
    pm = np.zeros((128, NQ, 32), np.float32)
    own = np.zeros((128, NQ, 32), np.float32)
    for qi in range(NQ):
        qb = qi // 2
        pm[:, qi, qb:] = -1e9
        own[:, qi, qb] = BIG
    p = np.arange(128)[:, None]
    c = np.arange(896)[None, :]
    cm = np.where(p > c - 384, -BIG, 0.0).astype(np.float32)
    cms = np.where(p >= c - 384, -BIG, 0.0).astype(np.float32)
    return dict(onesbd=onesbd, khot=khot.astype(NPBF), pm128=pm.astype(NPBF), own128=own.astype(NPBF), cm=cm.astype(NPBF), cms=cms.astype(NPBF),
                identb=np.eye(128, dtype=np.float32).astype(NPBF), identf=np.eye(128, dtype=np.float32))


def bdiag2(w, g0):
    o = np.zeros((128, 128), np.float32)
    o[:64, :64] = w[g0]
    o[64:, 64:] = w[g0 + 1]
    return o


def prep_l1(inp, b, j, T, cst):
    f = np.float32
    wi = inp["ev_w_in"][0]
    cols = np.concatenate([np.arange(k * 512 + j * 256, k * 512 + (j + 1) * 256) for k in range(6)])
    lw = inp["ev_conv_w"][0][:, j * 256:(j + 1) * 256]
    m = {
        "xT": np.ascontiguousarray(inp["x"][b].T), "cl": lay128(inp["c"][b]), "adaw": inp["ev_ada_w"][0],
        "adab_sh": lay128(inp["ev_ada_b"][0][0:1024]), "adab_sc": lay128(inp["ev_ada_b"][0][1024:2048]),
        "normg": lay128(inp["ev_norm"][0]), "win": np.ascontiguousarray(wi[:, cols]),
        "convw": np.ascontiguousarray(lw.reshape(4, 2, 128).transpose(2, 1, 0)),
        "convb": np.ascontiguousarray(inp["ev_conv_b"][0][j * 256:(j + 1) * 256].reshape(2, 128).T),
        "rgw": np.stack([bdiag2(inp["ev_rgate_w"][0], j * 4 + 2 * ci) for ci in range(2)], axis=1),
        "igw": np.stack([bdiag2(inp["ev_igate_w"][0], j * 4 + 2 * ci) for ci in range(2)], axis=1),
        "rgb": np.ascontiguousarray(inp["ev_rgate_b"][0].reshape(-1)[j * 256:(j + 1) * 256].reshape(2, 128).T),
        "igb": np.ascontiguousarray(inp["ev_igate_b"][0].reshape(-1)[j * 256:(j + 1) * 256].reshape(2, 128).T),
        "lam": np.ascontiguousarray(inp["ev_lru_lambda"][0][j * 256:(j + 1) * 256].reshape(2, 128).T),
        "qg": np.tile(inp["ev_q_norm"][0], 2).reshape(128, 1), "kg": np.tile(inp["ev_k_norm"][0], 2).reshape(128, 1),
        "onesbd": cst["onesbd"], "khot": cst["khot"], "pm128": cst["pm128"], "own128": cst["own128"], "cm": cst["cm"],
        "identb": cst["identb"], "identf": cst["identf"],
    }
    return {k: np.ascontiguousarray(v) if v.dtype == NPBF else np.ascontiguousarray(v, dtype=f) for k, v in m.items()}


def build_l2(T, stop=99):
    kb = KB()
    S = kb.S
    NT = T // 512
    Tn = T // 2
    NTn = Tn // 512
    CS = 8192 + 32
    xT = kb.din("xT", [D, T])
    mix0T = kb.din("mix0T", [D, T], BF16)
    wout0 = kb.din("wout0", [D, D])
    cl_d = kb.din("cl", [128, 8])
    adaw0 = kb.din("adaw0", [D, 3 * D])
    adab_g0 = kb.din("adab_g0", [128, 8])
    adaw1 = kb.din("adaw1", [D, 3 * D])
    adab_sh = kb.din("adab_sh", [128, 8])
    adab_sc = kb.din("adab_sc", [128, 8])
    normg = kb.din("normg", [128, 8])
    win = kb.din("win", [D, 1536])
    qg = kb.din("qg", [128, 1])
    kg = kb.din("kg", [128, 1])
    onesbd_d = kb.din("onesbd", [128, 128])
    cms_d = kb.din("cms", [128, 896], BF16)
    identb_d = kb.din("identb", [128, 128], BF16)
    negtri_d = kb.din("negtri", [128, 128], BF16)
    lre_d = kb.din("lre", [128, 8])
    lim_d = kb.din("lim", [128, 8])
    lstep_d = kb.din("lstep", [128, 8])
    bre_d = kb.din("bre_l", [128, 8, 128])
    bim_d = kb.din("bim_l", [128, 8, 128])
    cre_d = kb.din("cre_p", [128, 8, 128])
    cim_d = kb.din("cim_p", [128, 8, 128])
    dvec_d = kb.din("dvec", [128, 2])
    x1T = kb.dout("x1T", [D, T], F32)
    ysbT = kb.dout("ysbT", [256, T], BF16)
    yT = kb.dout("yT", [256, T], BF16)
    sg5T = kb.dout("sg5T", [256, T], BF16)
    hT = kb.dscr("hT", [8, 128, T], BF16, nreg=NT)
    with kb.st:
        P, PR = kb.psum8()
        C = make_consts(kb, T)
        arena = kb.sb("arena", [128, 7 * CS], BF16)

        def cbf(i, nreg=1):
            return Tl(arena.t[:, i * CS:i * CS + T], nreg)

        stage = [kb.sb("stg%d" % i, [128, 128]) for i in range(2)]
        arena2 = kb.sb("arena2", [128, 21 * 1024], BF16)
        car = Carver(arena2.t)
        stage_big = [car.get([128, 1024]) for i in range(2)]
        wbf = kb.sb("wbf", [128, 4, 8, 128], BF16)
        hts = [kb.sb("hts%d" % i, [128, 8, 512], BF16) for i in range(2)]
        gs = kb.sb("gs", [128, 8])
        sh = kb.sb("shf", [128, 8])
        sc1 = kb.sb("sc1", [128, 8])
        ng = kb.sb("ng", [128, 8])
        gate0 = kb.sb("gate0", [128, 8])
        tf = [kb.sb("tf%d" % i, [128, 512]) for i in range(4)]
        rstd = kb.sb("rstd", [128, 512])
        small = {}
        for nm, src, shp in (("qg", qg, [128, 1]), ("kg", kg, [128, 1]), ("onesbd", onesbd_d, [128, 128]),
                             ("lre", lre_d, [128, 8]), ("lim", lim_d, [128, 8]), ("lstep", lstep_d, [128, 8]), ("dvec", dvec_d, [128, 2])):
            small[nm] = kb.sb("c_" + nm, shp)
            S.dma(small[nm].all(), src.all())
        C["onesbd"] = small["onesbd"]
        cmt = kb.sb("cmt", [128, 896], BF16)
        S.dma(cmt.all(), cms_d.all())
        identb = kb.sb("identb_s", [128, 128], BF16)
        S.dma(identb.all(), identb_d.all())
        negtri = kb.sb("negtri_s", [128, 128], BF16)
        S.dma(negtri.all(), negtri_d.all())
        negones = kb.sb("negones", [128, 128], BF16)
        S.memset("pool", negones.all(), -1.0)

        adaw_sb = Tl(arena.t[:, 0:2 * CS].bitcast(F32)[:, 0:8192].rearrange("p (k n) -> p k n", k=8))
        ada_part(kb, cl_d, adaw0, adab_g0, 2, P[0], gate0, adaw_sb, tag="g0")
        ada_part(kb, cl_d, adaw1, adab_sc, 1, P[1], sc1, adaw_sb, plus_one=True, tag="a")
        ada_part(kb, cl_d, adaw1, adab_sh, 0, P[2], sh, adaw_sb, tag="b")
        S.dma(ng.all(), normg.all())
        S.dve("tensor_tensor", out=gs.all(), in0=ng.all(), in1=sc1.all(), op=ALU.mult)
        fence(S)
        wout_bf = Tl(arena.t[:, 0:8192].rearrange("p (k n) -> p k n", k=8))
        load_cast_weight(kb, wout0, wout_bf, 8, 1024, stage_big)
        mts = [Tl(arena.t[:, CS + i * 4096:CS + (i + 1) * 4096].rearrange("p (c t) -> p c t", c=8)) for i in range(2)]
        xts = [Tl(arena.t[:, (2 + 2 * i) * CS:(4 + 2 * i) * CS].bitcast(F32)[:, 0:4096].rearrange("p (c t) -> p c t", c=8)) for i in range(2)]
        sq = Tl(arena.t[:, 6 * CS:6 * CS + 4096].rearrange("p (c t) -> p c t", c=8))
        xv = xT.t.rearrange("(c p) t -> p c t", p=128)
        x1v = x1T.t.rearrange("(c p) t -> p c t", p=128)
        mv = mix0T.t.rearrange("(c p) t -> p c t", p=128)
        hv = hT.t.rearrange("c p t -> p c t")
        for tt in range(NT):
            sl = slice(tt * 512, (tt + 1) * 512)
            xt = xts[tt % 2]
            mt = mts[tt % 2]
            S.dma(xt.all(), V(xv[:, :, sl], xT.b))
            S.dma(mt.all(), V(mv[:, :, sl], mix0T.b))
            for d in range(8):
                po = P[4 + d % 4]
                for e in range(8):
                    S.pe("matmul", out=po.all(), lhsT=V(wout_bf.t[:, e, d * 128:(d + 1) * 128], wout_bf.b), rhs=V(mt.t[:, e, :], mt.b),
                         start=(e == 0), stop=(e == 7))
                S.dve("scalar_tensor_tensor", out=V(xt.t[:, d, :], xt.b), in0=po.all(), scalar=V(gate0.t[:, d:d + 1], gate0.b),
                      in1=V(xt.t[:, d, :], xt.b), op0=ALU.mult, op1=ALU.add)
            S.dma(V(x1v[:, :, sl], x1T.b), xt.all(), q="pool")
            ht = hts[tt % 2]
            norm_tile(kb, C, xt, ht, sq, rstd, P[2 + tt % 2], gs, sh, tf)
            S.dma(V(hv[:, :, sl], hT.b[tt]), ht.all(), q="pool")
        fence(S)
        if stop <= 1:
            S.emit()
            return kb.nc

        car.reset()
        acc = Tl(car.get([128, 512]).t[0:64, :])
        Dt = Tl(car.get([128, 512]).t[0:64, :])
        ytmp = car.get([128, 512])
        ef = [car.get([128, 1024]) for i in range(2)]
        spb = [car.get([128, 1024], BF16) for i in range(4)]
        At = [car.get([128, 1024], BF16) for i in range(2)]
        ones64 = V(C["ones_bf"].t[:, 0:64], C["ones_bf"].b)
        for hp in range(2):
            load_w_slots(kb, win, wbf, [(0, hp * 128), (1, 256 + hp * 128), (2, 512 + hp * 128), (3, 768 + hp * 128)], stage)
            Qh = [Tl(arena.t[0:64, (0 + h) * CS:(0 + h) * CS + T]) for h in range(2)]
            Kh = [Tl(arena.t[0:64, (2 + h) * CS:(2 + h) * CS + T]) for h in range(2)]
            Vt = Tl(arena.t[:, 4 * CS:4 * CS + T].rearrange("p (n c) -> p n c", c=128))
            sga = cbf(5)
            ym = cbf(6)
            PP = {"k": P[0], "q": P[1], "g": P[2], "v": P[3]}
            Pss = P[4]

            def cons(tag, tt, pp, Qh=Qh, Kh=Kh, sga=sga):
                sl = slice(tt * 512, (tt + 1) * 512)
                if tag == "g":
                    S.act("activation", out=V(sga.t[:, sl], sga.b), in_=pp.all(), func=AF.Silu)
                    return
                sqf, rs, hf = tf[0], tf[1], tf[2]
                headnorm(kb, C, pp, Pss, small["kg" if tag == "k" else "qg"], sqf, rs, hf)
                for h in range(2):
                    if tag == "k":
                        S.op(("pool", "dve")[h], "tensor_copy", out=V(Kh[h].t[0:64, sl], Kh[h].b), in_=V(hf.t[h * 64:(h + 1) * 64, :], hf.b))
                    else:
                        S.act("activation", out=V(Qh[h].t[0:64, sl], Qh[h].b), in_=V(hf.t[h * 64:(h + 1) * 64, :], hf.b), func=AF.Copy, scale=0.125)

            def vcons(tt, pp, Vt=Vt):
                S.dve("tensor_copy", out=V(Vt.t[:, 4 * tt:4 * tt + 4, :], Vt.b), in_=V(pp.t[:].rearrange("p (n c) -> p n c", c=128), pp.b))

            proj_pass(kb, hT, T, wbf, [(1, "k"), (0, "q"), (3, "g")], PP, hts, cons, vchunk=2, vconsumer=vcons)
            groups = [(h, qt, g) for h in range(2) for qt in range(NT) for g in range(qt + 1)]
            pairs = [(gi, ph) for gi in range(len(groups)) for ph in (0, 1)]
            NPR = len(pairs)
            PO, PB = P[6], P[7]

            def pinfo(pi):
                gi, ph = pairs[pi]
                h, qt, g = groups[gi]
                return gi, ph, h, qt, g

            def spp(gi, ph):
                return spb[(gi % 2) * 2 + ph]

            def sp_tile(gi, kl):
                ph, half = (0, 3 - kl) if kl >= 2 else (1, 1 - kl)
                t = spp(gi, ph)
                return V(t.t[:, half * 512:(half + 1) * 512], t.b)

            def s1_pe(pi):
                gi, ph, h, qt, g = pinfo(pi)
                qsl = slice(qt * 512, (qt + 1) * 512)
                for half in range(2):
                    kl = 3 - 2 * ph - half
                    kt = 4 * g + kl
                    Z = P[2 * (pi % 3) + half]
                    S.pe("matmul", out=Z.all(), lhsT=V(Kh[h].t[0:64, kt * 128:(kt + 1) * 128], Kh[h].b), rhs=V(Qh[h].t[0:64, qsl], Qh[h].b),
                         start=True, stop=(g != qt))
                    if g == qt:
                        r = kt - 4 * qt
                        S.pe("matmul", out=Z.all(), lhsT=identb.all(), rhs=V(cmt.t[:, 384 - 128 * r:896 - 128 * r], cmt.b), start=False, stop=True)

            def s1_act(pi):
                gi, ph, h, qt, g = pinfo(pi)
                e = ef[pi % 2]
                S.act("activation", out=e.all(), in_=PR[pi % 3].all(), func=AF.Exp)
                S.act("activation", out=spp(gi, ph).all(), in_=e.all(), func=AF.Ln, bias=V(C["one"].t[:, 0:1], C["one"].b))

            def s2_pe(pi):
                gi, ph, h, qt, g = pinfo(pi)
                for half in range(2):
                    kl = 3 - 2 * ph - half
                    Z = P[2 * (pi % 3) + half]
                    S.pe("matmul", out=Z.all(), lhsT=negtri.all(), rhs=sp_tile(gi, kl), start=False, stop=(kl == 3), skip_group_check=True)
                    for k2 in range(kl + 1, 4):
                        S.pe("matmul", out=Z.all(), lhsT=negones.all(), rhs=sp_tile(gi, k2), start=False, stop=(k2 == 3), skip_group_check=True)

            def s2_act(pi):
                S.act("activation", out=At[pi % 2].all(), in_=PR[pi % 3].all(), func=AF.Exp)

            def s3_pe(pi):
                gi, ph, h, qt, g = pinfo(pi)
                qsl = slice(qt * 512, (qt + 1) * 512)
                A = At[pi % 2]
                for half in range(2):
                    kl = 3 - 2 * ph - half
                    kt = 4 * g + kl
                    S.pe("matmul", out=V(PO.t[0:64, :], PO.b), lhsT=V(Vt.t[:, kt, h * 64:(h + 1) * 64], Vt.b), rhs=V(A.t[:, half * 512:(half + 1) * 512], A.b),
                         start=(kl == 3), stop=(kl == 0))
                    if g > 0:
                        S.pe("matmul", out=V(PB.t[0:64, :], PB.b), lhsT=ones64, rhs=sp_tile(gi, kl), start=(kl == 3), stop=(kl == 0))
                if ph != 1:
                    return
                if g == 0:
                    S.dve("tensor_copy", out=acc.all(), in_=V(PO.t[0:64, :], PO.b))
                else:
                    S.act("activation", out=Dt.all(), in_=V(PB.t[0:64, :], PB.b), func=AF.Exp, scale=-1.0)
                    S.dve("tensor_tensor", out=acc.all(), in0=acc.all(), in1=Dt.all(), op=ALU.mult)
                    S.dve("tensor_tensor", out=acc.all(), in0=acc.all(), in1=V(PO.t[0:64, :], PO.b), op=ALU.add)
                if g == qt:
                    S.dve("tensor_copy", out=V(ytmp.t[h * 64:(h + 1) * 64, :], ytmp.b), in_=acc.all())
                    S.pool("tensor_tensor", out=V(ym.t[h * 64:(h + 1) * 64, qsl], ym.b), in0=V(ytmp.t[h * 64:(h + 1) * 64, :], ytmp.b),
                           in1=V(sga.t[h * 64:(h + 1) * 64, qsl], sga.b), op=ALU.mult)

            s1_pe(0)
            s1_pe(1)
            s1_act(0)
            for i in range(NPR + 1):
                if i + 2 < NPR:
                    s1_pe(i + 2)
                if i < NPR:
                    s2_pe(i)
                if i + 1 < NPR:
                    s1_act(i + 1)
                if i < NPR:
                    s2_act(i)
                if i >= 1:
                    s3_pe(i - 1)
            S.dma(V(ysbT.t[hp * 128:(hp + 1) * 128, :], ysbT.b), ym.all(), q="pool")
            fence(S)
        if stop <= 2:
            S.emit()
            return kb.nc
        car.reset()
        build_s5(kb, C, P, arena, CS, T, win, wbf, stage, hts, hT, small, bre_d, bim_d, cre_d, cim_d, yT, sg5T, tf, car)
        S.emit()
    return kb.nc


def build_s5(kb, C, P, arena, CS, T, win, wbf, stage, hts, hT, small, bre_d, bim_d, cre_d, cim_d, yT, sg5T, tf, car):
    S = kb.S
    NT = T // 512
    Tn = T // 2
    NTn = Tn // 512
    import math
    load_w_slots(kb, win, wbf, [(0, 1024), (1, 1152), (2, 1280), (3, 1408)], stage)
    ubf = [Tl(arena.t[:, i * CS:i * CS + T]) for i in range(2)]
    sgt = [car.get([128, 512], BF16) for i in range(2)]
    cnt = [0]

    def cons(tag, tt, pp):
        sl = slice(tt * 512, (tt + 1) * 512)
        if tag[0] == "u":
            ci = int(tag[1])
            S.act("activation", out=V(ubf[ci].t[:, sl], ubf[ci].b), in_=pp.all(), func=AF.Copy)
        else:
            ci = int(tag[1])
            st = sgt[cnt[0] % 2]
            cnt[0] += 1
            S.act("activation", out=st.all(), in_=pp.all(), func=AF.Silu)
            S.dma(V(sg5T.t[ci * 128:(ci + 1) * 128, sl], sg5T.b), st.all(), q="pool")

    proj_pass(kb, hT, T, wbf, [(0, "u0"), (1, "u1"), (2, "g0"), (3, "g1")], P[0:4], hts, cons)

    def sm(name, shape=(128, 8)):
        return kb.sb("s5_" + name, list(shape))

    lre, lim, lstep = small["lre"], small["lim"], small["lstep"]
    step, lrs, th, rho = sm("step"), sm("lrs"), sm("th"), sm("rho")
    cc, ss, t1, t2 = sm("cc"), sm("ss"), sm("t1"), sm("t2")
    hpi = sm("hpi", (128, 1))
    S.memset("pool", hpi.all(), math.pi / 2)
    S.act("activation", out=step.all(), in_=lstep.all(), func=AF.Exp)
    S.dve("tensor_tensor", out=lrs.all(), in0=lre.all(), in1=step.all(), op=ALU.mult)
    S.dve("tensor_tensor", out=th.all(), in0=lim.all(), in1=step.all(), op=ALU.mult)
    S.act("activation", out=rho.all(), in_=lrs.all(), func=AF.Exp)
    S.act("activation", out=ss.all(), in_=th.all(), func=AF.Sin, scale=1.0 / 32)
    S.act("activation", out=cc.all(), in_=th.all(), func=AF.Sin, scale=1.0 / 32, bias=V(hpi.t[:, 0:1], hpi.b))

    def dbl(c, s, ta, tb):
        S.dve("tensor_tensor", out=ta.all(), in0=c.all(), in1=c.all(), op=ALU.mult)
        S.dve("tensor_tensor", out=tb.all(), in0=s.all(), in1=s.all(), op=ALU.mult)
        S.dve("scalar_tensor_tensor", out=s.all(), in0=s.all(), scalar=2.0, in1=c.all(), op0=ALU.mult, op1=ALU.mult)
        S.dve("tensor_tensor", out=c.all(), in0=ta.all(), in1=tb.all(), op=ALU.subtract)

    for _ in range(5):
        dbl(cc, ss, t1, t2)
    abr, abi, den, fre, fim = sm("abr"), sm("abi"), sm("den"), sm("fre"), sm("fim")
    S.dve("tensor_tensor", out=abr.all(), in0=rho.all(), in1=cc.all(), op=ALU.mult)
    S.dve("tensor_scalar", out=abr.all(), in0=abr.all(), scalar1=-1.0, scalar2=None, op0=ALU.add)
    S.dve("tensor_tensor", out=abi.all(), in0=rho.all(), in1=ss.all(), op=ALU.mult)
    S.dve("tensor_tensor", out=t1.all(), in0=lre.all(), in1=lre.all(), op=ALU.mult)
    S.dve("tensor_tensor", out=t2.all(), in0=lim.all(), in1=lim.all(), op=ALU.mult)
    S.dve("tensor_tensor", out=den.all(), in0=t1.all(), in1=t2.all(), op=ALU.add)
    S.dve("reciprocal", out=den.all(), in_=den.all())
    S.dve("tensor_tensor", out=t1.all(), in0=abr.all(), in1=lre.all(), op=ALU.mult)
    S.dve("tensor_tensor", out=t2.all(), in0=abi.all(), in1=lim.all(), op=ALU.mult)
    S.dve("tensor_tensor", out=fre.all(), in0=t1.all(), in1=t2.all(), op=ALU.add)
    S.dve("tensor_tensor", out=fre.all(), in0=fre.all(), in1=den.all(), op=ALU.mult)
    S.dve("tensor_tensor", out=t1.all(), in0=abi.all(), in1=lre.all(), op=ALU.mult)
    S.dve("tensor_tensor", out=t2.all(), in0=abr.all(), in1=lim.all(), op=ALU.mult)
    S.dve("tensor_tensor", out=fim.all(), in0=t1.all(), in1=t2.all(), op=ALU.subtract)
    S.dve("tensor_tensor", out=fim.all(), in0=fim.all(), in1=den.all(), op=ALU.mult)
    nfim = sm("nfim")
    S.dve("tensor_scalar", out=nfim.all(), in0=fim.all(), scalar1=-1.0, scalar2=None, op0=ALU.mult)
    nfre = sm("nfre")
    S.dve("tensor_scalar", out=nfre.all(), in0=fre.all(), scalar1=-1.0, scalar2=None, op0=ALU.mult)
    bst = car.get([128, 8, 128])
    Bre = car.get([128, 8, 128], BF16)
    Bim = car.get([128, 8, 128], BF16)
    S.dma(bst.all(), bre_d.all())
    S.dve("tensor_copy", out=Bre.all(), in_=bst.all())
    S.dma(bst.all(), bim_d.all())
    S.dve("tensor_copy", out=Bim.all(), in_=bst.all())
    cst = car.get([128, 8, 128])
    C1 = car.get([128, 8, 128], BF16)
    C2 = car.get([128, 8, 128], BF16)
    ctm = car.get([128, 128])
    S.dma(bst.all(), cre_d.all())
    S.dma(cst.all(), cim_d.all())
    for sc in range(8):
        S.dve("tensor_scalar", out=ctm.all(), in0=V(cst.t[:, sc, :], cst.b), scalar1=V(nfim.t[:, sc:sc + 1], nfim.b), scalar2=None, op0=ALU.mult)
        S.dve("scalar_tensor_tensor", out=V(C1.t[:, sc, :], C1.b), in0=V(bst.t[:, sc, :], bst.b), scalar=V(fre.t[:, sc:sc + 1], fre.b),
              in1=ctm.all(), op0=ALU.mult, op1=ALU.add)
        S.dve("tensor_scalar", out=ctm.all(), in0=V(cst.t[:, sc, :], cst.b), scalar1=V(nfre.t[:, sc:sc + 1], nfre.b), scalar2=None, op0=ALU.mult)
        S.dve("scalar_tensor_tensor", out=V(C2.t[:, sc, :], C2.b), in0=V(bst.t[:, sc, :], bst.b), scalar=V(nfim.t[:, sc:sc + 1], nfim.b),
              in1=ctm.all(), op0=ALU.mult, op1=ALU.add)

    def f32v(i0, n):
        return arena.t[:, i0 * CS:(i0 + 1) * CS].bitcast(F32)[:, 0:n]

    Tq = max(512, T // 4)
    NSEG = T // Tq
    NTq = Tq // 512
    c2 = arena.t[:, 2 * CS:3 * CS].bitcast(F32)
    c3 = arena.t[:, 3 * CS:4 * CS].bitcast(F32)
    c4 = arena.t[:, 4 * CS:5 * CS].bitcast(F32)
    c5 = arena.t[:, 5 * CS:6 * CS].bitcast(F32)
    tabC = Tl(c2[:, 0:Tq])
    tabS = Tl(c2[:, 2048:2048 + Tq])
    wre = [Tl(c3[:, 0:Tq]), Tl(c4[:, 0:Tq])]
    wim = [Tl(c3[:, 2048:2048 + Tq]), Tl(c4[:, 2048:2048 + Tq])]
    tmpA = Tl(c5[:, 0:max(Tq // 2, 512)])
    tmpB = Tl(c5[:, 2048:2048 + max(Tq // 2, 512)])
    En = [kb.sb("s5_En%d" % i, [128, 2]) for i in range(2)]
    e1, e2 = sm("e1", (128, 1)), sm("e2", (128, 1))
    ini = [sm("ini0", (128, 2)), sm("ini1", (128, 2))]
    rt = [car.get([128, 512]) for i in range(4)]
    ro = [car.get([128, 512]) for i in range(4)]
    xr = [car.get([128, 512], BF16) for i in range(2)]
    xi = [car.get([128, 512], BF16) for i in range(2)]
    yt = [car.get([128, 512], BF16) for i in range(2)]
    cnt = {"y": 0, "p": 0}
    for sc in range(8):
        uc, r0 = sc // 4, 32 * (sc % 4)
        S.memset("pool", V(tabC.t[:, 0:1], tabC.b), 1.0)
        S.memset("pool", V(tabS.t[:, 0:1], tabS.b), 0.0)
        cur = En[0]
        S.dve("tensor_copy", out=V(cur.t[:, 0:1], cur.b), in_=V(cc.t[:, sc:sc + 1], cc.b))
        S.dve("tensor_copy", out=V(cur.t[:, 1:2], cur.b), in_=V(ss.t[:, sc:sc + 1], ss.b))
        n = 1
        k = 0
        while n < Tq:
            cn = V(cur.t[:, 0:1], cur.b)
            sn = V(cur.t[:, 1:2], cur.b)
            S.dve("tensor_scalar", out=V(tmpA.t[:, 0:n], tmpA.b), in0=V(tabS.t[:, 0:n], tabS.b), scalar1=sn, scalar2=None, op0=ALU.mult)
            S.dve("scalar_tensor_tensor", out=V(tabC.t[:, n:2 * n], tabC.b), in0=V(tabC.t[:, 0:n], tabC.b), scalar=cn,
                  in1=V(tmpA.t[:, 0:n], tmpA.b), op0=ALU.mult, op1=ALU.subtract)
            S.pool("tensor_scalar", out=V(tmpB.t[:, 0:n], tmpB.b), in0=V(tabC.t[:, 0:n], tabC.b), scalar1=sn, scalar2=0.0, op0=ALU.mult, op1=ALU.add)
            S.dve("scalar_tensor_tensor", out=V(tabS.t[:, n:2 * n], tabS.b), in0=V(tabS.t[:, 0:n], tabS.b), scalar=cn,
                  in1=V(tmpB.t[:, 0:n], tmpB.b), op0=ALU.mult, op1=ALU.add)
            nxt = En[(k + 1) % 2]
            S.dve("tensor_tensor", out=e1.all(), in0=cn, in1=cn, op=ALU.mult)
            S.dve("tensor_tensor", out=e2.all(), in0=sn, in1=sn, op=ALU.mult)
            S.dve("tensor_tensor", out=V(nxt.t[:, 0:1], nxt.b), in0=e1.all(), in1=e2.all(), op=ALU.subtract)
            S.dve("scalar_tensor_tensor", out=V(nxt.t[:, 1:2], nxt.b), in0=sn, scalar=2.0, in1=cn, op0=ALU.mult, op1=ALU.mult)
            cur = nxt
            k += 1
            n *= 2
        ETn = cur

        def rotin_tile(seg, i, sc=sc, uc=uc):
            ls = slice(i * 512, (i + 1) * 512)
            gsl = slice(seg * Tq + i * 512, seg * Tq + (i + 1) * 512)
            Pr, Pi = P[(2 * cnt["p"]) % 4], P[(2 * cnt["p"] + 1) % 4]
            cnt["p"] += 1
            W, Wi = wre[seg % 2], wim[seg % 2]
            S.pe("matmul", out=Pr.all(), lhsT=V(Bre.t[:, sc, :], Bre.b), rhs=V(ubf[uc].t[:, gsl], ubf[uc].b), start=True, stop=True)
            S.pe("matmul", out=Pi.all(), lhsT=V(Bim.t[:, sc, :], Bim.b), rhs=V(ubf[uc].t[:, gsl], ubf[uc].b), start=True, stop=True)
            a0, a1, a2, a3 = rt
            S.dve("tensor_tensor", out=a0.all(), in0=Pr.all(), in1=V(tabC.t[:, ls], tabC.b), op=ALU.mult)
            S.dve("tensor_tensor", out=a1.all(), in0=Pi.all(), in1=V(tabS.t[:, ls], tabS.b), op=ALU.mult)
            S.pool("tensor_tensor", out=V(W.t[:, ls], W.b), in0=a0.all(), in1=a1.all(), op=ALU.add)
            S.dve("tensor_tensor", out=a2.all(), in0=Pi.all(), in1=V(tabC.t[:, ls], tabC.b), op=ALU.mult)
            S.dve("tensor_tensor", out=a3.all(), in0=Pr.all(), in1=V(tabS.t[:, ls], tabS.b), op=ALU.mult)
            S.pool("tensor_tensor", out=V(Wi.t[:, ls], Wi.b), in0=a2.all(), in1=a3.all(), op=ALU.subtract)

        def rotout_tile(seg, i, sc=sc, uc=uc, r0=r0):
            ls = slice(i * 512, (i + 1) * 512)
            gsl = slice(seg * Tq + i * 512, seg * Tq + (i + 1) * 512)
            W, Wi = wre[seg % 2], wim[seg % 2]
            a0, a1, a2, a3 = ro
            X, Xi = xr[i % 2], xi[i % 2]
            S.pool("tensor_tensor", out=a0.all(), in0=V(W.t[:, ls], W.b), in1=V(tabC.t[:, ls], tabC.b), op=ALU.mult)
            S.pool("tensor_tensor", out=a1.all(), in0=V(Wi.t[:, ls], Wi.b), in1=V(tabS.t[:, ls], tabS.b), op=ALU.mult)
            S.dve("tensor_tensor", out=X.all(), in0=a0.all(), in1=a1.all(), op=ALU.subtract)
            S.pool("tensor_tensor", out=a2.all(), in0=V(W.t[:, ls], W.b), in1=V(tabS.t[:, ls], tabS.b), op=ALU.mult)
            S.dve("tensor_tensor", out=a3.all(), in0=V(Wi.t[:, ls], Wi.b), in1=V(tabC.t[:, ls], tabC.b), op=ALU.mult)
            S.dve("tensor_tensor", out=Xi.all(), in0=a2.all(), in1=a3.all(), op=ALU.add)
            Py = P[4 + i % 2]
            S.pe("matmul", out=Py.all(), lhsT=V(C1.t[:, sc, :], C1.b), rhs=X.all(), start=True, stop=False)
            S.pe("matmul", out=Py.all(), lhsT=V(C2.t[:, sc, :], C2.b), rhs=Xi.all(), start=False, stop=True)
            Y = yt[cnt["y"] % 2]
            cnt["y"] += 1
            S.dve("scalar_tensor_tensor", out=V(Y.t[r0:r0 + 32, :], Y.b), in0=V(ubf[uc].t[r0:r0 + 32, gsl], ubf[uc].b),
                  scalar=V(small["dvec"].t[r0:r0 + 32, uc:uc + 1], small["dvec"].b), in1=V(Py.t[r0:r0 + 32, :], Py.b), op0=ALU.mult, op1=ALU.add)
            S.dma(V(yT.t[uc * 128 + r0:uc * 128 + r0 + 32, gsl], yT.b), V(Y.t[r0:r0 + 32, :], Y.b), q="sp")

        rho_b = V(rho.t[:, sc:sc + 1].to_broadcast([128, Tq]), rho.b)
        cT, sT = V(ETn.t[:, 0:1], ETn.b), V(ETn.t[:, 1:2], ETn.b)
        for i in range(NTq):
            rotin_tile(0, i)
        for seg in range(NSEG):
            W, Wi = wre[seg % 2], wim[seg % 2]
            if seg == 0:
                i_re, i_im = 0.0, 0.0
            else:
                iv = ini[seg % 2]
                i_re, i_im = V(iv.t[:, 0:1], iv.b), V(iv.t[:, 1:2], iv.b)
            S.dve("tensor_tensor_scan", out=W.all(), data0=rho_b, data1=W.all(), initial=i_re, op0=ALU.mult, op1=ALU.add)
            S.dve("tensor_tensor_scan", out=Wi.all(), data0=rho_b, data1=Wi.all(), initial=i_im, op0=ALU.mult, op1=ALU.add)
            if seg + 1 < NSEG:
                iv = ini[(seg + 1) % 2]
                lr, li = V(W.t[:, Tq - 1:Tq], W.b), V(Wi.t[:, Tq - 1:Tq], Wi.b)
                S.dve("tensor_tensor", out=e1.all(), in0=li, in1=sT, op=ALU.mult)
                S.dve("scalar_tensor_tensor", out=V(iv.t[:, 0:1], iv.b), in0=lr, scalar=cT, in1=e1.all(), op0=ALU.mult, op1=ALU.subtract)
                S.dve("tensor_tensor", out=e2.all(), in0=li, in1=cT, op=ALU.mult)
                S.dve("scalar_tensor_tensor", out=V(iv.t[:, 1:2], iv.b), in0=lr, scalar=sT, in1=e2.all(), op0=ALU.mult, op1=ALU.add)
            for i in range(NTq):
                if seg + 1 < NSEG:
                    rotin_tile(seg + 1, i)
                rotout_tile(seg, i)


def prep_l2(inp, b, j, T, cst, mix0T_b):
    f = np.float32
    wi = inp["od_w_in"][0]
    cols = np.concatenate([np.arange(k * 512 + j * 256, k * 512 + (j + 1) * 256) for k in range(6)])
    G0 = 16 * j
    lre = np.zeros((128, 8), f)
    lim = np.zeros((128, 8), f)
    lstep = np.zeros((128, 8), f)
    bre = np.zeros((128, 8, 128), f)
    bim = np.zeros((128, 8, 128), f)
    cre = np.zeros((128, 8, 128), f)
    cim = np.zeros((128, 8, 128), f)
    for sc in range(8):
        r0 = 32 * (sc % 4)
        for gl in range(2):
            g = G0 + 2 * sc + gl
            lre[gl * 64:(gl + 1) * 64, sc] = inp["od_s5_lambda_re"][0][g]
            lim[gl * 64:(gl + 1) * 64, sc] = inp["od_s5_lambda_im"][0][g]
            lstep[gl * 64:(gl + 1) * 64, sc] = inp["od_s5_log_step"][0][g]
            bre[r0 + gl * 16:r0 + (gl + 1) * 16, sc, gl * 64:(gl + 1) * 64] = inp["od_s5_b_re"][0][g].T
            bim[r0 + gl * 16:r0 + (gl + 1) * 16, sc, gl * 64:(gl + 1) * 64] = inp["od_s5_b_im"][0][g].T
            cre[gl * 64:(gl + 1) * 64, sc, r0 + gl * 16:r0 + (gl + 1) * 16] = inp["od_s5_c_re"][0][g].T
            cim[gl * 64:(gl + 1) * 64, sc, r0 + gl * 16:r0 + (gl + 1) * 16] = inp["od_s5_c_im"][0][g].T
    dflat = inp["od_s5_d"][0].reshape(-1)[j * 256:(j + 1) * 256]
    p = np.arange(128)[:, None]
    s_ = np.arange(128)[None, :]
    negtri = np.where(p >= s_, -1.0, 0.0).astype(f)
    m = {
        "xT": np.ascontiguousarray(inp["x"][b].T), "mix0T": mix0T_b, "wout0": inp["ev_w_out"][0], "cl": lay128(inp["c"][b]),
        "adaw0": inp["ev_ada_w"][0], "adab_g0": lay128(inp["ev_ada_b"][0][2048:]), "adaw1": inp["od_ada_w"][0],
        "adab_sh": lay128(inp["od_ada_b"][0][0:1024]), "adab_sc": lay128(inp["od_ada_b"][0][1024:2048]),
        "normg": lay128(inp["od_norm"][0]), "win": np.ascontiguousarray(wi[:, cols]),
        "qg": np.tile(inp["od_q_norm"][0], 2).reshape(128, 1), "kg": np.tile(inp["od_k_norm"][0], 2).reshape(128, 1),
        "onesbd": cst["onesbd"], "cms": cst["cms"], "identb": cst["identb"], "negtri": negtri.astype(NPBF),
        "lre": lre, "lim": lim, "lstep": lstep, "bre_l": bre, "bim_l": bim, "cre_p": cre, "cim_p": cim,
        "dvec": np.ascontiguousarray(dflat.reshape(2, 128).T),
    }
    return {k: np.ascontiguousarray(v) if v.dtype == NPBF else np.ascontiguousarray(v, dtype=f) for k, v in m.items()}


_CACHE = {}


def _get(name, fn, *a):
    key = (name,) + a
    if key not in _CACHE:
        _CACHE[key] = fn(*a)
    return _CACHE[key]


def kernel(**inp):
    inp = {k: np.asarray(v) for k, v in inp.items()}
    T = inp["x"].shape[1]
    cst = consts_l1(T)
    cores = [(b, j) for b in range(NB) for j in range(2)]
    ids = list(range(8))
    nc1 = build_l1(T)
    r1 = run_bass_kernel_spmd(nc1, [prep_l1(inp, b, j, T, cst) for (b, j) in cores], core_ids=ids).results
    mix0T = []
    for b in range(NB):
        a, c = r1[2 * b]["mixT"], r1[2 * b + 1]["mixT"]
        mix0T.append(np.ascontiguousarray(np.concatenate([a[0:256], c[0:256], a[256:512], c[256:512]], axis=0)))
    del r1
    nc2 = build_l2(T)
    r2 = run_bass_kernel_spmd(nc2, [prep_l2(inp, b, j, T, cst, mix0T[b]) for (b, j) in cores], core_ids=ids).results
    Th = T // 2
    nc3 = build_l3(Th)
    maps = []
    for (b, j) in cores:
        sl = slice(j * Th, (j + 1) * Th)
        cat = lambda nm: np.ascontiguousarray(np.concatenate([r2[2 * b][nm][:, sl], r2[2 * b + 1][nm][:, sl]], axis=0))
        maps.append({
            "x1T": np.ascontiguousarray(r2[2 * b]["x1T"][:, sl]), "ysbT": cat("ysbT"), "yT": cat("yT"), "sg5T": cat("sg5T"),
            "gluw": np.ascontiguousarray(inp["od_glu_w"][0], dtype=np.float32), "glub": lay128(inp["od_glu_b"][0]),
            "wout": np.ascontiguousarray(inp["od_w_out"][0], dtype=np.float32), "cl": lay128(inp["c"][b]),
            "adaw": np.ascontiguousarray(inp["od_ada_w"][0], dtype=np.float32), "adab_g": lay128(inp["od_ada_b"][0][2048:]),
        })
    del r2
    r3 = run_bass_kernel_spmd(nc3, maps, core_ids=ids).results
    out = np.empty((NB, T, D), np.float32)
    for i, (b, j) in enumerate(cores):
        out[b, j * Th:(j + 1) * Th, :] = r3[i]["outT"].T
    return out
```
